# Optimizing a Trainium2 kernel written in Bass

```python
import math
import jax
import jax.numpy as jnp
from jax import lax
import numpy as np

D_MODEL = 2048
BATCH = 1
SEQ = 16384
DEPTH = 2

GRID_W = 64
HEAD_DIM = 128
EPS = 1e-6
NEG_INF = -1e30

A_HEADS = 8
A_KV_HEADS = 2
A_WIDTH = A_HEADS * HEAD_DIM
A_KV_WIDTH = A_KV_HEADS * HEAD_DIM
ROPE_THETA = 10000.0
Q_BLOCK = 128
S5_WIDTH = D_MODEL // 2
S5_GROUP = 16
S5_GROUPS = S5_WIDTH // S5_GROUP
S5_STATE = 64
S5_DT_MIN = 0.001
S5_DT_MAX = 0.1
C_HEADS = 8
C_WIDTH = C_HEADS * HEAD_DIM
NA_ROWS = 8
NA_COLS = 16
SSD_WIDTH = D_MODEL // 2
SSD_HEAD_DIM = 64
SSD_HEADS = SSD_WIDTH // SSD_HEAD_DIM
SSD_GROUPS = 2
SSD_STATE = 128
SSD_CONV = 5
SSD_CHUNK = 128
SSD_CONV_CH = SSD_WIDTH + 2 * SSD_GROUPS * SSD_STATE

IN_EVEN = 2 * A_WIDTH + 2 * A_KV_WIDTH + 2 * S5_WIDTH
IN_ODD = 4 * C_WIDTH + SSD_WIDTH + SSD_CONV_CH + 2 * SSD_HEADS
N_EVEN = (DEPTH + 1) // 2
N_ODD = DEPTH // 2

kernel_name = 'hybrid_gqa_s5_natten_ssd_encoder'


def split_cols(t, sizes):
    outs, start = [], 0
    for s in sizes:
        outs.append(t[..., start:start + s])
        start += s
    return outs


def rms_norm(x, g):
    xf = x.astype(jnp.float32)
    y = xf * lax.rsqrt(jnp.mean(xf * xf, axis=-1, keepdims=True) + EPS)
    return (y * g.astype(jnp.float32)).astype(x.dtype)


def ada_modulate(x, c, norm_g, ada_w, ada_b):
    mod = jax.nn.silu(c) @ ada_w + ada_b
    shift, scale, gate = jnp.split(mod, 3, axis=-1)
    h = rms_norm(x, norm_g) * (1.0 + scale[:, None]) + shift[:, None]
    return h, gate[:, None]


def axial_rope_tables(seq_len):
    t = jnp.arange(seq_len)
    row = (t // GRID_W).astype(jnp.float32)
    col = (t % GRID_W).astype(jnp.float32)
    n_axis = HEAD_DIM // 4
    inv = ROPE_THETA ** (-jnp.arange(n_axis, dtype=jnp.float32) / n_axis)
    ang = jnp.concatenate([row[:, None] * inv, col[:, None] * inv], axis=-1)
    return jnp.cos(ang), jnp.sin(ang)


def apply_rope(x, cos, sin):
    xp = x.astype(jnp.float32).reshape(x.shape[:-1] + (HEAD_DIM // 2, 2))
    x1, x2 = xp[..., 0], xp[..., 1]
    cs = cos[None, :, None, :]
    sn = sin[None, :, None, :]
    out = jnp.stack([x1 * cs - x2 * sn, x1 * sn + x2 * cs], axis=-1)
    return out.reshape(x.shape).astype(x.dtype)


def gqa_block_attention(q, k, v):
    bsz, seq, hq, dh = q.shape
    hkv = k.shape[2]
    grp = hq // hkv
    nb = seq // Q_BLOCK
    qb = q.reshape(bsz, nb, Q_BLOCK, hkv, grp, dh).transpose(1, 0, 2, 3, 4, 5)
    scale = dh ** -0.5

    def block(qi):
        s = jnp.einsum('bqkgd,bskd->bkgqs', qi, k).astype(jnp.float32) * scale
        p = jax.nn.softmax(s, axis=-1).astype(v.dtype)
        return jnp.einsum('bkgqs,bskd->bqkgd', p, v)

    o = lax.map(block, qb)
    return o.transpose(1, 0, 2, 3, 4, 5).reshape(bsz, seq, hq * dh)


def _complex_affine_combine(e1, e2):
    a1r, a1i, x1r, x1i = e1
    a2r, a2i, x2r, x2i = e2
    return (a1r * a2r - a1i * a2i,
            a1r * a2i + a1i * a2r,
            a2r * x1r - a2i * x1i + x2r,
            a2r * x1i + a2i * x1r + x2i)


def s5_scan_direction(u, lam_re, lam_im, log_step, b_re, b_im, c_re, c_im, reverse):
    f32 = jnp.float32
    lr = lam_re.astype(f32)
    li = lam_im.astype(f32)
    dt = jnp.exp(log_step.astype(f32))[:, None]
    mag = jnp.exp(lr * dt)
    ab_re = mag * jnp.cos(li * dt)
    ab_im = mag * jnp.sin(li * dt)
    den = lr * lr + li * li
    num_re = ab_re - 1.0
    f_re = (num_re * lr + ab_im * li) / den
    f_im = (ab_im * lr - num_re * li) / den
    br = b_re.astype(f32)
    bi = b_im.astype(f32)
    bb_re = f_re[..., None] * br - f_im[..., None] * bi
    bb_im = f_re[..., None] * bi + f_im[..., None] * br
    x_re = jnp.einsum('gph,blgh->blgp', bb_re, u)
    x_im = jnp.einsum('gph,blgh->blgp', bb_im, u)
    a_re = jnp.broadcast_to(ab_re, x_re.shape)
    a_im = jnp.broadcast_to(ab_im, x_im.shape)
    _, _, h_re, h_im = lax.associative_scan(
        _complex_affine_combine, (a_re, a_im, x_re, x_im), reverse=reverse, axis=1)
    return (jnp.einsum('ghp,blgp->blgh', c_re.astype(f32), h_re)
            - jnp.einsum('ghp,blgp->blgh', c_im.astype(f32), h_im))


def s5_mixer(u, lam_re, lam_im, log_step, b_re, b_im, c_re, c_im, s5_d, w_glu, b_glu):
    bsz, seq, _ = u.shape
    ug = u.astype(jnp.float32).reshape(bsz, seq, S5_GROUPS, S5_GROUP)
    y = s5_d.astype(jnp.float32).reshape(S5_GROUPS, S5_GROUP) * ug
    for direction in range(2):
        y = y + s5_scan_direction(ug, lam_re[direction], lam_im[direction], log_step[direction],
                                  b_re[direction], b_im[direction], c_re[direction],
                                  c_im[direction], reverse=(direction == 1))
    y = jax.nn.gelu(y.reshape(bsz, seq, S5_WIDTH)).astype(u.dtype)
    val, gt = jnp.split(y @ w_glu + b_glu, 2, axis=-1)
    return val * jax.nn.sigmoid(gt)


def neighbourhood_attention(q, k, v, rpb):
    bsz, seq, heads, dh = q.shape
    rows = seq // GRID_W
    kr = min(NA_ROWS, rows)
    qg = q.reshape(bsz, rows, GRID_W, heads, dh)
    kg = k.reshape(bsz, rows, GRID_W, heads, dh)
    vg = v.reshape(bsz, rows, GRID_W, heads, dh)
    col = jnp.arange(GRID_W)
    col_start = jnp.clip(col - NA_COLS // 2, 0, GRID_W - NA_COLS)
    col_mask = (col[None, :] >= col_start[:, None]) & (col[None, :] < col_start[:, None] + NA_COLS)
    dc = jnp.clip(col[None, :] - col[:, None], -(NA_COLS - 1), NA_COLS - 1) + NA_COLS - 1
    rpb_c = rpb[:, :, dc].astype(jnp.float32)
    scale = dh ** -0.5

    def row_block(r):
        rs = jnp.clip(r - NA_ROWS // 2, 0, rows - kr)
        kb = lax.dynamic_slice_in_dim(kg, rs, kr, axis=1)
        vb = lax.dynamic_slice_in_dim(vg, rs, kr, axis=1)
        qr = lax.dynamic_index_in_dim(qg, r, axis=1, keepdims=False)
        s = jnp.einsum('bqhd,brkhd->bhqrk', qr, kb).astype(jnp.float32) * scale
        dr = rs + jnp.arange(kr) - r + NA_ROWS - 1
        bias = jnp.take(rpb_c, dr, axis=1).transpose(0, 2, 1, 3)
        s = jnp.where(col_mask[None, None, :, None, :], s + bias[None], NEG_INF)
        p = jax.nn.softmax(s.reshape(bsz, heads, GRID_W, kr * GRID_W), axis=-1)
        p = p.reshape(bsz, heads, GRID_W, kr, GRID_W).astype(v.dtype)
        return jnp.einsum('bhqrk,brkhd->bqhd', p, vb)

    o = lax.map(row_block, jnp.arange(rows))
    return o.transpose(1, 0, 2, 3, 4).reshape(bsz, seq, heads * dh)


def depthwise_conv_centred(x, w, bias):
    ch = x.shape[-1]
    y = lax.conv_general_dilated(x, w[:, None, :].astype(x.dtype), window_strides=(1,),
                                 padding=[(SSD_CONV // 2, SSD_CONV // 2)],
                                 dimension_numbers=('NWC', 'WIO', 'NWC'),
                                 feature_group_count=ch)
    return y + bias


def segsum(a):
    t = a.shape[-1]
    cs = jnp.cumsum(a, axis=-1)
    diff = cs[..., :, None] - cs[..., None, :]
    return jnp.where(jnp.tril(jnp.ones((t, t), dtype=bool)), diff, -jnp.inf)


def ssd_scan(x, dt, a, bm, cm):
    bsz, seq, heads, hp = x.shape
    n = bm.shape[-1]
    nc = seq // SSD_CHUNK
    xd = (x * dt[..., None]).reshape(bsz, nc, SSD_CHUNK, heads, hp)
    adt = (dt * a).reshape(bsz, nc, SSD_CHUNK, heads).transpose(0, 3, 1, 2)
    bc = bm.reshape(bsz, nc, SSD_CHUNK, heads, n)
    cc = cm.reshape(bsz, nc, SSD_CHUNK, heads, n)
    a_cum = jnp.cumsum(adt, axis=-1)
    scores = jnp.einsum('bclhn,bcshn->bhcls', cc, bc) * jnp.exp(segsum(adt))
    y_diag = jnp.einsum('bhcls,bcshp->bclhp', scores, xd)
    decay_states = jnp.exp(a_cum[..., -1:] - a_cum).transpose(0, 2, 3, 1)
    states = jnp.einsum('bclhn,bclhp->bchpn', bc * decay_states[..., None], xd)
    chunk_tot = jnp.pad(a_cum[..., -1], ((0, 0), (0, 0), (1, 0)))
    decay_chunk = jnp.exp(segsum(chunk_tot))
    states = jnp.concatenate([jnp.zeros_like(states[:, :1]), states], axis=1)
    states = jnp.einsum('bhzc,bchpn->bzhpn', decay_chunk, states)[:, :-1]
    out_decay = jnp.exp(a_cum).transpose(0, 2, 3, 1)
    y_off = jnp.einsum('bclhn,bchpn->bclhp', cc, states) * out_decay[..., None]
    return (y_diag + y_off).reshape(bsz, seq, heads, hp)


def ssd_mixer(z, xbc, dt_raw, conv_w, conv_b, dt_bias, a_log, ssd_d, norm_w):
    f32 = jnp.float32
    bsz, seq, _ = z.shape
    xbc = jax.nn.silu(depthwise_conv_centred(xbc, conv_w, conv_b)).astype(f32)
    xs, bm, cm = split_cols(xbc, (SSD_WIDTH, SSD_GROUPS * SSD_STATE, SSD_GROUPS * SSD_STATE))
    xs = xs.reshape(bsz, seq, SSD_HEADS, SSD_HEAD_DIM)
    rep = SSD_HEADS // SSD_GROUPS
    bm = jnp.repeat(bm.reshape(bsz, seq, SSD_GROUPS, SSD_STATE), rep, axis=2)
    cm = jnp.repeat(cm.reshape(bsz, seq, SSD_GROUPS, SSD_STATE), rep, axis=2)
    dt = jax.nn.softplus(dt_raw.astype(f32).reshape(bsz, seq, 2, SSD_HEADS) + dt_bias.astype(f32))
    a = -jnp.exp(a_log.astype(f32))
    y_fwd = ssd_scan(xs, dt[:, :, 0], a[0], bm, cm)
    flip = lambda t: jnp.flip(t, axis=1)
    y_bwd = flip(ssd_scan(flip(xs), flip(dt[:, :, 1]), a[1], flip(bm), flip(cm)))
    y = y_fwd + y_bwd + ssd_d.astype(f32)[:, None] * xs
    y = y.reshape(bsz, seq, SSD_WIDTH) * jax.nn.silu(z.astype(f32))
    return rms_norm(y, norm_w).astype(z.dtype)


def layer_attn_s5(x, c, norm_g, ada_w, ada_b, w_in, q_norm, k_norm, lam_re, lam_im, log_step,
                  b_re, b_im, c_re, c_im, s5_d, w_glu, b_glu, w_out):
    bsz, seq, _ = x.shape
    h, gate = ada_modulate(x, c, norm_g, ada_w, ada_b)
    q, k, v, g_a, u, g_b = split_cols(
        h @ w_in, (A_WIDTH, A_KV_WIDTH, A_KV_WIDTH, A_WIDTH, S5_WIDTH, S5_WIDTH))
    q = rms_norm(q.reshape(bsz, seq, A_HEADS, HEAD_DIM), q_norm)
    k = rms_norm(k.reshape(bsz, seq, A_KV_HEADS, HEAD_DIM), k_norm)
    cos, sin = axial_rope_tables(seq)
    q = apply_rope(q, cos, sin)
    k = apply_rope(k, cos, sin)
    o_a = gqa_block_attention(q, k, v.reshape(bsz, seq, A_KV_HEADS, HEAD_DIM)) * jax.nn.silu(g_a)
    o_b = s5_mixer(u, lam_re, lam_im, log_step, b_re, b_im, c_re, c_im, s5_d, w_glu, b_glu) * jax.nn.silu(g_b)
    out = jnp.concatenate([o_a, o_b], axis=-1) @ w_out
    return x + gate * out


def layer_na_ssd(x, c, norm_g, ada_w, ada_b, w_in, q_norm, k_norm, rpb, conv_w, conv_b,
                 dt_bias, a_log, ssd_d, norm_w, w_out):
    bsz, seq, _ = x.shape
    h, gate = ada_modulate(x, c, norm_g, ada_w, ada_b)
    q, k, v, g_c, z, xbc, dt_raw = split_cols(
        h @ w_in, (C_WIDTH, C_WIDTH, C_WIDTH, C_WIDTH, SSD_WIDTH, SSD_CONV_CH, 2 * SSD_HEADS))
    q = rms_norm(q.reshape(bsz, seq, C_HEADS, HEAD_DIM), q_norm)
    k = rms_norm(k.reshape(bsz, seq, C_HEADS, HEAD_DIM), k_norm)
    o_c = neighbourhood_attention(q, k, v.reshape(bsz, seq, C_HEADS, HEAD_DIM), rpb) * jax.nn.silu(g_c)
    o_d = ssd_mixer(z, xbc, dt_raw, conv_w, conv_b, dt_bias, a_log, ssd_d, norm_w)
    out = jnp.concatenate([o_c, o_d], axis=-1) @ w_out
    return x + gate * out


def setup_inputs(seed: int = 0) -> dict:
    key = jax.random.key(seed)
    ks = jax.random.split(key, 40)
    f32 = jnp.float32
    D = D_MODEL
    NE, NO = N_EVEN, N_ODD
    G, P, H = S5_GROUPS, S5_STATE, S5_GROUP

    def nrm(i, shape, s):
        return jax.random.normal(ks[i], shape, f32) * s

    n_idx = jnp.arange(P, dtype=f32)
    dt0 = jnp.exp(jax.random.uniform(ks[28], (NO, 2, SSD_HEADS), f32,
                                     minval=math.log(1e-3), maxval=math.log(1e-1)))
    return {
        'x': nrm(0, (BATCH, SEQ, D), 1.0),
        'c': nrm(1, (BATCH, D), 1.0),
        'e_norm_g': 1.0 + nrm(2, (NE, D), 0.02),
        'e_ada_w': nrm(3, (NE, D, 3 * D), D ** -0.5),
        'e_ada_b': nrm(4, (NE, 3 * D), 0.01),
        'e_w_in': nrm(5, (NE, D, IN_EVEN), D ** -0.5),
        'e_q_norm': 1.0 + nrm(6, (NE, HEAD_DIM), 0.02),
        'e_k_norm': 1.0 + nrm(7, (NE, HEAD_DIM), 0.02),
        's5_lam_re': -0.5 + nrm(8, (NE, 2, G, P), 0.01),
        's5_lam_im': math.pi * n_idx + nrm(9, (NE, 2, G, P), 0.01),
        's5_log_step': jax.random.uniform(ks[10], (NE, 2, G), f32,
                                          minval=math.log(S5_DT_MIN), maxval=math.log(S5_DT_MAX)),
        's5_b_re': nrm(11, (NE, 2, G, P, H), (2 * H) ** -0.5),
        's5_b_im': nrm(12, (NE, 2, G, P, H), (2 * H) ** -0.5),
        's5_c_re': nrm(13, (NE, 2, G, H, P), P ** -0.5),
        's5_c_im': nrm(14, (NE, 2, G, H, P), P ** -0.5),
        's5_d': nrm(15, (NE, S5_WIDTH), 1.0),
        's5_w_glu': nrm(16, (NE, S5_WIDTH, 2 * S5_WIDTH), S5_WIDTH ** -0.5),
        's5_b_glu': nrm(17, (NE, 2 * S5_WIDTH), 0.01),
        'e_w_out': nrm(18, (NE, A_WIDTH + S5_WIDTH, D), (A_WIDTH + S5_WIDTH) ** -0.5),
        'o_norm_g': 1.0 + nrm(19, (NO, D), 0.02),
        'o_ada_w': nrm(20, (NO, D, 3 * D), D ** -0.5),
        'o_ada_b': nrm(21, (NO, 3 * D), 0.01),
        'o_w_in': nrm(22, (NO, D, IN_ODD), D ** -0.5),
        'o_q_norm': 1.0 + nrm(23, (NO, HEAD_DIM), 0.02),
        'o_k_norm': 1.0 + nrm(24, (NO, HEAD_DIM), 0.02),
        'na_rpb': nrm(25, (NO, C_HEADS, 2 * NA_ROWS - 1, 2 * NA_COLS - 1), 0.02),
        'ssd_conv_w': nrm(26, (NO, SSD_CONV, SSD_CONV_CH), SSD_CONV ** -0.5),
        'ssd_conv_b': nrm(27, (NO, SSD_CONV_CH), 0.01),
        'ssd_dt_bias': dt0 + jnp.log(-jnp.expm1(-dt0)),
        'ssd_a_log': jnp.log(jax.random.uniform(ks[29], (NO, 2, SSD_HEADS), f32, minval=1.0, maxval=16.0)),
        'ssd_d': 1.0 + nrm(30, (NO, SSD_HEADS), 0.1),
        'ssd_norm_w': 1.0 + nrm(31, (NO, SSD_WIDTH), 0.02),
        'o_w_out': nrm(32, (NO, C_WIDTH + SSD_WIDTH, D), (C_WIDTH + SSD_WIDTH) ** -0.5),
    }


def reference(x, c, e_norm_g, e_ada_w, e_ada_b, e_w_in, e_q_norm, e_k_norm, s5_lam_re, s5_lam_im,
              s5_log_step, s5_b_re, s5_b_im, s5_c_re, s5_c_im, s5_d, s5_w_glu, s5_b_glu, e_w_out,
              o_norm_g, o_ada_w, o_ada_b, o_w_in, o_q_norm, o_k_norm, na_rpb, ssd_conv_w, ssd_conv_b,
              ssd_dt_bias, ssd_a_log, ssd_d, ssd_norm_w, o_w_out):
    for layer in range(DEPTH):
        i = layer // 2
        if layer % 2 == 0:
            x = layer_attn_s5(x, c, e_norm_g[i], e_ada_w[i], e_ada_b[i], e_w_in[i], e_q_norm[i],
                              e_k_norm[i], s5_lam_re[i], s5_lam_im[i], s5_log_step[i], s5_b_re[i],
                              s5_b_im[i], s5_c_re[i], s5_c_im[i], s5_d[i], s5_w_glu[i], s5_b_glu[i],
                              e_w_out[i])
        else:
            x = layer_na_ssd(x, c, o_norm_g[i], o_ada_w[i], o_ada_b[i], o_w_in[i], o_q_norm[i],
                             o_k_norm[i], na_rpb[i], ssd_conv_w[i], ssd_conv_b[i], ssd_dt_bias[i],
                             ssd_a_log[i], ssd_d[i], ssd_norm_w[i], o_w_out[i])
    return x
```

```python
import math
from contextlib import ExitStack
import numpy as np
import ml_dtypes
import concourse.bass as bass
import concourse.mybir as mybir
from concourse.bass_utils import run_bass_kernel_spmd

F32 = mybir.dt.float32
BF16 = mybir.dt.bfloat16
AF = mybir.ActivationFunctionType
ALU = mybir.AluOpType
AX = mybir.AxisListType
NPBF = ml_dtypes.bfloat16

NCORES = 8
D = 2048
SEQ = 16384
TPC = SEQ // NCORES
EPS = 1e-6


class Buf:
    def __init__(self, t, name, onchip=True):
        self.t = t
        self.name = name
        self.onchip = onchip
        self.last_w = None
        self.reads = {}
        self.dsem = None
        self.dcnt = 0

    def __getitem__(self, idx):
        return self.t[idx]


class KB:
    def __init__(self, nc, stack):
        self.nc = nc
        self.stack = stack
        self.eng = {"pe": nc.tensor, "dve": nc.vector, "act": nc.scalar, "pool": nc.gpsimd, "sp": nc.sync}
        self.sem, self.cnt, self.seen = {}, {}, {}
        for k in self.eng:
            self.sem[k] = stack.enter_context(nc.semaphore("s_" + k))
            self.cnt[k] = 0
            self.seen[k] = {}
        self.nbuf = 0
        self.dbufs = []

    def sb(self, shape, dt, name=None):
        self.nbuf += 1
        name = "s_" + (name or f"sb{self.nbuf}")
        return Buf(self.stack.enter_context(self.nc.sbuf_tensor(name, list(shape), dt)), name)

    def ps(self, shape, dt=F32, name=None):
        self.nbuf += 1
        name = "p_" + (name or f"ps{self.nbuf}")
        return Buf(self.stack.enter_context(self.nc.psum_tensor(name, list(shape), dt)), name)

    def _wait(self, e, tok):
        if tok is None:
            return
        sem, val, key = tok
        if key == e and e == "pe":
            return
        if self.seen[e].get(key, 0) >= val:
            return
        self.eng[e].wait_ge(sem, val)
        self.seen[e][key] = val

    def _deps(self, e, reads, writes):
        for b in reads:
            if b is not None:
                self._wait(e, b.last_w)
        for b in writes:
            if b is not None:
                self._wait(e, b.last_w)
                for r in b.reads.values():
                    self._wait(e, r)

    def _mark(self, tok, reads, writes):
        for b in reads:
            if b is not None:
                b.reads[tok[2]] = tok
        for b in writes:
            if b is not None:
                b.last_w = tok
                b.reads = {}

    def op(self, e, fn, reads=(), writes=()):
        self._deps(e, reads, writes)
        ins = fn(self.eng[e])
        self.cnt[e] += 1
        ins.then_inc(self.sem[e], 1)
        self._mark((self.sem[e], self.cnt[e], e), reads, writes)
        return ins

    def dma(self, q, out_b, out_ap, in_b, in_ap, **kw):
        self._deps(q, [in_b], [out_b])
        owner = out_b if out_b is not None else in_b
        if owner.dsem is None:
            owner.dsem = self.stack.enter_context(self.nc.semaphore("d_" + owner.name))
            self.dbufs.append(owner)
        ins = self.eng[q].dma_start(out=out_ap, in_=in_ap, **kw)
        owner.dcnt += 16
        ins.then_inc(owner.dsem, 16)
        tok = (owner.dsem, owner.dcnt, "d_" + owner.name)
        self._mark(tok, [in_b], [out_b])
        return tok

    def finish(self, e="sp"):
        for b in self.dbufs:
            self._wait(e, (b.dsem, b.dcnt, "d_" + b.name))

    def mm(self, ob, oap, lb, lap, rb, rap, start=True, stop=True):
        return self.op("pe", lambda e: e.matmul(oap, lhsT=lap, rhs=rap, start=start, stop=stop), [lb, rb], [ob])

    def act(self, ob, oap, ib, iap, func, extra=(), eng="act", **kw):
        return self.op(eng, lambda e: e.activation(out=oap, in_=iap, func=func, **kw), [ib, *extra], [ob])

    def ts(self, eng, ob, oap, ib, iap, s1, s2, op0, op1=None, extra=()):
        if op1 is None:
            return self.op(eng, lambda e: e.tensor_scalar(out=oap, in0=iap, scalar1=s1, scalar2=None, op0=op0), [ib, *extra], [ob])
        return self.op(eng, lambda e: e.tensor_scalar(out=oap, in0=iap, scalar1=s1, scalar2=s2, op0=op0, op1=op1), [ib, *extra], [ob])

    def tt(self, eng, ob, oap, ab, aap, bb, bap, op):
        return self.op(eng, lambda e: e.tensor_tensor(out=oap, in0=aap, in1=bap, op=op), [ab, bb], [ob])

    def stt(self, ob, oap, ab, aap, scalar, bb, bap, op0, op1, extra=()):
        return self.op("dve", lambda e: e.scalar_tensor_tensor(out=oap, in0=aap, scalar=scalar, in1=bap, op0=op0, op1=op1),
                       [ab, bb, *extra], [ob])

    def copy(self, eng, ob, oap, ib, iap):
        if eng == "act":
            return self.op("act", lambda e: e.copy(out=oap, in_=iap), [ib], [ob])
        return self.op(eng, lambda e: e.tensor_copy(out=oap, in_=iap), [ib], [ob])

    def memset(self, eng, ob, oap, val):
        return self.op(eng, lambda e: e.memset(oap, val), [], [ob])


def _run(nc, in_maps):
    res = run_bass_kernel_spmd(nc, in_maps, core_ids=list(range(NCORES)))
    return res.results


def wview(w_ap, n0, n1):
    return w_ap.rearrange("(k p) n -> p k n", p=128)[:, :, n0:n1]


class Rot:
    def __init__(self, items):
        self.items = items
        self.i = 0

    def next(self):
        b = self.items[self.i % len(self.items)]
        self.i += 1
        return b


def emit_mod(K, nc, dram, wst, pmod, consts):
    cT = K.sb([128, 16], F32, "cT")
    sc = K.sb([128, 16], F32, "sc")
    abT = K.sb([128, 48], F32, "abT")
    gT = K.sb([128, 16], F32, "gT")
    mod = K.sb([128, 48], F32, "mod")
    gs = K.sb([128, 16], F32, "gs")
    K.dma("sp", cT, cT[:], None, dram["cT"][:, :])
    K.dma("sp", abT, abT[:], None, dram["ada_bT"][:, :])
    K.dma("sp", gT, gT[:], None, dram["gT"][:, :])
    K.act(sc, sc[:], cT, cT[:], AF.Silu)
    NB = 6144 // 256
    pend = None
    for nb in range(NB + 1):
        cur = pend
        if nb < NB:
            st = wst.next()
            for h in range(4):
                K.dma("sp" if h % 2 == 0 else "pool", st, st[:, 4 * h:4 * h + 4, :], None,
                      wview(dram["ada_w"], nb * 256, nb * 256 + 256)[:, 4 * h:4 * h + 4, :])
            pend = (st, nb)
        if cur is not None:
            st, b = cur
            for jj in range(2):
                j = 2 * b + jj
                for k in range(16):
                    K.mm(pmod, pmod[:, j:j + 1], st, st[:, k, 128 * jj:128 * jj + 128], sc, sc[:, k:k + 1],
                         start=(k == 0), stop=(k == 15))
    K.tt("dve", mod, mod[:], pmod, pmod[:, 0:48], abT, abT[:], ALU.add)
    K.stt(gs, gs[:], mod, mod[:, 16:32], 1.0, gT, gT[:], ALU.add, ALU.mult)
    return mod, gs


def emit_rstd(K, ob, oap, ib, iap, inv_n):
    K.ts("dve", ob, oap, ib, iap, inv_n, EPS, ALU.mult, ALU.add)
    K.act(ob, oap, ob, oap, AF.Sqrt)
    K.op("dve", lambda e: e.reciprocal(out=oap, in_=oap), [ob], [ob])


def front_plan(N, kinds):
    chunks = []
    nbf = nf = 0
    for ci, kd in enumerate(kinds):
        n0 = ci * 128
        m = min(128, N - n0)
        isbf = (kd == "bf") or (isinstance(kd, tuple))
        if isbf:
            chunks.append((n0, m, kd, "bf", nbf)); nbf += m
        else:
            chunks.append((n0, m, kd, "f", nf)); nf += m
    return chunks, nbf, nf


def emit_front_core(K, nc, dram, N, kinds, rope, xsrc, outs, T=TPC):
    chunks, nbf, nf = front_plan(N, kinds)
    NT = T // 512
    ones = K.sb([128, 128], F32, "ones")
    K.memset("pool", ones, ones[:], 1.0)
    wst = Rot([K.sb([128, 16, 256], F32, f"wst{i}") for i in range(2)])
    wbs = Rot([K.sb([128, 16, 256], BF16, f"wb{i}") for i in range(2)])
    pmod = K.ps([128, 64], F32, "pmod")
    acc = [K.ps([128, 512], F32, f"acc{i}") for i in range(4)]
    mod, gs = emit_mod(K, nc, dram, wst, pmod, None)

    xTb = K.sb([128, 16, T], BF16, "xTb")
    rstd = K.sb([128, T], F32, "rstd")
    sq = K.sb([128, T], F32, "sq")
    pend = xsrc(0)
    for k in range(16):
        xs = pend
        if k + 1 < 16:
            pend = xsrc(k + 1)
        K.copy("dve", xTb, xTb[:, k, :], xs, xs[:, :])
        K.act(sq, sq[:], xs, xs[:, :], AF.Square)
        for t in range(NT):
            K.mm(acc[t], acc[t][:], ones, ones[:], sq, sq[:, 512 * t:512 * t + 512], start=(k == 0), stop=(k == 15))
    for t in range(NT):
        emit_rstd(K, rstd, rstd[:, 512 * t:512 * t + 512], acc[t], acc[t][:], 1.0 / D)

    if rope:
        cosT = K.sb([128, T], F32, "cosT")
        sinT = K.sb([128, T], F32, "sinT")
        K.dma("sp", cosT, cosT[:], None, dram["cosT"][:, :])
        K.dma("sp", sinT, sinT[:], None, dram["sinT"][:, :])
        swp = K.sb([128, 128], F32, "swp")
        K.dma("sp", swp, swp[:], None, dram["swapM"][:, :])
    qkg = K.sb([128, 2], F32, "qkg")
    K.dma("sp", qkg, qkg[:], None, dram["qkg"][:, :])
    psq = K.ps([128, 512], F32, "psq")
    psw = K.ps([128, 512], F32, "psw")
    bvec = K.sb([128, 64], F32, "bvec")
    tmpR = Rot([K.sb([128, 512], F32, f"tmp{i}") for i in range(2)])
    resR = Rot([K.sb([128, 512], F32, f"res{i}") for i in range(2)])
    resbR = Rot([K.sb([128, 512], BF16, f"resb{i}") for i in range(2)])
    sqkR = Rot([K.sb([128, 512], F32, f"sqk{i}") for i in range(2)])
    qnR = Rot([K.sb([128, 512], F32, f"qn{i}") for i in range(2)])
    accR = Rot(acc)

    NB = (N + 255) // 256

    def load_w(nb):
        st = wst.next()
        n0 = nb * 256
        w = min(256, N - n0)
        for h in range(4):
            K.dma("sp" if h % 2 == 0 else "pool", st, st[:, 4 * h:4 * h + 4, 0:w], None,
                  wview(dram["w_in"], n0, n0 + w)[:, 4 * h:4 * h + 4, :])
        return st

    pend = load_w(0)
    for nb in range(NB):
        st = pend
        if nb + 1 < NB:
            pend = load_w(nb + 1)
        n0 = nb * 256
        w = min(256, N - n0)
        wb = wbs.next()
        for k in range(16):
            K.ts("dve" if k % 2 == 0 else "pool", wb, wb[:, k, 0:w], st, st[:, k, 0:w], gs[:, k:k + 1], None, ALU.mult, extra=[gs])
        for jj in range((w + 127) // 128):
            ci = 2 * nb + jj
            cn0, m, kd, okind, orow = chunks[ci]
            c0 = 128 * jj
            for k in range(16):
                K.mm(pmod, pmod[0:m, 48:49], st, st[:, k, c0:c0 + m], mod, mod[:, k:k + 1], start=(k == 0), stop=(k == 15))
            bcol = bvec[0:m, ci:ci + 1]
            K.copy("dve", bvec, bcol, pmod, pmod[0:m, 48:49])
            for t in range(NT):
                ts_ = slice(512 * t, 512 * t + 512)
                a = accR.next()
                for k in range(16):
                    K.mm(a, a[0:m, :], wb, wb[:, k, c0:c0 + m], xTb, xTb[:, k, ts_], start=(k == 0), stop=(k == 15))
                tmp = tmpR.next()
                K.tt("dve", tmp, tmp[0:m, :], a, a[0:m, :], rstd, rstd[0:m, ts_], ALU.mult)
                if kd == "silu":
                    res = resR.next()
                    K.act(res, res[0:m, :], tmp, tmp[0:m, :], AF.Silu, extra=[bvec], bias=bcol)
                    K.dma("pool", None, outs["f"][orow:orow + m, ts_], res, res[0:m, :])
                elif kd == "f32":
                    res = resR.next()
                    K.act(res, res[0:m, :], tmp, tmp[0:m, :], AF.Identity, extra=[bvec], bias=bcol)
                    K.dma("pool", None, outs["f"][orow:orow + m, ts_], res, res[0:m, :])
                elif kd == "bf":
                    resb = resbR.next()
                    K.act(resb, resb[0:m, :], tmp, tmp[0:m, :], AF.Identity, extra=[bvec], bias=bcol)
                    K.dma("pool", None, outs["bf"][orow:orow + m, ts_], resb, resb[0:m, :])
                else:
                    _, gcol, do_rope = kd
                    res = resR.next()
                    K.act(res, res[:], tmp, tmp[:], AF.Identity, extra=[bvec], bias=bcol)
                    sqk = sqkR.next()
                    K.act(sqk, sqk[:], res, res[:], AF.Square)
                    K.mm(psq, psq[:], ones, ones[:], sqk, sqk[:])
                    emit_rstd(K, sqk, sqk[:], psq, psq[:], 1.0 / 128)
                    qn = qnR.next()
                    K.stt(qn, qn[:], res, res[:], qkg[:, gcol:gcol + 1], sqk, sqk[:], ALU.mult, ALU.mult, extra=[qkg])
                    resb = resbR.next()
                    if do_rope:
                        K.mm(psw, psw[:], swp, swp[:], qn, qn[:])
                        K.tt("dve", sqk, sqk[:], psw, psw[:], sinT, sinT[:, ts_], ALU.mult)
                        K.tt("pool", qn, qn[:], qn, qn[:], cosT, cosT[:, ts_], ALU.mult)
                        K.tt("dve", resb, resb[:], qn, qn[:], sqk, sqk[:], ALU.add)
                    else:
                        K.copy("dve", resb, resb[:], qn, qn[:])
                    K.dma("pool", None, outs["bf"][orow:orow + m, ts_], resb, resb[:])
    return mod


KINDS_E = [("qk", 0, True)] * 8 + [("qk", 1, True)] * 2 + ["bf"] * 2 + ["silu"] * 8 + ["bf"] * 8 + ["silu"] * 8
N_E = 4608
KINDS_O = [("qk", 0, False)] * 8 + [("qk", 1, False)] * 8 + ["bf"] * 8 + ["silu"] * 8 + ["silu"] * 8 + ["f32"] * 12 + ["f32"]
N_O = 6688


def front_dram(nc, N, rope, T=TPC):
    d = {}
    d["cT"] = nc.dram_tensor("cT", [128, 16], F32, kind="ExternalInput").ap()
    d["ada_bT"] = nc.dram_tensor("ada_bT", [128, 48], F32, kind="ExternalInput").ap()
    d["gT"] = nc.dram_tensor("gT", [128, 16], F32, kind="ExternalInput").ap()
    d["ada_w"] = nc.dram_tensor("ada_w", [D, 3 * D], F32, kind="ExternalInput").ap()
    d["w_in"] = nc.dram_tensor("w_in", [D, N], F32, kind="ExternalInput").ap()
    d["qkg"] = nc.dram_tensor("qkg", [128, 2], F32, kind="ExternalInput").ap()
    if rope:
        d["cosT"] = nc.dram_tensor("cosT", [128, T], F32, kind="ExternalInput").ap()
        d["sinT"] = nc.dram_tensor("sinT", [128, T], F32, kind="ExternalInput").ap()
        d["swapM"] = nc.dram_tensor("swapM", [128, 128], F32, kind="ExternalInput").ap()
    return d


def build_front(N, kinds, rope):
    nc = bass.Bass("TRN2", target_bir_lowering=False)
    dram = front_dram(nc, N, rope)
    xT = nc.dram_tensor("xT", [D, TPC], F32, kind="ExternalInput").ap()
    chunks, nbf, nf = front_plan(N, kinds)
    outs = {"bf": nc.dram_tensor("obf", [nbf, TPC], BF16, kind="ExternalOutput").ap(),
            "f": nc.dram_tensor("of", [nf, TPC], F32, kind="ExternalOutput").ap()}
    modo = nc.dram_tensor("modo", [128, 48], F32, kind="ExternalOutput").ap()
    with ExitStack() as st:
        K = KB(nc, st)
        xst = Rot([K.sb([128, TPC], F32, f"xst{i}") for i in range(2)])

        def xsrc(k):
            b = xst.next()
            K.dma("sp", b, b[:, 0:TPC // 2], None, xT[128 * k:128 * k + 128, 0:TPC // 2])
            K.dma("pool", b, b[:, TPC // 2:], None, xT[128 * k:128 * k + 128, TPC // 2:])
            return b
        mod = emit_front_core(K, nc, dram, N, kinds, rope, xsrc, outs)
        K.dma("pool", None, modo[:, :], mod, mod[:])
        K.finish()
    return nc


def pj(v, ncol):
    return np.ascontiguousarray(np.asarray(v, np.float32).reshape(ncol, 128).T)


def rope_tables():
    t = np.arange(SEQ)
    row = (t // 64).astype(np.float32)
    col = (t % 64).astype(np.float32)
    inv = (np.float32(10000.0) ** (-np.arange(32, dtype=np.float32) / np.float32(32))).astype(np.float32)
    ang = np.concatenate([row[:, None] * inv, col[:, None] * inv], axis=-1).astype(np.float32)
    cos, sin = np.cos(ang).astype(np.float32), np.sin(ang).astype(np.float32)
    cosT = np.repeat(cos, 2, axis=1).T
    sgn = np.tile(np.array([-1.0, 1.0], np.float32), 64)
    sinT = (np.repeat(sin, 2, axis=1) * sgn).T
    return np.ascontiguousarray(cosT), np.ascontiguousarray(sinT)


def swap_matrix():
    m = np.zeros((128, 128), np.float32)
    for i in range(64):
        m[2 * i + 1, 2 * i] = 1.0
        m[2 * i, 2 * i + 1] = 1.0
    return m


def front_common_inputs(c, ada_w, ada_b, norm_g, w_in, qn, kn):
    return {"cT": pj(c.reshape(-1), 16), "ada_bT": pj(ada_b.reshape(-1), 48), "gT": pj(norm_g.reshape(-1), 16),
            "ada_w": np.ascontiguousarray(ada_w), "w_in": np.ascontiguousarray(w_in),
            "qkg": np.ascontiguousarray(np.stack([qn.reshape(-1), kn.reshape(-1)], axis=1).astype(np.float32))}


def run_front(layer, xT, c, norm_g, ada_w, ada_b, w_in, q_norm, k_norm):
    rope = (layer == 0)
    nc = build_front(N_E if layer == 0 else N_O, KINDS_E if layer == 0 else KINDS_O, rope)
    base = front_common_inputs(c, ada_w[0], ada_b[0], norm_g[0], w_in[0], q_norm[0], k_norm[0])
    if rope:
        cosT, sinT = rope_tables()
        base["swapM"] = swap_matrix()
    maps = []
    for i in range(NCORES):
        sl = slice(i * TPC, (i + 1) * TPC)
        m = dict(base)
        m["xT"] = np.ascontiguousarray(xT[:, sl])
        if rope:
            m["cosT"] = np.ascontiguousarray(cosT[:, sl])
            m["sinT"] = np.ascontiguousarray(sinT[:, sl])
        maps.append(m)
    res = _run(nc, maps)
    obf = np.concatenate([r["obf"] for r in res], axis=1)
    of = np.concatenate([r["of"] for r in res], axis=1)
    return obf, of, res[0]["modo"]


def build_attn():
    nc = bass.Bass("TRN2", target_bir_lowering=False)
    qT = nc.dram_tensor("qT", [128, SEQ], BF16, kind="ExternalInput").ap()
    kT = nc.dram_tensor("kT", [128, SEQ], BF16, kind="ExternalInput").ap()
    vP = nc.dram_tensor("vP", [128, SEQ], BF16, kind="ExternalInput").ap()
    oT = nc.dram_tensor("oT", [128, SEQ], F32, kind="ExternalOutput").ap()
    scale = 128.0 ** -0.5
    with ExitStack() as st:
        K = KB(nc, st)
        q = K.sb([128, SEQ], BF16, "q")
        k = K.sb([128, SEQ], BF16, "k")
        v = K.sb([128, SEQ], BF16, "v")
        for h in range(4):
            sl = slice(h * 4096, (h + 1) * 4096)
            K.dma("sp", k, k[:, sl], None, kT[:, sl])
            K.dma("pool", q, q[:, sl], None, qT[:, sl])
            K.dma("sp", v, v[:, sl], None, vP[:, sl])
        ones = K.sb([128, 128], BF16, "ones")
        K.memset("pool", ones, ones[:], 1.0)
        pS = [K.ps([128, 512], F32, f"pS{i}") for i in range(3)]
        pO = [K.ps([128, 512], F32, f"pO{i}") for i in range(2)]
        pZ = [K.ps([128, 512], F32, f"pZ{i}") for i in range(2)]
        pT = Rot([K.sb([128, 512], BF16, f"pT{i}") for i in range(3)])
        rec = Rot([K.sb([128, 512], F32, f"rec{i}") for i in range(2)])
        osb = Rot([K.sb([128, 512], F32, f"osb{i}") for i in range(2)])
        NQ, NKB = SEQ // 512, SEQ // 128
        seq = [(a, b) for a in range(NQ) for b in range(NKB)]
        LOOK = 2

        def issue_S(idx):
            qt, kb = seq[idx]
            ps = pS[idx % 3]
            K.mm(ps, ps[:], k, k[:, kb * 128:(kb + 1) * 128], q, q[:, qt * 512:(qt + 1) * 512])

        for i in range(LOOK):
            issue_S(i)
        for idx, (qt, kb) in enumerate(seq):
            if idx + LOOK < len(seq):
                issue_S(idx + LOOK)
            ps = pS[idx % 3]
            p = pT.next()
            K.act(p, p[:], ps, ps[:], AF.Exp, scale=scale)
            o, z = pO[qt % 2], pZ[qt % 2]
            K.mm(o, o[:], v, v[:, kb * 128:(kb + 1) * 128], p, p[:], start=(kb == 0), stop=(kb == NKB - 1))
            K.mm(z, z[:], ones, ones[:], p, p[:], start=(kb == 0), stop=(kb == NKB - 1))
            if kb == NKB - 1:
                r = rec.next()
                K.op("dve", lambda e: e.reciprocal(out=r[:], in_=z[:]), [z], [r])
                ob = osb.next()
                K.tt("dve", ob, ob[:], o, o[:], r, r[:], ALU.mult)
                K.dma("pool", None, oT[:, qt * 512:(qt + 1) * 512], ob, ob[:])
        K.finish()
    return nc


def run_attn(obf):
    nc = build_attn()
    maps = []
    for i in range(NCORES):
        g = i // 4
        vT = obf[1280 + 128 * g:1280 + 128 * g + 128, :]
        vP = np.ascontiguousarray(vT.reshape(128, SEQ // 128, 128).transpose(2, 1, 0).reshape(128, SEQ))
        maps.append({"qT": np.ascontiguousarray(obf[128 * i:128 * i + 128, :]),
                     "kT": np.ascontiguousarray(obf[1024 + 128 * g:1024 + 128 * g + 128, :]),
                     "vP": vP})
    res = _run(nc, maps)
    return np.concatenate([r["oT"] for r in res], axis=0)


S5T = 32
S5NC = SEQ // S5T
TWO_PI = 2.0 * math.pi
S5_OFF = TWO_PI * 128


def s5_consts():
    s = np.arange(32, dtype=np.float32)
    et = np.concatenate([-s, 31 - s, s, s + 1, np.array([32.0], np.float32)]).astype(np.float32)
    ET = np.ascontiguousarray(np.broadcast_to(et, (64, 129)))
    mask = np.zeros((128, 4, 512), np.float32)
    for kb in range(4):
        for sl in range(8):
            sg = 8 * kb + sl
            for t in range(32):
                if sg <= t:
                    mask[sl * 16:(sl + 1) * 16, kb, t * 16:(t + 1) * 16] = 1.0
    ident = np.eye(64, dtype=np.float32)
    return ET, mask, ident


def build_s5(NPR=16):
    nc = bass.Bass("TRN2", target_bir_lowering=False)
    U = nc.dram_tensor("U", [NPR, 4, 128, S5NC], BF16, kind="ExternalInput").ap()
    PRM = nc.dram_tensor("PRM", [NPR, 64, 67], F32, kind="ExternalInput").ap()
    ETd = nc.dram_tensor("ET", [64, 129], F32, kind="ExternalInput").ap()
    MKd = nc.dram_tensor("MK", [128, 4, 512], F32, kind="ExternalInput").ap()
    IDd = nc.dram_tensor("ID", [64, 64], F32, kind="ExternalInput").ap()
    Y = nc.dram_tensor("Y", [NPR, 4, 128, S5NC], F32, kind="ExternalOutput").ap()
    with ExitStack() as st:
        K = KB(nc, st)
        ET = K.sb([64, 129], F32, "ET"); K.dma("sp", ET, ET[:], None, ETd[:, :])
        MK = K.sb([128, 4, 512], F32, "MK"); K.dma("sp", MK, MK[:], None, MKd[:, :, :])
        ID = K.sb([64, 64], F32, "ID"); K.dma("sp", ID, ID[:], None, IDd[:, :])
        uR = Rot([K.sb([128, 4, S5NC], BF16, f"u{i}") for i in range(2)])
        prR = Rot([K.sb([64, 67], F32, f"prm{i}") for i in range(2)])
        sc = K.sb([64, 16], F32, "sc")
        mag = K.sb([64, 129], F32, "mag")
        ang = K.sb([64, 129], F32, "ang")
        ang2 = K.sb([64, 129], F32, "ang2")
        angi = K.sb([64, 129], mybir.dt.int32, "angi")
        PWr = K.sb([64, 129], F32, "PWr")
        PWi = K.sb([64, 129], F32, "PWi")
        bb = K.sb([64, 32], F32, "bb")
        t1 = K.sb([64, 512], F32, "t1")
        t2 = K.sb([64, 512], F32, "t2")
        BLr = K.sb([64, 512], F32, "BLr"); BLi = K.sb([64, 512], F32, "BLi")
        WSr = K.sb([64, 512], F32, "WSr"); WSi = K.sb([64, 512], F32, "WSi")
        CLr = K.sb([64, 512], F32, "CLr"); CLm = K.sb([64, 512], F32, "CLm")
        WOr = K.sb([64, 512], BF16, "WOr"); WOm = K.sb([64, 512], BF16, "WOm")
        Msb = K.sb([128, 4, 512], BF16, "Msb")
        WSs = K.sb([128, 4, 128], BF16, "WSs")
        Pre = [K.sb([64, S5NC], F32, f"Pre{i}") for i in range(2)]
        Pim = [K.sb([64, S5NC], F32, f"Pim{i}") for i in range(2)]
        Hre = K.sb([64, S5NC], BF16, "Hre"); Him = K.sb([64, S5NC], BF16, "Him")
        K.memset("pool", Hre, Hre[:], 0.0); K.memset("pool", Him, Him[:], 0.0)
        A = K.sb([64, 8], F32, "A")
        ysb = Rot([K.sb([128, S5NC], F32, f"ysb{i}") for i in range(2)])
        pM = K.ps([128, 512], F32, "pM")
        pW = K.ps([128, 128], F32, "pW")
        pEr = K.ps([64, 512], F32, "pEr"); pEi = K.ps([64, 512], F32, "pEi")
        pY = [K.ps([128, 512], F32, f"pY{i}") for i in range(2)]

        def v3(ap):
            return ap.rearrange("p (s c) -> p s c", c=16)

        def bs(buf, a):
            return buf[:, a:a + 32].unsqueeze(2).to_broadcast([64, 32, 16])

        def bc(buf, a):
            return buf[:, a:a + 16].unsqueeze(1).to_broadcast([64, 32, 16])

        def table(outr, outi, a, xb, xr, xi, neg_im):
            K.tt("dve", t1, v3(t1[:, :]), PWr, bs(PWr, a), xb, bc(xb, xr), ALU.mult)
            K.tt("pool", t2, v3(t2[:, :]), PWi, bs(PWi, a), xb, bc(xb, xi), ALU.mult)
            K.tt("dve", outr, v3(outr[:, :]), t1, v3(t1[:, :]), t2, v3(t2[:, :]), ALU.subtract)
            K.tt("dve", t1, v3(t1[:, :]), PWr, bs(PWr, a), xb, bc(xb, xi), ALU.mult)
            K.tt("pool", t2, v3(t2[:, :]), PWi, bs(PWi, a), xb, bc(xb, xr), ALU.mult)
            if neg_im:
                K.stt(outi, v3(outi[:, :]), t1, v3(t1[:, :]), -1.0, t2, v3(t2[:, :]), ALU.mult, ALU.subtract)
            else:
                K.tt("dve", outi, v3(outi[:, :]), t1, v3(t1[:, :]), t2, v3(t2[:, :]), ALU.add)

        def load(pr):
            u = uR.next(); p = prR.next()
            K.dma("sp", u, u[:], None, U[pr].rearrange("k p j -> p k j"))
            K.dma("sp", p, p[:], None, PRM[pr])
            return u, p

        pend = load(0)
        for pr in range(NPR):
            u, p = pend
            if pr + 1 < NPR:
                pend = load(pr + 1)
            K.act(sc, sc[:, 0:1], p, p[:, 2:3], AF.Exp)
            K.tt("dve", sc, sc[:, 1:2], p, p[:, 0:1], sc, sc[:, 0:1], ALU.mult)
            K.tt("dve", sc, sc[:, 2:3], p, p[:, 1:2], sc, sc[:, 0:1], ALU.mult)
            K.act(mag, mag[:], ET, ET[:], AF.Exp, extra=[sc], scale=sc[:, 1:2])
            K.ts("dve", ang, ang[:], ET, ET[:], sc[:, 2:3], S5_OFF, ALU.mult, ALU.add, extra=[sc])
            K.ts("dve", ang2, ang2[:], ang, ang[:], 1.0 / TWO_PI, None, ALU.mult)
            K.copy("dve", angi, angi[:], ang2, ang2[:])
            K.copy("dve", ang2, ang2[:], angi, angi[:])
            K.stt(ang, ang[:], ang2, ang2[:], -TWO_PI, ang, ang[:], ALU.mult, ALU.add)
            K.act(PWi, PWi[:], ang, ang[:], AF.Sin, scale=0.5)
            K.act(ang2, ang2[:], ang, ang[:], AF.Sin, scale=0.25)
            K.tt("dve", ang2, ang2[:], ang2, ang2[:], ang2, ang2[:], ALU.mult)
            K.ts("dve", ang2, ang2[:], ang2, ang2[:], -2.0, 1.0, ALU.mult, ALU.add)
            K.tt("dve", PWr, PWr[:], PWi, PWi[:], PWi, PWi[:], ALU.mult)
            K.ts("dve", PWr, PWr[:], PWr, PWr[:], -2.0, 1.0, ALU.mult, ALU.add)
            K.stt(PWi, PWi[:], PWi, PWi[:], 2.0, ang2, ang2[:], ALU.mult, ALU.mult)
            K.tt("dve", PWr, PWr[:], PWr, PWr[:], mag, mag[:], ALU.mult)
            K.tt("dve", PWi, PWi[:], PWi, PWi[:], mag, mag[:], ALU.mult)
            K.ts("dve", sc, sc[:, 3:4], PWr, PWr[:, 65:66], 1.0, None, ALU.subtract)
            K.tt("dve", sc, sc[:, 4:5], p, p[:, 0:1], p, p[:, 0:1], ALU.mult)
            K.stt(sc, sc[:, 4:5], p, p[:, 1:2], p[:, 1:2], sc, sc[:, 4:5], ALU.mult, ALU.add)
            K.op("dve", lambda e: e.reciprocal(out=sc[:, 4:5], in_=sc[:, 4:5]), [sc], [sc])
            K.tt("dve", sc, sc[:, 5:6], sc, sc[:, 3:4], p, p[:, 0:1], ALU.mult)
            K.stt(sc, sc[:, 5:6], PWi, PWi[:, 65:66], p[:, 1:2], sc, sc[:, 5:6], ALU.mult, ALU.add, extra=[p])
            K.tt("dve", sc, sc[:, 5:6], sc, sc[:, 5:6], sc, sc[:, 4:5], ALU.mult)
            K.tt("dve", sc, sc[:, 6:7], sc, sc[:, 3:4], p, p[:, 1:2], ALU.mult)
            K.stt(sc, sc[:, 6:7], PWi, PWi[:, 65:66], p[:, 0:1], sc, sc[:, 6:7], ALU.mult, ALU.subtract, extra=[p])
            K.tt("dve", sc, sc[:, 6:7], sc, sc[:, 6:7], sc, sc[:, 4:5], ALU.mult)
            K.ts("dve", t1, t1[:, 0:16], p, p[:, 19:35], sc[:, 6:7], None, ALU.mult, extra=[sc])
            K.stt(bb, bb[:, 0:16], p, p[:, 3:19], sc[:, 5:6], t1, t1[:, 0:16], ALU.mult, ALU.subtract, extra=[sc])
            K.ts("dve", t1, t1[:, 0:16], p, p[:, 3:19], sc[:, 6:7], None, ALU.mult, extra=[sc])
            K.stt(bb, bb[:, 16:32], p, p[:, 19:35], sc[:, 5:6], t1, t1[:, 0:16], ALU.mult, ALU.add, extra=[sc])
            table(BLr, BLi, 0, bb, 0, 16, False)
            table(WSr, WSi, 32, bb, 0, 16, False)
            table(CLr, CLm, 64, p, 35, 51, True)
            table(WOr, WOm, 96, p, 35, 51, True)
            for kb in range(4):
                ks = slice(kb * 128, (kb + 1) * 128)
                K.mm(pM, pM[:], BLr, BLr[:, ks], CLr, CLr[:, :], start=True, stop=False)
                K.mm(pM, pM[:], BLi, BLi[:, ks], CLm, CLm[:, :], start=False, stop=True)
                K.tt("dve", Msb, Msb[:, kb, :], pM, pM[:], MK, MK[:, kb, :], ALU.mult)
                K.mm(pW, pW[:, 0:64], WSr, WSr[:, ks], ID, ID[:], start=True, stop=True)
                K.mm(pW, pW[:, 64:128], WSi, WSi[:, ks], ID, ID[:], start=True, stop=True)
                K.copy("act", WSs, WSs[:, kb, :], pW, pW[:])
            for kb in range(4):
                K.mm(pEr, pEr[:], WSs, WSs[:, kb, 0:64], u, u[:, kb, :], start=(kb == 0), stop=(kb == 3))
            for kb in range(4):
                K.mm(pEi, pEi[:], WSs, WSs[:, kb, 64:128], u, u[:, kb, :], start=(kb == 0), stop=(kb == 3))
            K.copy("act", Pre[0], Pre[0][:], pEr, pEr[:])
            K.copy("dve", Pim[0], Pim[0][:], pEi, pEi[:])
            K.copy("dve", A, A[:, 0:1], PWr, PWr[:, 128:129])
            K.copy("dve", A, A[:, 1:2], PWi, PWi[:, 128:129])
            K.ts("dve", A, A[:, 2:3], PWi, PWi[:, 128:129], -1.0, None, ALU.mult)
            cur = 0
            d = 1
            while d < S5NC:
                re, im, nre, nim = Pre[cur], Pim[cur], Pre[1 - cur], Pim[1 - cur]
                n = S5NC
                K.stt(nre, nre[:, d:n], re, re[:, 0:n - d], A[:, 0:1], re, re[:, d:n], ALU.mult, ALU.add, extra=[A])
                K.stt(nre, nre[:, d:n], im, im[:, 0:n - d], A[:, 2:3], nre, nre[:, d:n], ALU.mult, ALU.add, extra=[A])
                K.stt(nim, nim[:, d:n], im, im[:, 0:n - d], A[:, 0:1], im, im[:, d:n], ALU.mult, ALU.add, extra=[A])
                K.stt(nim, nim[:, d:n], re, re[:, 0:n - d], A[:, 1:2], nim, nim[:, d:n], ALU.mult, ALU.add, extra=[A])
                K.copy("pool", nre, nre[:, 0:d], re, re[:, 0:d])
                K.copy("pool", nim, nim[:, 0:d], im, im[:, 0:d])
                cur = 1 - cur
                d *= 2
                if d < S5NC:
                    K.tt("dve", A, A[:, 3:4], A, A[:, 0:1], A, A[:, 0:1], ALU.mult)
                    K.stt(A, A[:, 3:4], A, A[:, 1:2], A[:, 2:3], A, A[:, 3:4], ALU.mult, ALU.add)
                    K.stt(A, A[:, 4:5], A, A[:, 0:1], 2.0, A, A[:, 1:2], ALU.mult, ALU.mult)
                    K.copy("dve", A, A[:, 0:1], A, A[:, 3:4])
                    K.copy("dve", A, A[:, 1:2], A, A[:, 4:5])
                    K.ts("dve", A, A[:, 2:3], A, A[:, 4:5], -1.0, None, ALU.mult)
            K.copy("act", Hre, Hre[:, 1:S5NC], Pre[cur], Pre[cur][:, 0:S5NC - 1])
            K.copy("dve", Him, Him[:, 1:S5NC], Pim[cur], Pim[cur][:, 0:S5NC - 1])
            for tb in range(4):
                tsl = slice(tb * 128, (tb + 1) * 128)
                py = pY[tb % 2]
                for kb in range(tb + 1):
                    K.mm(py, py[:], Msb, Msb[:, kb, tsl], u, u[:, kb, :], start=(kb == 0), stop=False)
                K.mm(py, py[:], WOr, WOr[:, tsl], Hre, Hre[:], start=False, stop=False)
                K.mm(py, py[:], WOm, WOm[:, tsl], Him, Him[:], start=False, stop=True)
                yb = ysb.next()
                K.copy("act" if tb % 2 == 0 else "dve", yb, yb[:], py, py[:])
                K.dma("pool", None, Y[pr, tb], yb, yb[:])
        K.finish()
    return nc


def s5_pack_u(uT):
    outs = []
    for i in range(NCORES):
        U = np.empty((16, 4, 128, S5NC), uT.dtype)
        for gl in range(8):
            g = 8 * i + gl
            ug = uT[16 * g:16 * g + 16, :]
            for d in range(2):
                x = ug[:, ::-1] if d == 1 else ug
                x = x.reshape(16, S5NC, 32).transpose(2, 0, 1)
                U[2 * gl + d] = x.reshape(4, 128, S5NC)
        outs.append(U)
    return outs


def s5_unpack_y(Ys):
    yf = np.empty((1024, SEQ), np.float32); yb = np.empty((1024, SEQ), np.float32)
    for i in range(NCORES):
        for gl in range(8):
            g = 8 * i + gl
            for d in range(2):
                y = Ys[i][2 * gl + d].reshape(32, 16, S5NC).transpose(1, 2, 0).reshape(16, SEQ)
                if d == 0:
                    yf[16 * g:16 * g + 16] = y
                else:
                    yb[16 * g:16 * g + 16] = y[:, ::-1]
    return yf, yb


def s5_params(lam_re, lam_im, log_step, b_re, b_im, c_re, c_im):
    outs = []
    for i in range(NCORES):
        P = np.empty((16, 64, 67), np.float32)
        for gl in range(8):
            g = 8 * i + gl
            for d in range(2):
                pr = 2 * gl + d
                P[pr, :, 0] = lam_re[d, g]; P[pr, :, 1] = lam_im[d, g]; P[pr, :, 2] = log_step[d, g]
                P[pr, :, 3:19] = b_re[d, g]; P[pr, :, 19:35] = b_im[d, g]
                P[pr, :, 35:51] = c_re[d, g].T; P[pr, :, 51:67] = c_im[d, g].T
        outs.append(P)
    return outs


def run_s5(uT, lam_re, lam_im, log_step, b_re, b_im, c_re, c_im):
    nc = build_s5()
    ET, MK, ID = s5_consts()
    Us = s5_pack_u(uT)
    Ps = s5_params(lam_re, lam_im, log_step, b_re, b_im, c_re, c_im)
    maps = [{"U": Us[i], "PRM": Ps[i], "ET": ET, "MK": MK, "ID": ID} for i in range(NCORES)]
    res = _run(nc, maps)
    return s5_unpack_y([r["Y"] for r in res])


def build_post(layer, T=TPC):
    nc = bass.Bass("TRN2", target_bir_lowering=False)
    def di(name, shape, dt=F32):
        return nc.dram_tensor(name, shape, dt, kind="ExternalInput").ap()
    xT = di("xT", [D, T]); w_out = di("w_out", [D, D]); modd = di("mod", [128, 48])
    aT = di("aT", [1024, T]); agT = di("agT", [1024, T])
    if layer == 0:
        yfT = di("yfT", [1024, T]); ybT = di("ybT", [1024, T]); uT = di("uT", [1024, T], BF16); gbT = di("gbT", [1024, T])
        s5d = di("s5d", [128, 8]); w_glu = di("w_glu", [1024, 2048]); bglu = di("bglu", [128, 16])
    else:
        ysT = di("ysT", [1024, T]); ys2T = di("ys2T", [1024, T]); zT = di("zT", [1024, T]); nw = di("nw", [128, 8])
    oT = nc.dram_tensor("oT", [D, T], F32, kind="ExternalOutput").ap()
    NT = T // 512
    with ExitStack() as st:
        K = KB(nc, st)
        mod = K.sb([128, 48], F32, "mod"); K.dma("sp", mod, mod[:], None, modd[:, :])
        wo = K.sb([128, 16, D], BF16, "wo")
        stg = Rot([K.sb([128, D], F32, f"stg{i}") for i in range(2)])
        cast_i = [0]

        def load_w(dst, w_ap, nk):
            for k in range(nk):
                s_ = stg.next()
                K.dma("sp" if k % 2 == 0 else "pool", s_, s_[:], None, w_ap[128 * k:128 * k + 128, :])
                e = ("dve", "pool", "act")[cast_i[0] % 3]; cast_i[0] += 1
                K.copy(e, dst, dst[:, k, :], s_, s_[:])
        if layer == 0:
            wg = K.sb([128, 8, 2048], BF16, "wg")
            load_w(wg, w_glu, 8)
            sd = K.sb([128, 8], F32, "sd"); K.dma("sp", sd, sd[:], None, s5d[:, :])
            bg = K.sb([128, 16], F32, "bg"); K.dma("sp", bg, bg[:], None, bglu[:, :])
        else:
            nws = K.sb([128, 8], F32, "nws"); K.dma("sp", nws, nws[:], None, nw[:, :])
            ones = K.sb([128, 128], F32, "ones"); K.memset("pool", ones, ones[:], 1.0)
        load_w(wo, w_out, 16)
        A = Rot([K.sb([128, 16, 512], BF16, f"A{i}") for i in range(2)])
        ld = {nm: Rot([K.sb([128, 512], F32, f"ld_{nm}{i}") for i in range(3)]) for nm in ("a", "b", "c", "x")}
        ldu = Rot([K.sb([128, 512], BF16, f"ldu{i}") for i in range(2)])
        tmp = {nm: Rot([K.sb([128, 512], F32, f"tm_{nm}{i}") for i in range(2)]) for nm in ("p", "q", "r")}
        osb = Rot([K.sb([128, 512], F32, f"osb{i}") for i in range(3)])
        pacc = Rot([K.ps([128, 512], F32, f"pa{i}") for i in range(4)])
        pss = K.ps([128, 512], F32, "pss")
        if layer == 0:
            gy = Rot([K.sb([128, 8, 512], BF16, f"gy{i}") for i in range(2)])
        else:
            yz = Rot([K.sb([128, 8, 512], F32, f"yz{i}") for i in range(2)])
            rs = K.sb([128, 512], F32, "rs")

        def load(nm, ap, q="sp"):
            b = ld[nm].next()
            K.dma(q, b, b[:], None, ap)
            return b

        for t in range(NT):
            ts_ = slice(512 * t, 512 * t + 512)
            Ab = A.next()
            if layer == 0:
                g = gy.next()
                for k in range(8):
                    rows = slice(128 * k, 128 * k + 128)
                    yf = load("a", yfT[rows, ts_]); yb = load("b", ybT[rows, ts_], "pool")
                    ub = ldu.next(); K.dma("sp", ub, ub[:], None, uT[rows, ts_])
                    y = tmp["p"].next()
                    K.stt(y, y[:], ub, ub[:], sd[:, k:k + 1], yf, yf[:], ALU.mult, ALU.add, extra=[sd])
                    K.tt("pool", y, y[:], y, y[:], yb, yb[:], ALU.add)
                    q_ = tmp["q"].next()
                    K.tt("pool", q_, q_[:], y, y[:], y, y[:], ALU.mult)
                    K.ts("dve", q_, q_[:], q_, q_[:], 0.044715, 1.0, ALU.mult, ALU.add)
                    K.tt("dve", q_, q_[:], q_, q_[:], y, y[:], ALU.mult)
                    K.act(q_, q_[:], q_, q_[:], AF.Sigmoid, scale=1.5957691216057308)
                    K.tt("dve", g, g[:, k, :], q_, q_[:], y, y[:], ALU.mult)
                for n in range(8):
                    pv, pg = pacc.next(), pacc.next()
                    for k in range(8):
                        K.mm(pv, pv[:], wg, wg[:, k, 128 * n:128 * n + 128], g, g[:, k, :], start=(k == 0), stop=(k == 7))
                    for k in range(8):
                        K.mm(pg, pg[:], wg, wg[:, k, 1024 + 128 * n:1024 + 128 * n + 128], g, g[:, k, :], start=(k == 0), stop=(k == 7))
                    sg = tmp["r"].next()
                    K.act(sg, sg[:], pg, pg[:], AF.Sigmoid, extra=[bg], bias=bg[:, 8 + n:9 + n])
                    gb = load("c", gbT[128 * n:128 * n + 128, ts_])
                    v1 = tmp["p"].next()
                    K.stt(v1, v1[:], pv, pv[:], bg[:, n:n + 1], sg, sg[:], ALU.add, ALU.mult, extra=[bg])
                    K.tt("dve", Ab, Ab[:, 8 + n, :], v1, v1[:], gb, gb[:], ALU.mult)
            else:
                yzb = yz.next()
                for k in range(8):
                    rows = slice(128 * k, 128 * k + 128)
                    ys = load("a", ysT[rows, ts_]); z = load("b", zT[rows, ts_], "pool"); y2 = load("c", ys2T[rows, ts_])
                    K.tt("pool", ys, ys[:], ys, ys[:], y2, y2[:], ALU.add)
                    K.tt("dve", yzb, yzb[:, k, :], ys, ys[:], z, z[:], ALU.mult)
                    sq = tmp["q"].next()
                    K.act(sq, sq[:], yzb, yzb[:, k, :], AF.Square)
                    K.mm(pss, pss[:], ones, ones[:], sq, sq[:], start=(k == 0), stop=(k == 7))
                emit_rstd(K, rs, rs[:], pss, pss[:], 1.0 / 1024)
                for k in range(8):
                    K.stt(Ab, Ab[:, 8 + k, :], yzb, yzb[:, k, :], nws[:, k:k + 1], rs, rs[:], ALU.mult, ALU.mult, extra=[nws])
            for n in range(8):
                a = load("a", aT[128 * n:128 * n + 128, ts_]); ag = load("b", agT[128 * n:128 * n + 128, ts_], "pool")
                K.tt("pool", Ab, Ab[:, n, :], a, a[:], ag, ag[:], ALU.mult)
            for n in range(16):
                po = pacc.next()
                for k in range(16):
                    K.mm(po, po[:], wo, wo[:, k, 128 * n:128 * n + 128], Ab, Ab[:, k, :], start=(k == 0), stop=(k == 15))
                xb = load("x", xT[128 * n:128 * n + 128, ts_])
                ob = osb.next()
                K.stt(ob, ob[:], po, po[:], mod[:, 32 + n:33 + n], xb, xb[:], ALU.mult, ALU.add, extra=[mod])
                K.dma("pool", None, oT[128 * n:128 * n + 128, ts_], ob, ob[:])
        K.finish()
    return nc


def run_post(layer, xT, mod, w_out, aT, agT, **kw):
    nc = build_post(layer)
    maps = []
    for i in range(NCORES):
        sl = slice(i * TPC, (i + 1) * TPC)
        m = {"xT": np.ascontiguousarray(xT[:, sl]), "w_out": np.ascontiguousarray(w_out), "mod": np.ascontiguousarray(mod),
             "aT": np.ascontiguousarray(aT[:, sl]), "agT": np.ascontiguousarray(agT[:, sl])}
        if layer == 0:
            for nm in ("yfT", "ybT", "uT", "gbT"):
                m[nm] = np.ascontiguousarray(kw[nm][:, sl])
            m["s5d"] = pj(kw["s5_d"], 8); m["w_glu"] = np.ascontiguousarray(kw["w_glu"]); m["bglu"] = pj(kw["b_glu"], 16)
        else:
            for nm in ("ysT", "ys2T", "zT"):
                m[nm] = np.ascontiguousarray(kw[nm][:, sl])
            m["nw"] = pj(kw["norm_w"], 8)
        maps.append(m)
    res = _run(nc, maps)
    return np.concatenate([r["oT"] for r in res], axis=1)


GW = 64
NROW = SEQ // GW


def na_bias_tables(rpb_h):
    col = np.arange(GW)
    cs = np.clip(col - 8, 0, GW - 16)
    cmask = (col[None, :] >= cs[:, None]) & (col[None, :] < cs[:, None] + 16)
    dc = np.clip(col[None, :] - col[:, None], -15, 15) + 15
    out = np.empty((8, GW, 8, GW), np.float32)
    for v in range(8):
        off = 7 - v
        for kr in range(8):
            b = rpb_h[off + kr][dc]
            b = np.where(cmask, b, np.float32(-1e30)).astype(np.float32)
            out[v, :, kr, :] = b.T
    return out


def build_na():
    nc = bass.Bass("TRN2", target_bir_lowering=False)
    qT = nc.dram_tensor("qT", [128, SEQ], BF16, kind="ExternalInput").ap()
    kT = nc.dram_tensor("kT", [128, SEQ], BF16, kind="ExternalInput").ap()
    vR = nc.dram_tensor("vR", [64, NROW * 128], BF16, kind="ExternalInput").ap()
    bT = nc.dram_tensor("bT", [64, 8, 512], F32, kind="ExternalInput").ap()
    oT = nc.dram_tensor("oT", [128, SEQ], F32, kind="ExternalOutput").ap()
    scale = 128.0 ** -0.5
    with ExitStack() as st:
        K = KB(nc, st)
        q = K.sb([128, SEQ], BF16, "q"); k = K.sb([128, SEQ], BF16, "k"); v = K.sb([64, NROW * 128], BF16, "v")
        bias = K.sb([64, 8, 512], F32, "bias")
        for h in range(4):
            sl = slice(h * 4096, (h + 1) * 4096)
            K.dma("sp", k, k[:, sl], None, kT[:, sl])
            K.dma("pool", q, q[:, sl], None, qT[:, sl])
        for h in range(4):
            sl = slice(h * 8192, (h + 1) * 8192)
            K.dma("sp", v, v[:, sl], None, vR[:, sl])
        K.dma("sp", bias, bias[:], None, bT[:, :, :])
        ones = K.sb([64, 128], BF16, "ones"); K.memset("pool", ones, ones[:], 1.0)
        pS = Rot([K.ps([64, 512], F32, f"pS{i}") for i in range(3)])
        pO = Rot([K.ps([128, 512], F32, f"pO{i}") for i in range(2)])
        pZ = Rot([K.ps([128, 512], F32, f"pZ{i}") for i in range(2)])
        tS = Rot([K.sb([64, 512], F32, f"tS{i}") for i in range(3)])
        pT = Rot([K.sb([64, 512], BF16, f"pT{i}") for i in range(3)])
        rec = Rot([K.sb([128, 512], F32, f"rec{i}") for i in range(2)])
        osb = Rot([K.sb([128, 512], F32, f"osb{i}") for i in range(2)])
        for r0 in range(0, NROW, 8):
            o, z = pO.next(), pZ.next()
            for rr in range(8):
                r = r0 + rr
                rs = min(max(r - 4, 0), NROW - 8)
                var = r if r < 4 else (4 if r <= 252 else 4 + (r - 252))
                ps = pS.next()
                qs = slice(r * 64, r * 64 + 64)
                for j in range(8):
                    K.mm(ps, ps[:, j * 64:(j + 1) * 64], k, k[:, (rs + j) * 64:(rs + j) * 64 + 64], q, q[:, qs])
                t = tS.next()
                K.stt(t, t[:], ps, ps[:], scale, bias, bias[:, var, :], ALU.mult, ALU.add)
                p = pT.next()
                K.act(p, p[:], t, t[:], AF.Exp)
                cs_ = slice(rr * 64, rr * 64 + 64)
                for j in range(8):
                    K.mm(o, o[:, cs_], v, v[:, (rs + j) * 128:(rs + j) * 128 + 128], p, p[:, j * 64:(j + 1) * 64], start=(j == 0), stop=(j == 7))
                for j in range(8):
                    K.mm(z, z[:, cs_], ones, ones[:], p, p[:, j * 64:(j + 1) * 64], start=(j == 0), stop=(j == 7))
            rc = rec.next()
            K.op("dve", lambda e: e.reciprocal(out=rc[:], in_=z[:]), [z], [rc])
            ob = osb.next()
            K.tt("dve", ob, ob[:], o, o[:], rc, rc[:], ALU.mult)
            K.dma("pool", None, oT[:, r0 * 64:r0 * 64 + 512], ob, ob[:])
        K.finish()
    return nc


def run_na(obf, rpb):
    nc = build_na()
    maps = []
    for i in range(NCORES):
        vT = obf[2048 + 128 * i:2048 + 128 * i + 128, :]
        vR = np.ascontiguousarray(vT.reshape(128, NROW, 64).transpose(2, 1, 0).reshape(64, NROW * 128))
        bt = na_bias_tables(np.asarray(rpb[i], np.float32))
        bT = np.ascontiguousarray(bt.transpose(1, 0, 2, 3).reshape(64, 8, 512))
        maps.append({"qT": np.ascontiguousarray(obf[128 * i:128 * i + 128, :]),
                     "kT": np.ascontiguousarray(obf[1024 + 128 * i:1024 + 128 * i + 128, :]), "vR": vR, "bT": bT})
    res = _run(nc, maps)
    return np.concatenate([r["oT"] for r in res], axis=0)


NCH = SEQ // 128


def build_ssd(nchunks=NCH, ndirs=2):
    nc = bass.Bass("TRN2", target_bir_lowering=False)
    XBC = nc.dram_tensor("XBC", [2, 3, 128, SEQ + 4], F32, kind="ExternalInput").ap()
    CW = nc.dram_tensor("CW", [2, 3, 128, 6], F32, kind="ExternalInput").ap()
    DTR = nc.dram_tensor("DTR", [2, 2, 128, NCH], F32, kind="ExternalInput").ap()
    SCd = nc.dram_tensor("SC", [2, 128, 6], F32, kind="ExternalInput").ap()
    TRI = nc.dram_tensor("TRI", [128, 128], F32, kind="ExternalInput").ap()
    SUd = nc.dram_tensor("SU", [128, 128], F32, kind="ExternalInput").ap()
    IDd = nc.dram_tensor("ID", [128, 128], BF16, kind="ExternalInput").ap()
    Y = nc.dram_tensor("Y", [2, 128, NCH, 128], F32, kind="ExternalOutput").ap()
    PIECE = 2048
    with ExitStack() as st:
        K = KB(nc, st)
        tri = K.sb([128, 128], F32, "tri"); K.dma("sp", tri, tri[:], None, TRI[:, :])
        su = K.sb([128, 128], F32, "su"); K.dma("sp", su, su[:], None, SUd[:, :])
        idb = K.sb([128, 128], BF16, "idb"); K.dma("sp", idb, idb[:], None, IDd[:, :])
        ones = K.sb([128, 128], F32, "ones"); K.memset("pool", ones, ones[:], 1.0)
        stage = Rot([K.sb([128, PIECE + 4], F32, f"stage{i}") for i in range(2)])
        acc = K.sb([128, PIECE], F32, "acc")
        xfm = K.sb([128, PIECE], BF16, "xfm")
        BT = K.sb([128, SEQ], BF16, "BT"); CT = K.sb([128, SEQ], BF16, "CT")
        Xt = K.sb([128, NCH, 128], BF16, "Xt"); Bt = K.sb([128, NCH, 128], BF16, "Bt")
        cw = K.sb([128, 3, 6], F32, "cw")
        scs = K.sb([128, 6], F32, "scs")
        dtr = K.sb([128, 2, NCH], F32, "dtr")
        dt = K.sb([128, 2, NCH], F32, "dt"); adt = K.sb([128, 2, NCH], F32, "adt")
        cum = K.sb([128, 2, NCH], F32, "cum"); ecum = K.sb([128, 2, NCH], F32, "ecum")
        dts = K.sb([128, 2, NCH], F32, "dts"); etot = K.sb([128, 2, NCH], F32, "etot")
        na = K.sb([128, 2], F32, "na")
        stT = [K.sb([128, 64], F32, f"stT{h}") for h in range(2)]
        stB = [K.sb([128, 64], BF16, f"stB{h}") for h in range(2)]
        pA = Rot([K.ps([128, 128], F32, f"pA{i}") for i in range(2)])
        pG = Rot([K.ps([128, 128], F32, f"pG{i}") for i in range(2)])
        pY = Rot([K.ps([128, 128], F32, f"pYd{i}") for i in range(2)])
        pYo = K.ps([128, 128], F32, "pYo")
        pSt = K.ps([128, 128], F32, "pSt")
        scm = Rot([K.sb([128, 128], F32, f"scm{i}") for i in range(2)])
        lseg = Rot([K.sb([128, 128], F32, f"lseg{i}") for i in range(2)])
        Lm = Rot([K.sb([128, 128], F32, f"Lm{i}") for i in range(2)])
        Gt = Rot([K.sb([128, 128], BF16, f"Gt{i}") for i in range(2)])
        xd = Rot([K.sb([128, 64], BF16, f"xd{i}") for i in range(2)])
        xdd = Rot([K.sb([128, 64], BF16, f"xdd{i}") for i in range(2)])
        ysb = Rot([K.sb([128, 128], F32, f"ysb{i}") for i in range(3)])
        ws = K.sb([128, 2], F32, "ws")
        for d in range(ndirs):
            K.dma("sp", cw, cw[:], None, CW[d].rearrange("g p t -> p g t"))
            K.dma("sp", scs, scs[:], None, SCd[d])
            K.dma("sp", dtr, dtr[:], None, DTR[d].rearrange("h p c -> p h c"))
            import os
            for g in range(0 if os.environ.get('SKIP_CONV') else 3):
                for pc in range(SEQ // PIECE):
                    sg = stage.next()
                    K.dma("sp" if pc % 2 == 0 else "pool", sg, sg[:], None, XBC[d, g, :, pc * PIECE:pc * PIECE + PIECE + 4])
                    K.ts("dve", acc, acc[:], sg, sg[:, 0:PIECE], cw[:, g, 0:1], None, ALU.mult, extra=[cw])
                    for j in range(1, 5):
                        K.stt(acc, acc[:], sg, sg[:, j:j + PIECE], cw[:, g, j:j + 1], acc, acc[:], ALU.mult, ALU.add, extra=[cw])
                    dst, dbuf = ((xfm[:, :], xfm), (BT[:, pc * PIECE:(pc + 1) * PIECE], BT), (CT[:, pc * PIECE:(pc + 1) * PIECE], CT))[g]
                    K.act(dbuf, dst, acc, acc[:], AF.Silu, extra=[cw], bias=cw[:, g, 5:6])
                    if g < 2:
                        for cc in range(PIECE // 128):
                            c = pc * (PIECE // 128) + cc
                            pa = pA.next()
                            src = xfm[:, cc * 128:(cc + 1) * 128] if g == 0 else BT[:, c * 128:(c + 1) * 128]
                            K.mm(pa, pa[:], dbuf, src, idb, idb[:])
                            tgt = Xt if g == 0 else Bt
                            K.copy("act" if cc % 2 == 0 else "pool" if False else "dve", tgt, tgt[:, c, :], pa, pa[:])
            import os
            for h in range(0 if os.environ.get('SKIP_DT') else 2):
                K.ts("dve", cum, cum[:, h, :], dtr, dtr[:, h, :], scs[:, h:h + 1], None, ALU.add, extra=[scs])
                K.act(ecum, ecum[:, h, :], cum, cum[:, h, :], AF.Exp)
                K.ts("dve", ecum, ecum[:, h, :], ecum, ecum[:, h, :], 1.0, None, ALU.add)
                K.ts("dve", dt, dt[:, h, :], cum, cum[:, h, :], 0.0, 0.3, ALU.max, ALU.add)
                for _it in range(6):
                    K.act(dts, dts[:, h, :], dt, dt[:, h, :], AF.Exp, scale=-1.0)
                    K.tt("dve", dts, dts[:, h, :], dts, dts[:, h, :], ecum, ecum[:, h, :], ALU.mult)
                    K.stt(dt, dt[:, h, :], dts, dts[:, h, :], -1.0, dt, dt[:, h, :], ALU.add, ALU.add)
                K.act(na, na[:, h:h + 1], scs, scs[:, 2 + h:3 + h], AF.Exp)
                K.ts("dve", na, na[:, h:h + 1], na, na[:, h:h + 1], -1.0, None, ALU.mult)
                K.ts("dve", adt, adt[:, h, :], dt, dt[:, h, :], na[:, h:h + 1], None, ALU.mult, extra=[na])
                pa = pA.next()
                K.mm(pa, pa[:], tri, tri[:], adt, adt[:, h, :])
                K.copy("dve", cum, cum[:, h, :], pa, pa[:])
                K.act(ecum, ecum[:, h, :], cum, cum[:, h, :], AF.Exp)
                pb = pA.next()
                K.mm(pb, pb[:], ones, ones[:], adt, adt[:, h, :])
                K.copy("dve", dts, dts[:, h, :], pb, pb[:])
                K.act(etot, etot[:, h, :], dts, dts[:, h, :], AF.Exp)
                K.tt("dve", dts, dts[:, h, :], dts, dts[:, h, :], cum, cum[:, h, :], ALU.subtract)
                K.act(dts, dts[:, h, :], dts, dts[:, h, :], AF.Exp)
                K.tt("dve", dts, dts[:, h, :], dts, dts[:, h, :], dt, dt[:, h, :], ALU.mult)
                K.memset("pool", stT[h], stT[h][:], 0.0)
                K.memset("pool", stB[h], stB[h][:], 0.0)
            for c in range(nchunks):
                cs_ = slice(c * 128, (c + 1) * 128)
                pa = pA.next()
                K.mm(pa, pa[:], BT, BT[:, cs_], CT, CT[:, cs_])
                sm = scm.next()
                K.tt("dve", sm, sm[:], pa, pa[:], tri, tri[:], ALU.mult)
                py = pY.next()
                for h in range(2):
                    hs = slice(64 * h, 64 * h + 64)
                    ls = lseg.next()
                    K.ts("pool", ls, ls[:], su, su[:], adt[:, h, c:c + 1], None, ALU.mult, extra=[adt])
                    pg = pG.next()
                    K.mm(pg, pg[:], ls, ls[:], tri, tri[:])
                    L = Lm.next()
                    K.act(L, L[:], pg, pg[:], AF.Exp)
                    G = Gt.next()
                    K.tt("dve", G, G[:], L, L[:], sm, sm[:], ALU.mult)
                    x1 = xd.next()
                    K.ts("pool", x1, x1[:], Xt, Xt[:, c, hs], dt[:, h, c:c + 1], None, ALU.mult, extra=[dt])
                    K.mm(py, py[:, hs], G, G[:], x1, x1[:])
                    K.mm(pYo, pYo[:, hs], CT, CT[:, cs_], stB[h], stB[h][:])
                    x2 = xdd.next()
                    K.ts("pool", x2, x2[:], Xt, Xt[:, c, hs], dts[:, h, c:c + 1], None, ALU.mult, extra=[dts])
                    K.mm(pSt, pSt[:, hs], Bt, Bt[:, c, :], x2, x2[:])
                yb = ysb.next()
                K.copy("act", yb, yb[:], py, py[:])
                for h in range(2):
                    hs = slice(64 * h, 64 * h + 64)
                    K.stt(yb, yb[:, hs], pYo, pYo[:, hs], ecum[:, h, c:c + 1], yb, yb[:, hs], ALU.mult, ALU.add, extra=[ecum])
                    if d == 0:
                        K.stt(yb, yb[:, hs], Xt, Xt[:, c, hs], scs[:, 4 + h:5 + h], yb, yb[:, hs], ALU.mult, ALU.add, extra=[scs])
                    K.stt(stT[h], stT[h][:], stT[h], stT[h][:], etot[:, h, c:c + 1], pSt, pSt[:, hs], ALU.mult, ALU.add, extra=[etot])
                    K.copy("act", stB[h], stB[h][:], stT[h], stT[h][:])
                K.dma("pool", None, Y[d, :, c, :], yb, yb[:])
        K.finish()
    return nc


def run_ssd(of1, conv_w, conv_b, dt_bias, a_log, ssd_d):
    nc = build_ssd()
    xbc = of1[2048:3584]
    dtr = of1[3584:3616]
    tri = np.triu(np.ones((128, 128), np.float32))
    su = np.tril(np.ones((128, 128), np.float32), -1)
    idm = np.eye(128, dtype=np.float32).astype(NPBF)
    maps = []
    for i in range(NCORES):
        g = i // 4
        rows = [slice(128 * i, 128 * i + 128), slice(1024 + 128 * g, 1024 + 128 * g + 128), slice(1280 + 128 * g, 1280 + 128 * g + 128)]
        XBC = np.zeros((2, 3, 128, SEQ + 4), np.float32)
        CWm = np.empty((2, 3, 128, 6), np.float32)
        DTR = np.empty((2, 2, 128, NCH), np.float32)
        SC = np.empty((2, 128, 6), np.float32)
        for d in range(2):
            for gi, rs in enumerate(rows):
                src = xbc[rs]
                XBC[d, gi, :, 2:SEQ + 2] = src[:, ::-1] if d == 1 else src
                w = conv_w[:, rs].T
                CWm[d, gi, :, 0:5] = w[:, ::-1] if d == 1 else w
                CWm[d, gi, :, 5] = conv_b[rs]
            for h in range(2):
                hd = 2 * i + h
                s_ = dtr[d * 16 + hd]
                s_ = s_[::-1] if d == 1 else s_
                DTR[d, h] = s_.reshape(NCH, 128).T
                SC[d, :, h] = dt_bias[d, hd]; SC[d, :, 2 + h] = a_log[d, hd]; SC[d, :, 4 + h] = ssd_d[hd]
        maps.append({"XBC": XBC, "CW": CWm, "DTR": DTR, "SC": SC, "TRI": tri, "SU": su, "ID": idm})
    res = _run(nc, maps)
    yf = np.empty((1024, SEQ), np.float32); yb = np.empty((1024, SEQ), np.float32)
    for i in range(NCORES):
        Yc = res[i]["Y"]
        for d in range(2):
            y = Yc[d].transpose(2, 1, 0).reshape(128, SEQ)
            if d == 0:
                yf[128 * i:128 * i + 128] = y
            else:
                yb[128 * i:128 * i + 128] = y[:, ::-1]
    return yf, yb


def kernel(x, c, e_norm_g, e_ada_w, e_ada_b, e_w_in, e_q_norm, e_k_norm, s5_lam_re, s5_lam_im,
           s5_log_step, s5_b_re, s5_b_im, s5_c_re, s5_c_im, s5_d, s5_w_glu, s5_b_glu, e_w_out,
           o_norm_g, o_ada_w, o_ada_b, o_w_in, o_q_norm, o_k_norm, na_rpb, ssd_conv_w, ssd_conv_b,
           ssd_dt_bias, ssd_a_log, ssd_d, ssd_norm_w, o_w_out):
    f = lambda a: np.asarray(a, np.float32)
    xT = np.ascontiguousarray(f(x)[0].T)
    obf0, of0, mod0 = run_front(0, xT, f(c), f(e_norm_g), f(e_ada_w), f(e_ada_b), f(e_w_in), f(e_q_norm), f(e_k_norm))
    oa = run_attn(obf0)
    yf, yb = run_s5(obf0[1536:2560], f(s5_lam_re)[0], f(s5_lam_im)[0], f(s5_log_step)[0], f(s5_b_re)[0], f(s5_b_im)[0],
                    f(s5_c_re)[0], f(s5_c_im)[0])
    x1T = run_post(0, xT, mod0, f(e_w_out)[0], oa, of0[0:1024], yfT=yf, ybT=yb, uT=obf0[1536:2560], gbT=of0[1024:2048],
                   s5_d=f(s5_d)[0], w_glu=f(s5_w_glu)[0], b_glu=f(s5_b_glu)[0])
    obf1, of1, mod1 = run_front(1, x1T, f(c), f(o_norm_g), f(o_ada_w), f(o_ada_b), f(o_w_in), f(o_q_norm), f(o_k_norm))
    oc = run_na(obf1, f(na_rpb)[0])
    sf, sb_ = run_ssd(of1, f(ssd_conv_w)[0], f(ssd_conv_b)[0], f(ssd_dt_bias)[0], f(ssd_a_log)[0], f(ssd_d)[0])
    x2T = run_post(1, x1T, mod1, f(o_w_out)[0], oc, of1[0:1024], ysT=sf, ys2T=sb_, zT=of1[1024:2048], norm_w=f(ssd_norm_w)[0])
    return np.ascontiguousarray(x2T.T)[None].astype(np.float32)
```

```python
import math
from contextlib import ExitStack
import numpy as np
import ml_dtypes
import concourse.bass as bass
import concourse.mybir as mybir
from concourse.bass_utils import run_bass_kernel_spmd

F32 = mybir.dt.float32
BF16 = mybir.dt.bfloat16
AF = mybir.ActivationFunctionType
ALU = mybir.AluOpType
AX = mybir.AxisListType
NPBF = ml_dtypes.bfloat16

NCORES = 8
D = 2048
SEQ = 16384
TPC = SEQ // NCORES
EPS = 1e-6


class Buf:
    def __init__(self, t, name, onchip=True):
        self.t = t
        self.name = name
        self.onchip = onchip
        self.last_w = None
        self.reads = {}
        self.dsem = None
        self.dcnt = 0

    def __getitem__(self, idx):
        return self.t[idx]


class KB:
    def __init__(self, nc, stack):
        self.nc = nc
        self.stack = stack
        self.eng = {"pe": nc.tensor, "dve": nc.vector, "act": nc.scalar, "pool": nc.gpsimd, "sp": nc.sync}
        self.sem, self.cnt, self.seen = {}, {}, {}
        for k in self.eng:
            self.sem[k] = stack.enter_context(nc.semaphore("s_" + k))
            self.cnt[k] = 0
            self.seen[k] = {}
        self.nbuf = 0
        self.dbufs = []

    def sb(self, shape, dt, name=None):
        self.nbuf += 1
        name = "s_" + (name or f"sb{self.nbuf}")
        return Buf(self.stack.enter_context(self.nc.sbuf_tensor(name, list(shape), dt)), name)

    def ps(self, shape, dt=F32, name=None):
        self.nbuf += 1
        name = "p_" + (name or f"ps{self.nbuf}")
        return Buf(self.stack.enter_context(self.nc.psum_tensor(name, list(shape), dt)), name)

    def _wait(self, e, tok):
        if tok is None:
            return
        sem, val, key = tok
        if key == e and e == "pe":
            return
        if self.seen[e].get(key, 0) >= val:
            return
        self.eng[e].wait_ge(sem, val)
        self.seen[e][key] = val

    def _deps(self, e, reads, writes):
        for b in reads:
            if b is not None:
                self._wait(e, b.last_w)
        for b in writes:
            if b is not None:
                self._wait(e, b.last_w)
                for r in b.reads.values():
                    self._wait(e, r)

    def _mark(self, tok, reads, writes):
        for b in reads:
            if b is not None:
                b.reads[tok[2]] = tok
        for b in writes:
            if b is not None:
                b.last_w = tok
                b.reads = {}

    def op(self, e, fn, reads=(), writes=()):
        self._deps(e, reads, writes)
        ins = fn(self.eng[e])
        self.cnt[e] += 1
        ins.then_inc(self.sem[e], 1)
        self._mark((self.sem[e], self.cnt[e], e), reads, writes)
        return ins

    def dma(self, q, out_b, out_ap, in_b, in_ap, **kw):
        self._deps(q, [in_b], [out_b])
        owner = out_b if out_b is not None else in_b
        if owner.dsem is None:
            owner.dsem = self.stack.enter_context(self.nc.semaphore("d_" + owner.name))
            self.dbufs.append(owner)
        ins = self.eng[q].dma_start(out=out_ap, in_=in_ap, **kw)
        owner.dcnt += 16
        ins.then_inc(owner.dsem, 16)
        tok = (owner.dsem, owner.dcnt, "d_" + owner.name)
        self._mark(tok, [in_b], [out_b])
        return tok

    def finish(self, e="sp"):
        for b in self.dbufs:
            self._wait(e, (b.dsem, b.dcnt, "d_" + b.name))

    def mm(self, ob, oap, lb, lap, rb, rap, start=True, stop=True):
        return self.op("pe", lambda e: e.matmul(oap, lhsT=lap, rhs=rap, start=start, stop=stop), [lb, rb], [ob])

    def act(self, ob, oap, ib, iap, func, extra=(), eng="act", **kw):
        return self.op(eng, lambda e: e.activation(out=oap, in_=iap, func=func, **kw), [ib, *extra], [ob])

    def ts(self, eng, ob, oap, ib, iap, s1, s2, op0, op1=None, extra=()):
        if op1 is None:
            return self.op(eng, lambda e: e.tensor_scalar(out=oap, in0=iap, scalar1=s1, scalar2=None, op0=op0), [ib, *extra], [ob])
        return self.op(eng, lambda e: e.tensor_scalar(out=oap, in0=iap, scalar1=s1, scalar2=s2, op0=op0, op1=op1), [ib, *extra], [ob])

    def tt(self, eng, ob, oap, ab, aap, bb, bap, op):
        return self.op(eng, lambda e: e.tensor_tensor(out=oap, in0=aap, in1=bap, op=op), [ab, bb], [ob])

    def stt(self, ob, oap, ab, aap, scalar, bb, bap, op0, op1, extra=()):
        return self.op("dve", lambda e: e.scalar_tensor_tensor(out=oap, in0=aap, scalar=scalar, in1=bap, op0=op0, op1=op1),
                       [ab, bb, *extra], [ob])

    def copy(self, eng, ob, oap, ib, iap):
        if eng == "act":
            return self.op("act", lambda e: e.copy(out=oap, in_=iap), [ib], [ob])
        return self.op(eng, lambda e: e.tensor_copy(out=oap, in_=iap), [ib], [ob])

    def memset(self, eng, ob, oap, val):
        return self.op(eng, lambda e: e.memset(oap, val), [], [ob])


TRACE = False
LAST_EXEC_NS = [None]


def _run(nc, in_maps):
    if TRACE:
        res = run_bass_kernel_spmd(nc, in_maps, core_ids=list(range(NCORES)), trace=True)
        LAST_EXEC_NS[0] = res.exec_time_ns
    else:
        res = run_bass_kernel_spmd(nc, in_maps, core_ids=list(range(NCORES)))
    return res.results


def wview(w_ap, n0, n1):
    return w_ap.rearrange("(k p) n -> p k n", p=128)[:, :, n0:n1]


class Rot:
    def __init__(self, items):
        self.items = items
        self.i = 0

    def next(self):
        b = self.items[self.i % len(self.items)]
        self.i += 1
        return b


def emit_mod(K, nc, dram, wst, pmod, consts):
    cT = K.sb([128, 16], F32, "cT")
    sc = K.sb([128, 16], F32, "sc")
    abT = K.sb([128, 48], F32, "abT")
    gT = K.sb([128, 16], F32, "gT")
    mod = K.sb([128, 48], F32, "mod")
    gs = K.sb([128, 16], F32, "gs")
    K.dma("sp", cT, cT[:], None, dram["cT"][:, :])
    K.dma("sp", abT, abT[:], None, dram["ada_bT"][:, :])
    K.dma("sp", gT, gT[:], None, dram["gT"][:, :])
    K.act(sc, sc[:], cT, cT[:], AF.Silu)
    NB = 6144 // 256
    pend = None
    for nb in range(NB + 1):
        cur = pend
        if nb < NB:
            st = wst.next()
            for h in range(4):
                K.dma("sp" if h % 2 == 0 else "pool", st, st[:, 4 * h:4 * h + 4, :], None,
                      wview(dram["ada_w"], nb * 256, nb * 256 + 256)[:, 4 * h:4 * h + 4, :])
            pend = (st, nb)
        if cur is not None:
            st, b = cur
            for jj in range(2):
                j = 2 * b + jj
                for k in range(16):
                    K.mm(pmod, pmod[:, j:j + 1], st, st[:, k, 128 * jj:128 * jj + 128], sc, sc[:, k:k + 1],
                         start=(k == 0), stop=(k == 15))
    K.tt("dve", mod, mod[:], pmod, pmod[:, 0:48], abT, abT[:], ALU.add)
    K.stt(gs, gs[:], mod, mod[:, 16:32], 1.0, gT, gT[:], ALU.add, ALU.mult)
    return mod, gs


def emit_rstd(K, ob, oap, ib, iap, inv_n):
    K.ts("dve", ob, oap, ib, iap, inv_n, EPS, ALU.mult, ALU.add)
    K.act(ob, oap, ob, oap, AF.Sqrt)
    K.op("dve", lambda e: e.reciprocal(out=oap, in_=oap), [ob], [ob])


def front_plan(N, kinds):
    chunks = []
    nbf = nf = 0
    for ci, kd in enumerate(kinds):
        n0 = ci * 128
        m = min(128, N - n0)
        isbf = (kd == "bf") or (isinstance(kd, tuple))
        if isbf:
            chunks.append((n0, m, kd, "bf", nbf)); nbf += m
        else:
            chunks.append((n0, m, kd, "f", nf)); nf += m
    return chunks, nbf, nf


def emit_front_core(K, nc, dram, N, kinds, rope, xsrc, outs, T=TPC):
    chunks, nbf, nf = front_plan(N, kinds)
    NT = T // 512
    ones = K.sb([128, 128], F32, "ones")
    K.memset("pool", ones, ones[:], 1.0)
    wst = Rot([K.sb([128, 16, 256], F32, f"wst{i}") for i in range(2)])
    wbs = Rot([K.sb([128, 16, 256], BF16, f"wb{i}") for i in range(2)])
    pmod = K.ps([128, 64], F32, "pmod")
    acc = [K.ps([128, 512], F32, f"acc{i}") for i in range(4)]
    mod, gs = emit_mod(K, nc, dram, wst, pmod, None)

    xTb = K.sb([128, 16, T], BF16, "xTb")
    rstd = K.sb([128, T], F32, "rstd")
    sq = K.sb([128, T], F32, "sq")
    pend = xsrc(0)
    for k in range(16):
        xs = pend
        if k + 1 < 16:
            pend = xsrc(k + 1)
        K.copy("dve", xTb, xTb[:, k, :], xs, xs[:, :])
        K.act(sq, sq[:], xs, xs[:, :], AF.Square)
        for t in range(NT):
            K.mm(acc[t], acc[t][:], ones, ones[:], sq, sq[:, 512 * t:512 * t + 512], start=(k == 0), stop=(k == 15))
    for t in range(NT):
        emit_rstd(K, rstd, rstd[:, 512 * t:512 * t + 512], acc[t], acc[t][:], 1.0 / D)

    if rope:
        cosT = K.sb([128, T], F32, "cosT")
        sinT = K.sb([128, T], F32, "sinT")
        K.dma("sp", cosT, cosT[:], None, dram["cosT"][:, :])
        K.dma("sp", sinT, sinT[:], None, dram["sinT"][:, :])
        swp = K.sb([128, 128], F32, "swp")
        K.dma("sp", swp, swp[:], None, dram["swapM"][:, :])
    qkg = K.sb([128, 2], F32, "qkg")
    K.dma("sp", qkg, qkg[:], None, dram["qkg"][:, :])
    psq = K.ps([128, 512], F32, "psq")
    psw = K.ps([128, 512], F32, "psw")
    bvec = K.sb([128, 64], F32, "bvec")
    tmpR = Rot([K.sb([128, 512], F32, f"tmp{i}") for i in range(2)])
    resR = Rot([K.sb([128, 512], F32, f"res{i}") for i in range(2)])
    resbR = Rot([K.sb([128, 512], BF16, f"resb{i}") for i in range(2)])
    sqkR = Rot([K.sb([128, 512], F32, f"sqk{i}") for i in range(2)])
    qnR = Rot([K.sb([128, 512], F32, f"qn{i}") for i in range(2)])
    accR = Rot(acc)

    NB = (N + 255) // 256

    def load_w(nb):
        st = wst.next()
        n0 = nb * 256
        w = min(256, N - n0)
        for h in range(4):
            K.dma("sp" if h % 2 == 0 else "pool", st, st[:, 4 * h:4 * h + 4, 0:w], None,
                  wview(dram["w_in"], n0, n0 + w)[:, 4 * h:4 * h + 4, :])
        return st

    pend = load_w(0)
    for nb in range(NB):
        st = pend
        if nb + 1 < NB:
            pend = load_w(nb + 1)
        n0 = nb * 256
        w = min(256, N - n0)
        wb = wbs.next()
        for k in range(16):
            if k % 2 == 0:
                K.ts("dve", wb, wb[:, k, 0:w], st, st[:, k, 0:w], gs[:, k:k + 1], None, ALU.mult, extra=[gs])
            else:
                K.act(wb, wb[:, k, 0:w], st, st[:, k, 0:w], AF.Identity, extra=[gs], scale=gs[:, k:k + 1])
        for jj in range((w + 127) // 128):
            ci = 2 * nb + jj
            cn0, m, kd, okind, orow = chunks[ci]
            c0 = 128 * jj
            for k in range(16):
                K.mm(pmod, pmod[0:m, 48:49], st, st[:, k, c0:c0 + m], mod, mod[:, k:k + 1], start=(k == 0), stop=(k == 15))
            bcol = bvec[0:m, ci:ci + 1]
            K.copy("dve", bvec, bcol, pmod, pmod[0:m, 48:49])
            for t in range(NT):
                ts_ = slice(512 * t, 512 * t + 512)
                a = accR.next()
                for k in range(16):
                    K.mm(a, a[0:m, :], wb, wb[:, k, c0:c0 + m], xTb, xTb[:, k, ts_], start=(k == 0), stop=(k == 15))
                tmp = tmpR.next()
                K.tt("dve", tmp, tmp[0:m, :], a, a[0:m, :], rstd, rstd[0:m, ts_], ALU.mult)
                if kd == "silu":
                    res = resR.next()
                    K.act(res, res[0:m, :], tmp, tmp[0:m, :], AF.Silu, extra=[bvec], bias=bcol)
                    K.dma("pool", None, outs["f"][orow:orow + m, ts_], res, res[0:m, :])
                elif kd == "f32":
                    res = resR.next()
                    K.act(res, res[0:m, :], tmp, tmp[0:m, :], AF.Identity, extra=[bvec], bias=bcol)
                    K.dma("pool", None, outs["f"][orow:orow + m, ts_], res, res[0:m, :])
                elif kd == "bf":
                    resb = resbR.next()
                    K.act(resb, resb[0:m, :], tmp, tmp[0:m, :], AF.Identity, extra=[bvec], bias=bcol)
                    K.dma("pool", None, outs["bf"][orow:orow + m, ts_], resb, resb[0:m, :])
                else:
                    _, gcol, do_rope = kd
                    res = resR.next()
                    K.act(res, res[:], tmp, tmp[:], AF.Identity, extra=[bvec], bias=bcol)
                    sqk = sqkR.next()
                    K.act(sqk, sqk[:], res, res[:], AF.Square)
                    K.mm(psq, psq[:], ones, ones[:], sqk, sqk[:])
                    emit_rstd(K, sqk, sqk[:], psq, psq[:], 1.0 / 128)
                    qn = qnR.next()
                    K.stt(qn, qn[:], res, res[:], qkg[:, gcol:gcol + 1], sqk, sqk[:], ALU.mult, ALU.mult, extra=[qkg])
                    resb = resbR.next()
                    if do_rope:
                        K.mm(psw, psw[:], swp, swp[:], qn, qn[:])
                        K.tt("dve", sqk, sqk[:], psw, psw[:], sinT, sinT[:, ts_], ALU.mult)
                        K.tt("dve", qn, qn[:], qn, qn[:], cosT, cosT[:, ts_], ALU.mult)
                        K.tt("dve", resb, resb[:], qn, qn[:], sqk, sqk[:], ALU.add)
                    else:
                        K.copy("dve", resb, resb[:], qn, qn[:])
                    K.dma("pool", None, outs["bf"][orow:orow + m, ts_], resb, resb[:])
    return mod


KINDS_E = [("qk", 0, True)] * 8 + [("qk", 1, True)] * 2 + ["bf"] * 2 + ["silu"] * 8 + ["bf"] * 8 + ["silu"] * 8
N_E = 4608
KINDS_O = [("qk", 0, False)] * 8 + [("qk", 1, False)] * 8 + ["bf"] * 8 + ["silu"] * 8 + ["silu"] * 8 + ["f32"] * 12 + ["f32"]
N_O = 6688


def front_dram(nc, N, rope, T=TPC):
    d = {}
    d["cT"] = nc.dram_tensor("cT", [128, 16], F32, kind="ExternalInput").ap()
    d["ada_bT"] = nc.dram_tensor("ada_bT", [128, 48], F32, kind="ExternalInput").ap()
    d["gT"] = nc.dram_tensor("gT", [128, 16], F32, kind="ExternalInput").ap()
    d["ada_w"] = nc.dram_tensor("ada_w", [D, 3 * D], F32, kind="ExternalInput").ap()
    d["w_in"] = nc.dram_tensor("w_in", [D, N], F32, kind="ExternalInput").ap()
    d["qkg"] = nc.dram_tensor("qkg", [128, 2], F32, kind="ExternalInput").ap()
    if rope:
        d["cosT"] = nc.dram_tensor("cosT", [128, T], F32, kind="ExternalInput").ap()
        d["sinT"] = nc.dram_tensor("sinT", [128, T], F32, kind="ExternalInput").ap()
        d["swapM"] = nc.dram_tensor("swapM", [128, 128], F32, kind="ExternalInput").ap()
    return d


def build_front(N, kinds, rope):
    nc = bass.Bass("TRN2", target_bir_lowering=False)
    dram = front_dram(nc, N, rope)
    xT = nc.dram_tensor("xT", [D, TPC], F32, kind="ExternalInput").ap()
    chunks, nbf, nf = front_plan(N, kinds)
    outs = {"bf": nc.dram_tensor("obf", [nbf, TPC], BF16, kind="ExternalOutput").ap(),
            "f": nc.dram_tensor("of", [nf, TPC], F32, kind="ExternalOutput").ap()}
    modo = nc.dram_tensor("modo", [128, 48], F32, kind="ExternalOutput").ap()
    with ExitStack() as st:
        K = KB(nc, st)
        xst = Rot([K.sb([128, TPC], F32, f"xst{i}") for i in range(2)])

        def xsrc(k):
            b = xst.next()
            K.dma("sp", b, b[:, 0:TPC // 2], None, xT[128 * k:128 * k + 128, 0:TPC // 2])
            K.dma("pool", b, b[:, TPC // 2:], None, xT[128 * k:128 * k + 128, TPC // 2:])
            return b
        mod = emit_front_core(K, nc, dram, N, kinds, rope, xsrc, outs)
        K.dma("pool", None, modo[:, :], mod, mod[:])
        K.finish()
    return nc


def pj(v, ncol):
    return np.ascontiguousarray(np.asarray(v, np.float32).reshape(ncol, 128).T)


def rope_tables():
    t = np.arange(SEQ)
    row = (t // 64).astype(np.float32)
    col = (t % 64).astype(np.float32)
    inv = (np.float32(10000.0) ** (-np.arange(32, dtype=np.float32) / np.float32(32))).astype(np.float32)
    ang = np.concatenate([row[:, None] * inv, col[:, None] * inv], axis=-1).astype(np.float32)
    cos, sin = np.cos(ang).astype(np.float32), np.sin(ang).astype(np.float32)
    cosT = np.repeat(cos, 2, axis=1).T
    sgn = np.tile(np.array([-1.0, 1.0], np.float32), 64)
    sinT = (np.repeat(sin, 2, axis=1) * sgn).T
    return np.ascontiguousarray(cosT), np.ascontiguousarray(sinT)


def swap_matrix():
    m = np.zeros((128, 128), np.float32)
    for i in range(64):
        m[2 * i + 1, 2 * i] = 1.0
        m[2 * i, 2 * i + 1] = 1.0
    return m


def front_common_inputs(c, ada_w, ada_b, norm_g, w_in, qn, kn):
    return {"cT": pj(c.reshape(-1), 16), "ada_bT": pj(ada_b.reshape(-1), 48), "gT": pj(norm_g.reshape(-1), 16),
            "ada_w": np.ascontiguousarray(ada_w), "w_in": np.ascontiguousarray(w_in),
            "qkg": np.ascontiguousarray(np.stack([qn.reshape(-1), kn.reshape(-1)], axis=1).astype(np.float32))}


def run_front(layer, xT, c, norm_g, ada_w, ada_b, w_in, q_norm, k_norm):
    rope = (layer == 0)
    nc = build_front(N_E if layer == 0 else N_O, KINDS_E if layer == 0 else KINDS_O, rope)
    base = front_common_inputs(c, ada_w[0], ada_b[0], norm_g[0], w_in[0], q_norm[0], k_norm[0])
    if rope:
        cosT, sinT = rope_tables()
        base["swapM"] = swap_matrix()
    maps = []
    for i in range(NCORES):
        sl = slice(i * TPC, (i + 1) * TPC)
        m = dict(base)
        m["xT"] = np.ascontiguousarray(xT[:, sl])
        if rope:
            m["cosT"] = np.ascontiguousarray(cosT[:, sl])
            m["sinT"] = np.ascontiguousarray(sinT[:, sl])
        maps.append(m)
    res = _run(nc, maps)
    obf = np.concatenate([r["obf"] for r in res], axis=1)
    of = np.concatenate([r["of"] for r in res], axis=1)
    return obf, of, res[0]["modo"]


def build_attn():
    nc = bass.Bass("TRN2", target_bir_lowering=False)
    qT = nc.dram_tensor("qT", [128, SEQ], BF16, kind="ExternalInput").ap()
    kT = nc.dram_tensor("kT", [128, SEQ], BF16, kind="ExternalInput").ap()
    vP = nc.dram_tensor("vP", [128, SEQ], BF16, kind="ExternalInput").ap()
    oT = nc.dram_tensor("oT", [128, SEQ], F32, kind="ExternalOutput").ap()
    scale = 128.0 ** -0.5
    with ExitStack() as st:
        K = KB(nc, st)
        q = K.sb([128, SEQ], BF16, "q")
        k = K.sb([128, SEQ], BF16, "k")
        v = K.sb([128, SEQ], BF16, "v")
        for h in range(4):
            sl = slice(h * 4096, (h + 1) * 4096)
            K.dma("sp", k, k[:, sl], None, kT[:, sl])
            K.dma("pool", q, q[:, sl], None, qT[:, sl])
            K.dma("sp", v, v[:, sl], None, vP[:, sl])
        ones = K.sb([128, 128], BF16, "ones")
        K.memset("pool", ones, ones[:], 1.0)
        pS = [K.ps([128, 512], F32, f"pS{i}") for i in range(3)]
        pO = [K.ps([128, 512], F32, f"pO{i}") for i in range(2)]
        pZ = [K.ps([128, 512], F32, f"pZ{i}") for i in range(2)]
        pT = Rot([K.sb([128, 512], BF16, f"pT{i}") for i in range(3)])
        rec = Rot([K.sb([128, 512], F32, f"rec{i}") for i in range(2)])
        osb = Rot([K.sb([128, 512], F32, f"osb{i}") for i in range(2)])
        NQ, NKB = SEQ // 512, SEQ // 128
        seq = [(a, b) for a in range(NQ) for b in range(NKB)]
        LOOK = 2

        def issue_S(idx):
            qt, kb = seq[idx]
            ps = pS[idx % 3]
            K.mm(ps, ps[:], k, k[:, kb * 128:(kb + 1) * 128], q, q[:, qt * 512:(qt + 1) * 512])

        for i in range(LOOK):
            issue_S(i)
        for idx, (qt, kb) in enumerate(seq):
            if idx + LOOK < len(seq):
                issue_S(idx + LOOK)
            ps = pS[idx % 3]
            p = pT.next()
            K.act(p, p[:], ps, ps[:], AF.Exp, scale=scale)
            o, z = pO[qt % 2], pZ[qt % 2]
            K.mm(o, o[:], v, v[:, kb * 128:(kb + 1) * 128], p, p[:], start=(kb == 0), stop=(kb == NKB - 1))
            K.mm(z, z[:], ones, ones[:], p, p[:], start=(kb == 0), stop=(kb == NKB - 1))
            if kb == NKB - 1:
                r = rec.next()
                K.op("dve", lambda e: e.reciprocal(out=r[:], in_=z[:]), [z], [r])
                ob = osb.next()
                K.tt("dve", ob, ob[:], o, o[:], r, r[:], ALU.mult)
                K.dma("pool", None, oT[:, qt * 512:(qt + 1) * 512], ob, ob[:])
        K.finish()
    return nc


def run_attn(obf):
    nc = build_attn()
    maps = []
    for i in range(NCORES):
        g = i // 4
        vT = obf[1280 + 128 * g:1280 + 128 * g + 128, :]
        vP = np.ascontiguousarray(vT.reshape(128, SEQ // 128, 128).transpose(2, 1, 0).reshape(128, SEQ))
        maps.append({"qT": np.ascontiguousarray(obf[128 * i:128 * i + 128, :]),
                     "kT": np.ascontiguousarray(obf[1024 + 128 * g:1024 + 128 * g + 128, :]),
                     "vP": vP})
    res = _run(nc, maps)
    return np.concatenate([r["oT"] for r in res], axis=0)


S5T = 32
S5NC = SEQ // S5T
TWO_PI = 2.0 * math.pi
S5_OFF = TWO_PI * 128


def s5_consts():
    s = np.arange(32, dtype=np.float32)
    et = np.concatenate([-s, 31 - s, s, s + 1, np.array([32.0], np.float32)]).astype(np.float32)
    ET = np.ascontiguousarray(np.broadcast_to(et, (64, 129)))
    mask = np.zeros((128, 4, 512), np.float32)
    for kb in range(4):
        for sl in range(8):
            sg = 8 * kb + sl
            for t in range(32):
                if sg <= t:
                    mask[sl * 16:(sl + 1) * 16, kb, t * 16:(t + 1) * 16] = 1.0
    ident = np.eye(64, dtype=np.float32)
    return ET, mask, ident


def build_s5(NPR=16):
    nc = bass.Bass("TRN2", target_bir_lowering=False)
    U = nc.dram_tensor("U", [NPR, 4, 128, S5NC], BF16, kind="ExternalInput").ap()
    PRM = nc.dram_tensor("PRM", [NPR, 64, 67], F32, kind="ExternalInput").ap()
    ETd = nc.dram_tensor("ET", [64, 129], F32, kind="ExternalInput").ap()
    MKd = nc.dram_tensor("MK", [128, 4, 512], F32, kind="ExternalInput").ap()
    IDd = nc.dram_tensor("ID", [64, 64], F32, kind="ExternalInput").ap()
    Y = nc.dram_tensor("Y", [NPR, 4, 128, S5NC], F32, kind="ExternalOutput").ap()
    with ExitStack() as st:
        K = KB(nc, st)
        ET = K.sb([64, 129], F32, "ET"); K.dma("sp", ET, ET[:], None, ETd[:, :])
        MK = K.sb([128, 4, 512], F32, "MK"); K.dma("sp", MK, MK[:], None, MKd[:, :, :])
        ID = K.sb([64, 64], F32, "ID"); K.dma("sp", ID, ID[:], None, IDd[:, :])
        uR = Rot([K.sb([128, 4, S5NC], BF16, f"u{i}") for i in range(2)])
        prR = Rot([K.sb([64, 67], F32, f"prm{i}") for i in range(2)])
        sc = K.sb([64, 16], F32, "sc")
        mag = K.sb([64, 129], F32, "mag")
        ang = K.sb([64, 129], F32, "ang")
        ang2 = K.sb([64, 129], F32, "ang2")
        angi = K.sb([64, 129], mybir.dt.int32, "angi")
        PWr = K.sb([64, 129], F32, "PWr")
        PWi = K.sb([64, 129], F32, "PWi")
        bb = K.sb([64, 32], F32, "bb")
        t1 = K.sb([64, 512], F32, "t1")
        t2 = K.sb([64, 512], F32, "t2")
        BLr = K.sb([64, 512], F32, "BLr"); BLi = K.sb([64, 512], F32, "BLi")
        WSr = K.sb([64, 512], F32, "WSr"); WSi = K.sb([64, 512], F32, "WSi")
        CLr = K.sb([64, 512], F32, "CLr"); CLm = K.sb([64, 512], F32, "CLm")
        WOr = K.sb([64, 512], BF16, "WOr"); WOm = K.sb([64, 512], BF16, "WOm")
        Msb = K.sb([128, 4, 512], BF16, "Msb")
        WSs = K.sb([128, 4, 128], BF16, "WSs")
        Pre = [K.sb([64, S5NC], F32, f"Pre{i}") for i in range(2)]
        Pim = [K.sb([64, S5NC], F32, f"Pim{i}") for i in range(2)]
        Hre = K.sb([64, S5NC], BF16, "Hre"); Him = K.sb([64, S5NC], BF16, "Him")
        K.memset("pool", Hre, Hre[:], 0.0); K.memset("pool", Him, Him[:], 0.0)
        A = K.sb([64, 8], F32, "A")
        ysb = Rot([K.sb([128, S5NC], F32, f"ysb{i}") for i in range(2)])
        pM = K.ps([128, 512], F32, "pM")
        pW = K.ps([128, 128], F32, "pW")
        pEr = K.ps([64, 512], F32, "pEr"); pEi = K.ps([64, 512], F32, "pEi")
        pY = [K.ps([128, 512], F32, f"pY{i}") for i in range(2)]

        def v3(ap):
            return ap.rearrange("p (s c) -> p s c", c=16)

        def bs(buf, a):
            return buf[:, a:a + 32].unsqueeze(2).to_broadcast([64, 32, 16])

        def bc(buf, a):
            return buf[:, a:a + 16].unsqueeze(1).to_broadcast([64, 32, 16])

        def table(outr, outi, a, xb, xr, xi, neg_im):
            K.tt("dve", t1, v3(t1[:, :]), PWr, bs(PWr, a), xb, bc(xb, xr), ALU.mult)
            K.tt("dve", t2, v3(t2[:, :]), PWi, bs(PWi, a), xb, bc(xb, xi), ALU.mult)
            K.tt("dve", outr, v3(outr[:, :]), t1, v3(t1[:, :]), t2, v3(t2[:, :]), ALU.subtract)
            K.tt("dve", t1, v3(t1[:, :]), PWr, bs(PWr, a), xb, bc(xb, xi), ALU.mult)
            K.tt("dve", t2, v3(t2[:, :]), PWi, bs(PWi, a), xb, bc(xb, xr), ALU.mult)
            if neg_im:
                K.stt(outi, v3(outi[:, :]), t1, v3(t1[:, :]), -1.0, t2, v3(t2[:, :]), ALU.mult, ALU.subtract)
            else:
                K.tt("dve", outi, v3(outi[:, :]), t1, v3(t1[:, :]), t2, v3(t2[:, :]), ALU.add)

        def load(pr):
            u = uR.next(); p = prR.next()
            K.dma("sp", u, u[:], None, U[pr].rearrange("k p j -> p k j"))
            K.dma("sp", p, p[:], None, PRM[pr])
            return u, p

        pend = load(0)
        for pr in range(NPR):
            u, p = pend
            if pr + 1 < NPR:
                pend = load(pr + 1)
            K.act(sc, sc[:, 0:1], p, p[:, 2:3], AF.Exp)
            K.tt("dve", sc, sc[:, 1:2], p, p[:, 0:1], sc, sc[:, 0:1], ALU.mult)
            K.tt("dve", sc, sc[:, 2:3], p, p[:, 1:2], sc, sc[:, 0:1], ALU.mult)
            K.act(mag, mag[:], ET, ET[:], AF.Exp, extra=[sc], scale=sc[:, 1:2])
            K.ts("dve", ang, ang[:], ET, ET[:], sc[:, 2:3], S5_OFF, ALU.mult, ALU.add, extra=[sc])
            K.ts("dve", ang2, ang2[:], ang, ang[:], 1.0 / TWO_PI, None, ALU.mult)
            K.copy("dve", angi, angi[:], ang2, ang2[:])
            K.copy("dve", ang2, ang2[:], angi, angi[:])
            K.stt(ang, ang[:], ang2, ang2[:], -TWO_PI, ang, ang[:], ALU.mult, ALU.add)
            K.act(PWi, PWi[:], ang, ang[:], AF.Sin, scale=0.5)
            K.act(ang2, ang2[:], ang, ang[:], AF.Sin, scale=0.25)
            K.tt("dve", ang2, ang2[:], ang2, ang2[:], ang2, ang2[:], ALU.mult)
            K.ts("dve", ang2, ang2[:], ang2, ang2[:], -2.0, 1.0, ALU.mult, ALU.add)
            K.tt("dve", PWr, PWr[:], PWi, PWi[:], PWi, PWi[:], ALU.mult)
            K.ts("dve", PWr, PWr[:], PWr, PWr[:], -2.0, 1.0, ALU.mult, ALU.add)
            K.stt(PWi, PWi[:], PWi, PWi[:], 2.0, ang2, ang2[:], ALU.mult, ALU.mult)
            K.tt("dve", PWr, PWr[:], PWr, PWr[:], mag, mag[:], ALU.mult)
            K.tt("dve", PWi, PWi[:], PWi, PWi[:], mag, mag[:], ALU.mult)
            K.ts("dve", sc, sc[:, 3:4], PWr, PWr[:, 65:66], 1.0, None, ALU.subtract)
            K.tt("dve", sc, sc[:, 4:5], p, p[:, 0:1], p, p[:, 0:1], ALU.mult)
            K.stt(sc, sc[:, 4:5], p, p[:, 1:2], p[:, 1:2], sc, sc[:, 4:5], ALU.mult, ALU.add)
            K.op("dve", lambda e: e.reciprocal(out=sc[:, 4:5], in_=sc[:, 4:5]), [sc], [sc])
            K.tt("dve", sc, sc[:, 5:6], sc, sc[:, 3:4], p, p[:, 0:1], ALU.mult)
            K.stt(sc, sc[:, 5:6], PWi, PWi[:, 65:66], p[:, 1:2], sc, sc[:, 5:6], ALU.mult, ALU.add, extra=[p])
            K.tt("dve", sc, sc[:, 5:6], sc, sc[:, 5:6], sc, sc[:, 4:5], ALU.mult)
            K.tt("dve", sc, sc[:, 6:7], sc, sc[:, 3:4], p, p[:, 1:2], ALU.mult)
            K.stt(sc, sc[:, 6:7], PWi, PWi[:, 65:66], p[:, 0:1], sc, sc[:, 6:7], ALU.mult, ALU.subtract, extra=[p])
            K.tt("dve", sc, sc[:, 6:7], sc, sc[:, 6:7], sc, sc[:, 4:5], ALU.mult)
            K.ts("dve", t1, t1[:, 0:16], p, p[:, 19:35], sc[:, 6:7], None, ALU.mult, extra=[sc])
            K.stt(bb, bb[:, 0:16], p, p[:, 3:19], sc[:, 5:6], t1, t1[:, 0:16], ALU.mult, ALU.subtract, extra=[sc])
            K.ts("dve", t1, t1[:, 0:16], p, p[:, 3:19], sc[:, 6:7], None, ALU.mult, extra=[sc])
            K.stt(bb, bb[:, 16:32], p, p[:, 19:35], sc[:, 5:6], t1, t1[:, 0:16], ALU.mult, ALU.add, extra=[sc])
            table(BLr, BLi, 0, bb, 0, 16, False)
            table(WSr, WSi, 32, bb, 0, 16, False)
            table(CLr, CLm, 64, p, 35, 51, True)
            table(WOr, WOm, 96, p, 35, 51, True)
            for kb in range(4):
                ks = slice(kb * 128, (kb + 1) * 128)
                K.mm(pM, pM[:], BLr, BLr[:, ks], CLr, CLr[:, :], start=True, stop=False)
                K.mm(pM, pM[:], BLi, BLi[:, ks], CLm, CLm[:, :], start=False, stop=True)
                K.tt("dve", Msb, Msb[:, kb, :], pM, pM[:], MK, MK[:, kb, :], ALU.mult)
                K.mm(pW, pW[:, 0:64], WSr, WSr[:, ks], ID, ID[:], start=True, stop=True)
                K.mm(pW, pW[:, 64:128], WSi, WSi[:, ks], ID, ID[:], start=True, stop=True)
                K.copy("act", WSs, WSs[:, kb, :], pW, pW[:])
            for kb in range(4):
                K.mm(pEr, pEr[:], WSs, WSs[:, kb, 0:64], u, u[:, kb, :], start=(kb == 0), stop=(kb == 3))
            for kb in range(4):
                K.mm(pEi, pEi[:], WSs, WSs[:, kb, 64:128], u, u[:, kb, :], start=(kb == 0), stop=(kb == 3))
            K.copy("act", Pre[0], Pre[0][:], pEr, pEr[:])
            K.copy("dve", Pim[0], Pim[0][:], pEi, pEi[:])
            K.copy("dve", A, A[:, 0:1], PWr, PWr[:, 128:129])
            K.copy("dve", A, A[:, 1:2], PWi, PWi[:, 128:129])
            K.ts("dve", A, A[:, 2:3], PWi, PWi[:, 128:129], -1.0, None, ALU.mult)
            cur = 0
            d = 1
            while d < S5NC:
                re, im, nre, nim = Pre[cur], Pim[cur], Pre[1 - cur], Pim[1 - cur]
                n = S5NC
                K.stt(nre, nre[:, d:n], re, re[:, 0:n - d], A[:, 0:1], re, re[:, d:n], ALU.mult, ALU.add, extra=[A])
                K.stt(nre, nre[:, d:n], im, im[:, 0:n - d], A[:, 2:3], nre, nre[:, d:n], ALU.mult, ALU.add, extra=[A])
                K.stt(nim, nim[:, d:n], im, im[:, 0:n - d], A[:, 0:1], im, im[:, d:n], ALU.mult, ALU.add, extra=[A])
                K.stt(nim, nim[:, d:n], re, re[:, 0:n - d], A[:, 1:2], nim, nim[:, d:n], ALU.mult, ALU.add, extra=[A])
                K.copy("act", nre, nre[:, 0:d], re, re[:, 0:d])
                K.copy("act", nim, nim[:, 0:d], im, im[:, 0:d])
                cur = 1 - cur
                d *= 2
                if d < S5NC:
                    K.tt("dve", A, A[:, 3:4], A, A[:, 0:1], A, A[:, 0:1], ALU.mult)
                    K.stt(A, A[:, 3:4], A, A[:, 1:2], A[:, 2:3], A, A[:, 3:4], ALU.mult, ALU.add)
                    K.stt(A, A[:, 4:5], A, A[:, 0:1], 2.0, A, A[:, 1:2], ALU.mult, ALU.mult)
                    K.copy("dve", A, A[:, 0:1], A, A[:, 3:4])
                    K.copy("dve", A, A[:, 1:2], A, A[:, 4:5])
                    K.ts("dve", A, A[:, 2:3], A, A[:, 4:5], -1.0, None, ALU.mult)
            K.copy("act", Hre, Hre[:, 1:S5NC], Pre[cur], Pre[cur][:, 0:S5NC - 1])
            K.copy("dve", Him, Him[:, 1:S5NC], Pim[cur], Pim[cur][:, 0:S5NC - 1])
            for tb in range(4):
                tsl = slice(tb * 128, (tb + 1) * 128)
                py = pY[tb % 2]
                for kb in range(tb + 1):
                    K.mm(py, py[:], Msb, Msb[:, kb, tsl], u, u[:, kb, :], start=(kb == 0), stop=False)
                K.mm(py, py[:], WOr, WOr[:, tsl], Hre, Hre[:], start=False, stop=False)
                K.mm(py, py[:], WOm, WOm[:, tsl], Him, Him[:], start=False, stop=True)
                yb = ysb.next()
                K.copy("act" if tb % 2 == 0 else "dve", yb, yb[:], py, py[:])
                K.dma("pool", None, Y[pr, tb], yb, yb[:])
        K.finish()
    return nc


def s5_pack_u(uT):
    outs = []
    for i in range(NCORES):
        U = np.empty((16, 4, 128, S5NC), uT.dtype)
        for gl in range(8):
            g = 8 * i + gl
            ug = uT[16 * g:16 * g + 16, :]
            for d in range(2):
                x = ug[:, ::-1] if d == 1 else ug
                x = x.reshape(16, S5NC, 32).transpose(2, 0, 1)
                U[2 * gl + d] = x.reshape(4, 128, S5NC)
        outs.append(U)
    return outs


def s5_unpack_y(Ys):
    yf = np.empty((1024, SEQ), np.float32); yb = np.empty((1024, SEQ), np.float32)
    for i in range(NCORES):
        for gl in range(8):
            g = 8 * i + gl
            for d in range(2):
                y = Ys[i][2 * gl + d].reshape(32, 16, S5NC).transpose(1, 2, 0).reshape(16, SEQ)
                if d == 0:
                    yf[16 * g:16 * g + 16] = y
                else:
                    yb[16 * g:16 * g + 16] = y[:, ::-1]
    return yf, yb


def s5_params(lam_re, lam_im, log_step, b_re, b_im, c_re, c_im):
    outs = []
    for i in range(NCORES):
        P = np.empty((16, 64, 67), np.float32)
        for gl in range(8):
            g = 8 * i + gl
            for d in range(2):
                pr = 2 * gl + d
                P[pr, :, 0] = lam_re[d, g]; P[pr, :, 1] = lam_im[d, g]; P[pr, :, 2] = log_step[d, g]
                P[pr, :, 3:19] = b_re[d, g]; P[pr, :, 19:35] = b_im[d, g]
                P[pr, :, 35:51] = c_re[d, g].T; P[pr, :, 51:67] = c_im[d, g].T
        outs.append(P)
    return outs


def run_s5(uT, lam_re, lam_im, log_step, b_re, b_im, c_re, c_im):
    nc = build_s5()
    ET, MK, ID = s5_consts()
    Us = s5_pack_u(uT)
    Ps = s5_params(lam_re, lam_im, log_step, b_re, b_im, c_re, c_im)
    maps = [{"U": Us[i], "PRM": Ps[i], "ET": ET, "MK": MK, "ID": ID} for i in range(NCORES)]
    res = _run(nc, maps)
    return s5_unpack_y([r["Y"] for r in res])


def build_post(layer, T=TPC):
    nc = bass.Bass("TRN2", target_bir_lowering=False)
    def di(name, shape, dt=F32):
        return nc.dram_tensor(name, shape, dt, kind="ExternalInput").ap()
    xT = di("xT", [D, T]); w_out = di("w_out", [D, D]); modd = di("mod", [128, 48])
    aT = di("aT", [1024, T]); agT = di("agT", [1024, T])
    if layer == 0:
        yfT = di("yfT", [1024, T]); ybT = di("ybT", [1024, T]); uT = di("uT", [1024, T], BF16); gbT = di("gbT", [1024, T])
        s5d = di("s5d", [128, 8]); w_glu = di("w_glu", [1024, 2048]); bglu = di("bglu", [128, 16])
    else:
        ysT = di("ysT", [1024, T]); ys2T = di("ys2T", [1024, T]); zT = di("zT", [1024, T]); nw = di("nw", [128, 8])
    oT = nc.dram_tensor("oT", [D, T], F32, kind="ExternalOutput").ap()
    NT = T // 512
    with ExitStack() as st:
        K = KB(nc, st)
        mod = K.sb([128, 48], F32, "mod"); K.dma("sp", mod, mod[:], None, modd[:, :])
        wo = K.sb([128, 16, D], BF16, "wo")
        stg = Rot([K.sb([128, D], F32, f"stg{i}") for i in range(2)])
        cast_i = [0]

        def load_w(dst, w_ap, nk):
            for k in range(nk):
                s_ = stg.next()
                K.dma("sp" if k % 2 == 0 else "pool", s_, s_[:], None, w_ap[128 * k:128 * k + 128, :])
                e = ("dve", "act")[cast_i[0] % 2]; cast_i[0] += 1
                K.copy(e, dst, dst[:, k, :], s_, s_[:])
        if layer == 0:
            wg = K.sb([128, 8, 2048], BF16, "wg")
            load_w(wg, w_glu, 8)
            sd = K.sb([128, 8], F32, "sd"); K.dma("sp", sd, sd[:], None, s5d[:, :])
            bg = K.sb([128, 16], F32, "bg"); K.dma("sp", bg, bg[:], None, bglu[:, :])
        else:
            nws = K.sb([128, 8], F32, "nws"); K.dma("sp", nws, nws[:], None, nw[:, :])
            ones = K.sb([128, 128], F32, "ones"); K.memset("pool", ones, ones[:], 1.0)
        load_w(wo, w_out, 16)
        A = Rot([K.sb([128, 16, 512], BF16, f"A{i}") for i in range(2)])
        ld = {nm: Rot([K.sb([128, 512], F32, f"ld_{nm}{i}") for i in range(3)]) for nm in ("a", "b", "c", "x")}
        ldu = Rot([K.sb([128, 512], BF16, f"ldu{i}") for i in range(2)])
        tmp = {nm: Rot([K.sb([128, 512], F32, f"tm_{nm}{i}") for i in range(2)]) for nm in ("p", "q", "r")}
        osb = Rot([K.sb([128, 512], F32, f"osb{i}") for i in range(3)])
        pacc = Rot([K.ps([128, 512], F32, f"pa{i}") for i in range(4)])
        pss = K.ps([128, 512], F32, "pss")
        if layer == 0:
            gy = Rot([K.sb([128, 8, 512], BF16, f"gy{i}") for i in range(2)])
        else:
            yz = Rot([K.sb([128, 8, 512], F32, f"yz{i}") for i in range(2)])
            rs = K.sb([128, 512], F32, "rs")

        def load(nm, ap, q="sp"):
            b = ld[nm].next()
            K.dma(q, b, b[:], None, ap)
            return b

        for t in range(NT):
            ts_ = slice(512 * t, 512 * t + 512)
            Ab = A.next()
            if layer == 0:
                g = gy.next()
                for k in range(8):
                    rows = slice(128 * k, 128 * k + 128)
                    yf = load("a", yfT[rows, ts_]); yb = load("b", ybT[rows, ts_], "pool")
                    ub = ldu.next(); K.dma("sp", ub, ub[:], None, uT[rows, ts_])
                    y = tmp["p"].next()
                    K.stt(y, y[:], ub, ub[:], sd[:, k:k + 1], yf, yf[:], ALU.mult, ALU.add, extra=[sd])
                    K.tt("dve", y, y[:], y, y[:], yb, yb[:], ALU.add)
                    q_ = tmp["q"].next()
                    K.act(q_, q_[:], y, y[:], AF.Square)
                    K.ts("dve", q_, q_[:], q_, q_[:], 0.044715, 1.0, ALU.mult, ALU.add)
                    K.tt("dve", q_, q_[:], q_, q_[:], y, y[:], ALU.mult)
                    K.act(q_, q_[:], q_, q_[:], AF.Sigmoid, scale=1.5957691216057308)
                    K.tt("dve", g, g[:, k, :], q_, q_[:], y, y[:], ALU.mult)
                for n in range(8):
                    pv, pg = pacc.next(), pacc.next()
                    for k in range(8):
                        K.mm(pv, pv[:], wg, wg[:, k, 128 * n:128 * n + 128], g, g[:, k, :], start=(k == 0), stop=(k == 7))
                    for k in range(8):
                        K.mm(pg, pg[:], wg, wg[:, k, 1024 + 128 * n:1024 + 128 * n + 128], g, g[:, k, :], start=(k == 0), stop=(k == 7))
                    sg = tmp["r"].next()
                    K.act(sg, sg[:], pg, pg[:], AF.Sigmoid, extra=[bg], bias=bg[:, 8 + n:9 + n])
                    gb = load("c", gbT[128 * n:128 * n + 128, ts_])
                    v1 = tmp["p"].next()
                    K.stt(v1, v1[:], pv, pv[:], bg[:, n:n + 1], sg, sg[:], ALU.add, ALU.mult, extra=[bg])
                    K.tt("dve", Ab, Ab[:, 8 + n, :], v1, v1[:], gb, gb[:], ALU.mult)
            else:
                yzb = yz.next()
                for k in range(8):
                    rows = slice(128 * k, 128 * k + 128)
                    ys = load("a", ysT[rows, ts_]); z = load("b", zT[rows, ts_], "pool"); y2 = load("c", ys2T[rows, ts_])
                    K.tt("dve", ys, ys[:], ys, ys[:], y2, y2[:], ALU.add)
                    K.tt("dve", yzb, yzb[:, k, :], ys, ys[:], z, z[:], ALU.mult)
                    sq = tmp["q"].next()
                    K.act(sq, sq[:], yzb, yzb[:, k, :], AF.Square)
                    K.mm(pss, pss[:], ones, ones[:], sq, sq[:], start=(k == 0), stop=(k == 7))
                emit_rstd(K, rs, rs[:], pss, pss[:], 1.0 / 1024)
                for k in range(8):
                    K.stt(Ab, Ab[:, 8 + k, :], yzb, yzb[:, k, :], nws[:, k:k + 1], rs, rs[:], ALU.mult, ALU.mult, extra=[nws])
            for n in range(8):
                a = load("a", aT[128 * n:128 * n + 128, ts_]); ag = load("b", agT[128 * n:128 * n + 128, ts_], "pool")
                K.tt("dve", Ab, Ab[:, n, :], a, a[:], ag, ag[:], ALU.mult)
            for n in range(16):
                po = pacc.next()
                for k in range(16):
                    K.mm(po, po[:], wo, wo[:, k, 128 * n:128 * n + 128], Ab, Ab[:, k, :], start=(k == 0), stop=(k == 15))
                xb = load("x", xT[128 * n:128 * n + 128, ts_])
                ob = osb.next()
                K.stt(ob, ob[:], po, po[:], mod[:, 32 + n:33 + n], xb, xb[:], ALU.mult, ALU.add, extra=[mod])
                K.dma("pool", None, oT[128 * n:128 * n + 128, ts_], ob, ob[:])
        K.finish()
    return nc


def run_post(layer, xT, mod, w_out, aT, agT, **kw):
    nc = build_post(layer)
    maps = []
    for i in range(NCORES):
        sl = slice(i * TPC, (i + 1) * TPC)
        m = {"xT": np.ascontiguousarray(xT[:, sl]), "w_out": np.ascontiguousarray(w_out), "mod": np.ascontiguousarray(mod),
             "aT": np.ascontiguousarray(aT[:, sl]), "agT": np.ascontiguousarray(agT[:, sl])}
        if layer == 0:
            for nm in ("yfT", "ybT", "uT", "gbT"):
                m[nm] = np.ascontiguousarray(kw[nm][:, sl])
            m["s5d"] = pj(kw["s5_d"], 8); m["w_glu"] = np.ascontiguousarray(kw["w_glu"]); m["bglu"] = pj(kw["b_glu"], 16)
        else:
            for nm in ("ysT", "ys2T", "zT"):
                m[nm] = np.ascontiguousarray(kw[nm][:, sl])
            m["nw"] = pj(kw["norm_w"], 8)
        maps.append(m)
    res = _run(nc, maps)
    return np.concatenate([r["oT"] for r in res], axis=1)


GW = 64
NROW = SEQ // GW


def na_bias_tables(rpb_h):
    col = np.arange(GW)
    cs = np.clip(col - 8, 0, GW - 16)
    cmask = (col[None, :] >= cs[:, None]) & (col[None, :] < cs[:, None] + 16)
    dc = np.clip(col[None, :] - col[:, None], -15, 15) + 15
    out = np.empty((8, GW, 8, GW), np.float32)
    for v in range(8):
        off = 7 - v
        for kr in range(8):
            b = rpb_h[off + kr][dc]
            b = np.where(cmask, b, np.float32(-1e30)).astype(np.float32)
            out[v, :, kr, :] = b.T
    return out


def build_na():
    nc = bass.Bass("TRN2", target_bir_lowering=False)
    qT = nc.dram_tensor("qT", [128, SEQ], BF16, kind="ExternalInput").ap()
    kT = nc.dram_tensor("kT", [128, SEQ], BF16, kind="ExternalInput").ap()
    vR = nc.dram_tensor("vR", [64, NROW * 128], BF16, kind="ExternalInput").ap()
    bT = nc.dram_tensor("bT", [64, 8, 512], F32, kind="ExternalInput").ap()
    oT = nc.dram_tensor("oT", [128, SEQ], F32, kind="ExternalOutput").ap()
    scale = 128.0 ** -0.5
    with ExitStack() as st:
        K = KB(nc, st)
        q = K.sb([128, SEQ], BF16, "q"); k = K.sb([128, SEQ], BF16, "k"); v = K.sb([64, NROW * 128], BF16, "v")
        bias = K.sb([64, 8, 512], F32, "bias")
        for h in range(4):
            sl = slice(h * 4096, (h + 1) * 4096)
            K.dma("sp", k, k[:, sl], None, kT[:, sl])
            K.dma("pool", q, q[:, sl], None, qT[:, sl])
        for h in range(4):
            sl = slice(h * 8192, (h + 1) * 8192)
            K.dma("sp", v, v[:, sl], None, vR[:, sl])
        K.dma("sp", bias, bias[:], None, bT[:, :, :])
        ones = K.sb([64, 128], BF16, "ones"); K.memset("pool", ones, ones[:], 1.0)
        pS = Rot([K.ps([64, 512], F32, f"pS{i}") for i in range(3)])
        pO = Rot([K.ps([128, 512], F32, f"pO{i}") for i in range(2)])
        pZ = Rot([K.ps([128, 512], F32, f"pZ{i}") for i in range(2)])
        tS = Rot([K.sb([64, 512], F32, f"tS{i}") for i in range(3)])
        pT = Rot([K.sb([64, 512], BF16, f"pT{i}") for i in range(3)])
        rec = Rot([K.sb([128, 512], F32, f"rec{i}") for i in range(2)])
        osb = Rot([K.sb([128, 512], F32, f"osb{i}") for i in range(2)])
        for r0 in range(0, NROW, 8):
            o, z = pO.next(), pZ.next()
            for rr in range(8):
                r = r0 + rr
                rs = min(max(r - 4, 0), NROW - 8)
                var = r if r < 4 else (4 if r <= 252 else 4 + (r - 252))
                ps = pS.next()
                qs = slice(r * 64, r * 64 + 64)
                for j in range(8):
                    K.mm(ps, ps[:, j * 64:(j + 1) * 64], k, k[:, (rs + j) * 64:(rs + j) * 64 + 64], q, q[:, qs])
                t = tS.next()
                K.stt(t, t[:], ps, ps[:], scale, bias, bias[:, var, :], ALU.mult, ALU.add)
                p = pT.next()
                K.act(p, p[:], t, t[:], AF.Exp)
                cs_ = slice(rr * 64, rr * 64 + 64)
                for j in range(8):
                    K.mm(o, o[:, cs_], v, v[:, (rs + j) * 128:(rs + j) * 128 + 128], p, p[:, j * 64:(j + 1) * 64], start=(j == 0), stop=(j == 7))
                for j in range(8):
                    K.mm(z, z[:, cs_], ones, ones[:], p, p[:, j * 64:(j + 1) * 64], start=(j == 0), stop=(j == 7))
            rc = rec.next()
            K.op("dve", lambda e: e.reciprocal(out=rc[:], in_=z[:]), [z], [rc])
            ob = osb.next()
            K.tt("dve", ob, ob[:], o, o[:], rc, rc[:], ALU.mult)
            K.dma("pool", None, oT[:, r0 * 64:r0 * 64 + 512], ob, ob[:])
        K.finish()
    return nc


def run_na(obf, rpb):
    nc = build_na()
    maps = []
    for i in range(NCORES):
        vT = obf[2048 + 128 * i:2048 + 128 * i + 128, :]
        vR = np.ascontiguousarray(vT.reshape(128, NROW, 64).transpose(2, 1, 0).reshape(64, NROW * 128))
        bt = na_bias_tables(np.asarray(rpb[i], np.float32))
        bT = np.ascontiguousarray(bt.transpose(1, 0, 2, 3).reshape(64, 8, 512))
        maps.append({"qT": np.ascontiguousarray(obf[128 * i:128 * i + 128, :]),
                     "kT": np.ascontiguousarray(obf[1024 + 128 * i:1024 + 128 * i + 128, :]), "vR": vR, "bT": bT})
    res = _run(nc, maps)
    return np.concatenate([r["oT"] for r in res], axis=0)


NCH = SEQ // 128


def build_ssd(nchunks=NCH, ndirs=2):
    nc = bass.Bass("TRN2", target_bir_lowering=False)
    XBC = nc.dram_tensor("XBC", [2, 3, 128, SEQ + 4], F32, kind="ExternalInput").ap()
    CW = nc.dram_tensor("CW", [2, 3, 128, 6], F32, kind="ExternalInput").ap()
    DTR = nc.dram_tensor("DTR", [2, 2, 128, NCH], F32, kind="ExternalInput").ap()
    SCd = nc.dram_tensor("SC", [2, 128, 6], F32, kind="ExternalInput").ap()
    TRI = nc.dram_tensor("TRI", [128, 128], F32, kind="ExternalInput").ap()
    SUd = nc.dram_tensor("SU", [128, 128], F32, kind="ExternalInput").ap()
    IDd = nc.dram_tensor("ID", [128, 128], BF16, kind="ExternalInput").ap()
    Y = nc.dram_tensor("Y", [2, 128, NCH, 128], F32, kind="ExternalOutput").ap()
    PIECE = 2048
    with ExitStack() as st:
        K = KB(nc, st)
        tri = K.sb([128, 128], F32, "tri"); K.dma("sp", tri, tri[:], None, TRI[:, :])
        su = K.sb([128, 128], F32, "su"); K.dma("sp", su, su[:], None, SUd[:, :])
        idb = K.sb([128, 128], BF16, "idb"); K.dma("sp", idb, idb[:], None, IDd[:, :])
        ones = K.sb([128, 128], F32, "ones"); K.memset("pool", ones, ones[:], 1.0)
        stage = Rot([K.sb([128, PIECE + 4], F32, f"stage{i}") for i in range(2)])
        acc = K.sb([128, PIECE], F32, "acc")
        xfm = K.sb([128, PIECE], BF16, "xfm")
        BT = K.sb([128, SEQ], BF16, "BT"); CT = K.sb([128, SEQ], BF16, "CT")
        Xt = K.sb([128, NCH, 128], BF16, "Xt"); Bt = K.sb([128, NCH, 128], BF16, "Bt")
        cw = K.sb([128, 3, 6], F32, "cw")
        scs = K.sb([128, 6], F32, "scs")
        dtr = K.sb([128, 2, NCH], F32, "dtr")
        dt = K.sb([128, 2, NCH], F32, "dt"); adt = K.sb([128, 2, NCH], F32, "adt")
        cum = K.sb([128, 2, NCH], F32, "cum"); ecum = K.sb([128, 2, NCH], F32, "ecum")
        dts = K.sb([128, 2, NCH], F32, "dts"); etot = K.sb([128, 2, NCH], F32, "etot")
        na = K.sb([128, 2], F32, "na")
        stT = [K.sb([128, 64], F32, f"stT{h}") for h in range(2)]
        stB = [[K.sb([128, 64], BF16, f"stB{h}_{i}") for i in range(2)] for h in range(2)]
        pSC = K.ps([128, 512], F32, "pSC")
        pSEG = [K.ps([128, 512], F32, f"pSEG{h}") for h in range(2)]
        pYD = [K.ps([128, 512], F32, f"pYD{i}") for i in range(2)]
        pYO = K.ps([128, 512], F32, "pYO")
        pST = [K.ps([128, 512], F32, f"pST{i}") for i in range(2)]

        class _PA:
            def __init__(self):
                self.i = 0

            def next(self):
                self.i += 1
                return pSC if self.i % 2 else pYO
        pA = _PA()
        scm = Rot([K.sb([128, 512], F32, f"scm{i}") for i in range(2)])
        lseg = Rot([K.sb([128, 512], F32, f"lseg{i}") for i in range(4)])
        Lm = Rot([K.sb([128, 512], F32, f"Lm{i}") for i in range(4)])
        Gt = Rot([K.sb([128, 512], BF16, f"Gt{i}") for i in range(4)])
        xd = Rot([K.sb([128, 4, 64], BF16, f"xd{i}") for i in range(4)])
        xdd = Rot([K.sb([128, 4, 64], BF16, f"xdd{i}") for i in range(4)])
        ysb = Rot([K.sb([128, 512], F32, f"ysb{i}") for i in range(3)])
        ws = K.sb([128, 2], F32, "ws")
        for d in range(ndirs):
            K.dma("sp", cw, cw[:], None, CW[d].rearrange("g p t -> p g t"))
            K.dma("sp", scs, scs[:], None, SCd[d])
            K.dma("sp", dtr, dtr[:], None, DTR[d].rearrange("h p c -> p h c"))
            import os
            for g in range(0 if os.environ.get('SKIP_CONV') else 3):
                for pc in range(SEQ // PIECE):
                    sg = stage.next()
                    K.dma("sp" if pc % 2 == 0 else "pool", sg, sg[:], None, XBC[d, g, :, pc * PIECE:pc * PIECE + PIECE + 4])
                    K.ts("dve", acc, acc[:], sg, sg[:, 0:PIECE], cw[:, g, 0:1], None, ALU.mult, extra=[cw])
                    for j in range(1, 5):
                        K.stt(acc, acc[:], sg, sg[:, j:j + PIECE], cw[:, g, j:j + 1], acc, acc[:], ALU.mult, ALU.add, extra=[cw])
                    dst, dbuf = ((xfm[:, :], xfm), (BT[:, pc * PIECE:(pc + 1) * PIECE], BT), (CT[:, pc * PIECE:(pc + 1) * PIECE], CT))[g]
                    K.act(dbuf, dst, acc, acc[:], AF.Silu, extra=[cw], bias=cw[:, g, 5:6])
                    if g < 2:
                        for cc in range(PIECE // 128):
                            c = pc * (PIECE // 128) + cc
                            pa = pA.next()
                            src = xfm[:, cc * 128:(cc + 1) * 128] if g == 0 else BT[:, c * 128:(c + 1) * 128]
                            K.mm(pa, pa[:, 0:128], dbuf, src, idb, idb[:])
                            tgt = Xt if g == 0 else Bt
                            K.copy("act" if cc % 2 == 0 else "pool" if False else "dve", tgt, tgt[:, c, :], pa, pa[:, 0:128])
            import os
            for h in range(0 if os.environ.get('SKIP_DT') else 2):
                K.ts("dve", cum, cum[:, h, :], dtr, dtr[:, h, :], scs[:, h:h + 1], None, ALU.add, extra=[scs])
                K.act(ecum, ecum[:, h, :], cum, cum[:, h, :], AF.Exp)
                K.ts("dve", ecum, ecum[:, h, :], ecum, ecum[:, h, :], 1.0, None, ALU.add)
                K.ts("dve", dt, dt[:, h, :], cum, cum[:, h, :], 0.0, 0.3, ALU.max, ALU.add)
                for _it in range(6):
                    K.act(dts, dts[:, h, :], dt, dt[:, h, :], AF.Exp, scale=-1.0)
                    K.tt("dve", dts, dts[:, h, :], dts, dts[:, h, :], ecum, ecum[:, h, :], ALU.mult)
                    K.stt(dt, dt[:, h, :], dts, dts[:, h, :], -1.0, dt, dt[:, h, :], ALU.add, ALU.add)
                K.act(na, na[:, h:h + 1], scs, scs[:, 2 + h:3 + h], AF.Exp)
                K.ts("dve", na, na[:, h:h + 1], na, na[:, h:h + 1], -1.0, None, ALU.mult)
                K.ts("dve", adt, adt[:, h, :], dt, dt[:, h, :], na[:, h:h + 1], None, ALU.mult, extra=[na])
                pa = pA.next()
                K.mm(pa, pa[:, 0:128], tri, tri[:], adt, adt[:, h, :])
                K.copy("dve", cum, cum[:, h, :], pa, pa[:, 0:128])
                K.act(ecum, ecum[:, h, :], cum, cum[:, h, :], AF.Exp)
                pb = pA.next()
                K.mm(pb, pb[:, 0:128], ones, ones[:], adt, adt[:, h, :])
                K.copy("dve", dts, dts[:, h, :], pb, pb[:, 0:128])
                K.act(etot, etot[:, h, :], dts, dts[:, h, :], AF.Exp)
                K.tt("dve", dts, dts[:, h, :], dts, dts[:, h, :], cum, cum[:, h, :], ALU.subtract)
                K.act(dts, dts[:, h, :], dts, dts[:, h, :], AF.Exp)
                K.tt("dve", dts, dts[:, h, :], dts, dts[:, h, :], dt, dt[:, h, :], ALU.mult)
                K.memset("pool", stT[h], stT[h][:], 0.0)
                K.memset("pool", stB[h][0], stB[h][0][:], 0.0)
                K.memset("pool", stB[h][1], stB[h][1][:], 0.0)
            G = 4
            NG = nchunks // G

            def g4(ap, inner):
                return ap.rearrange("p (g i) -> p g i", g=G)

            def indep(k):
                c0 = k * G
                for g in range(G):
                    c = c0 + g
                    K.mm(pSC, pSC[:, g * 128:(g + 1) * 128], BT, BT[:, c * 128:(c + 1) * 128], CT, CT[:, c * 128:(c + 1) * 128])
                sm = scm.next()
                K.tt("dve", sm, g4(sm[:, :], 128), pSC, g4(pSC[:, :], 128), tri, tri[:, :].unsqueeze(1).to_broadcast([128, G, 128]), ALU.mult)
                pyd, pst = pYD[k % 2], pST[k % 2]
                for h in range(2):
                    hs = slice(64 * h, 64 * h + 64)
                    lsb = lseg.next()
                    K.tt("dve", lsb, g4(lsb[:, :], 128), su, su[:, :].unsqueeze(1).to_broadcast([128, G, 128]),
                         adt, adt[:, h, c0:c0 + G].unsqueeze(2).to_broadcast([128, G, 128]), ALU.mult)
                    for g in range(G):
                        K.mm(pSEG[h], pSEG[h][:, g * 128:(g + 1) * 128], lsb, lsb[:, g * 128:(g + 1) * 128], tri, tri[:])
                    L = Lm.next()
                    K.act(L, L[:], pSEG[h], pSEG[h][:], AF.Exp)
                    Gb = Gt.next()
                    K.tt("dve", Gb, Gb[:], L, L[:], sm, sm[:], ALU.mult)
                    x1 = xd.next()
                    K.tt("dve", x1, x1[:], Xt, Xt[:, c0:c0 + G, hs], dt, dt[:, h, c0:c0 + G].unsqueeze(2).to_broadcast([128, G, 64]), ALU.mult)
                    x2 = xdd.next()
                    K.tt("dve", x2, x2[:], Xt, Xt[:, c0:c0 + G, hs], dts, dts[:, h, c0:c0 + G].unsqueeze(2).to_broadcast([128, G, 64]), ALU.mult)
                    for g in range(G):
                        K.mm(pyd, pyd[:, g * 128 + 64 * h:g * 128 + 64 * h + 64], Gb, Gb[:, g * 128:(g + 1) * 128], x1, x1[:, g, :])
                    for g in range(G):
                        K.mm(pst, pst[:, g * 128 + 64 * h:g * 128 + 64 * h + 64], Bt, Bt[:, c0 + g, :], x2, x2[:, g, :])

            def dep(k):
                c0 = k * G
                pyd, pst = pYD[k % 2], pST[k % 2]
                for g in range(G):
                    c = c0 + g
                    for h in range(2):
                        col = slice(g * 128 + 64 * h, g * 128 + 64 * h + 64)
                        sb_old = stB[h][stp[h] % 2]
                        sb_new = stB[h][(stp[h] + 1) % 2]
                        K.mm(pYO, pYO[:, col], CT, CT[:, c * 128:(c + 1) * 128], sb_old, sb_old[:])
                        K.stt(sb_new, sb_new[:], stT[h], stT[h][:], etot[:, h, c:c + 1], pst, pst[:, col], ALU.mult, ALU.add, extra=[etot])
                        K.stt(stT[h], stT[h][:], stT[h], stT[h][:], etot[:, h, c:c + 1], pst, pst[:, col], ALU.mult, ALU.add, extra=[etot])
                        stp[h] += 1
                yb = ysb.next()
                K.tt("dve", yb, yb[:, :].rearrange("p (g h j) -> p g h j", g=G, h=2), pYO, pYO[:, :].rearrange("p (g h j) -> p g h j", g=G, h=2),
                     ecum, ecum[:, :, c0:c0 + G].rearrange("p h g -> p g h").unsqueeze(3).to_broadcast([128, G, 2, 64]), ALU.mult)
                K.tt("dve", yb, yb[:], yb, yb[:], pyd, pyd[:], ALU.add)
                if d == 0:
                    for h in range(2):
                        hs = slice(64 * h, 64 * h + 64)
                        K.stt(yb, g4(yb[:, :], 128)[:, :, hs], Xt, Xt[:, c0:c0 + G, hs], scs[:, 4 + h:5 + h], yb, g4(yb[:, :], 128)[:, :, hs],
                              ALU.mult, ALU.add, extra=[scs])
                K.dma("pool", None, Y[d, :, c0:c0 + G, :], yb, g4(yb[:, :], 128))

            stp = [0, 0]
            if NG > 0:
                indep(0)
            for k in range(NG):
                if k + 1 < NG:
                    indep(k + 1)
                dep(k)
        K.finish()
    return nc


def run_ssd(of1, conv_w, conv_b, dt_bias, a_log, ssd_d):
    nc = build_ssd()
    xbc = of1[2048:3584]
    dtr = of1[3584:3616]
    tri = np.triu(np.ones((128, 128), np.float32))
    su = np.tril(np.ones((128, 128), np.float32), -1)
    idm = np.eye(128, dtype=np.float32).astype(NPBF)
    maps = []
    for i in range(NCORES):
        g = i // 4
        rows = [slice(128 * i, 128 * i + 128), slice(1024 + 128 * g, 1024 + 128 * g + 128), slice(1280 + 128 * g, 1280 + 128 * g + 128)]
        XBC = np.zeros((2, 3, 128, SEQ + 4), np.float32)
        CWm = np.empty((2, 3, 128, 6), np.float32)
        DTR = np.empty((2, 2, 128, NCH), np.float32)
        SC = np.empty((2, 128, 6), np.float32)
        for d in range(2):
            for gi, rs in enumerate(rows):
                src = xbc[rs]
                XBC[d, gi, :, 2:SEQ + 2] = src[:, ::-1] if d == 1 else src
                w = conv_w[:, rs].T
                CWm[d, gi, :, 0:5] = w[:, ::-1] if d == 1 else w
                CWm[d, gi, :, 5] = conv_b[rs]
            for h in range(2):
                hd = 2 * i + h
                s_ = dtr[d * 16 + hd]
                s_ = s_[::-1] if d == 1 else s_
                DTR[d, h] = s_.reshape(NCH, 128).T
                SC[d, :, h] = dt_bias[d, hd]; SC[d, :, 2 + h] = a_log[d, hd]; SC[d, :, 4 + h] = ssd_d[hd]
        maps.append({"XBC": XBC, "CW": CWm, "DTR": DTR, "SC": SC, "TRI": tri, "SU": su, "ID": idm})
    res = _run(nc, maps)
    yf = np.empty((1024, SEQ), np.float32); yb = np.empty((1024, SEQ), np.float32)
    for i in range(NCORES):
        Yc = res[i]["Y"]
        for d in range(2):
            y = Yc[d].transpose(2, 1, 0).reshape(128, SEQ)
            if d == 0:
                yf[128 * i:128 * i + 128] = y
            else:
                yb[128 * i:128 * i + 128] = y[:, ::-1]
    return yf, yb


def kernel(x, c, e_norm_g, e_ada_w, e_ada_b, e_w_in, e_q_norm, e_k_norm, s5_lam_re, s5_lam_im,
           s5_log_step, s5_b_re, s5_b_im, s5_c_re, s5_c_im, s5_d, s5_w_glu, s5_b_glu, e_w_out,
           o_norm_g, o_ada_w, o_ada_b, o_w_in, o_q_norm, o_k_norm, na_rpb, ssd_conv_w, ssd_conv_b,
           ssd_dt_bias, ssd_a_log, ssd_d, ssd_norm_w, o_w_out):
    f = lambda a: np.asarray(a, np.float32)
    xT = np.ascontiguousarray(f(x)[0].T)
    obf0, of0, mod0 = run_front(0, xT, f(c), f(e_norm_g), f(e_ada_w), f(e_ada_b), f(e_w_in), f(e_q_norm), f(e_k_norm))
    oa = run_attn(obf0)
    yf, yb = run_s5(obf0[1536:2560], f(s5_lam_re)[0], f(s5_lam_im)[0], f(s5_log_step)[0], f(s5_b_re)[0], f(s5_b_im)[0],
                    f(s5_c_re)[0], f(s5_c_im)[0])
    x1T = run_post(0, xT, mod0, f(e_w_out)[0], oa, of0[0:1024], yfT=yf, ybT=yb, uT=obf0[1536:2560], gbT=of0[1024:2048],
                   s5_d=f(s5_d)[0], w_glu=f(s5_w_glu)[0], b_glu=f(s5_b_glu)[0])
    obf1, of1, mod1 = run_front(1, x1T, f(c), f(o_norm_g), f(o_ada_w), f(o_ada_b), f(o_w_in), f(o_q_norm), f(o_k_norm))
    oc = run_na(obf1, f(na_rpb)[0])
    sf, sb_ = run_ssd(of1, f(ssd_conv_w)[0], f(ssd_conv_b)[0], f(ssd_dt_bias)[0], f(ssd_a_log)[0], f(ssd_d)[0])
    x2T = run_post(1, x1T, mod1, f(o_w_out)[0], oc, of1[0:1024], ysT=sf, ys2T=sb_, zT=of1[1024:2048], norm_w=f(ssd_norm_w)[0])
    return np.ascontiguousarray(x2T.T)[None].astype(np.float32)
```

```python
import math
from contextlib import ExitStack
import numpy as np
import ml_dtypes
import concourse.bass as bass
import concourse.mybir as mybir
from concourse.bass_utils import run_bass_kernel_spmd

F32 = mybir.dt.float32
BF16 = mybir.dt.bfloat16
AF = mybir.ActivationFunctionType
ALU = mybir.AluOpType
AX = mybir.AxisListType
NPBF = ml_dtypes.bfloat16

NCORES = 8
D = 2048
SEQ = 16384
TPC = SEQ // NCORES
EPS = 1e-6


class Buf:
    def __init__(self, t, name, onchip=True):
        self.t = t
        self.name = name
        self.onchip = onchip
        self.last_w = None
        self.reads = {}
        self.dsem = None
        self.dcnt = 0

    def __getitem__(self, idx):
        return self.t[idx]


SELF_ORDERED = {"pe"}


class KB:
    def __init__(self, nc, stack):
        self.nc = nc
        self.stack = stack
        self.eng = {"pe": nc.tensor, "dve": nc.vector, "act": nc.scalar, "pool": nc.gpsimd, "sp": nc.sync}
        self.sem, self.cnt, self.seen = {}, {}, {}
        for k in self.eng:
            self.sem[k] = stack.enter_context(nc.semaphore("s_" + k))
            self.cnt[k] = 0
            self.seen[k] = {}
        self.nbuf = 0
        self.dbufs = []

    def sb(self, shape, dt, name=None):
        self.nbuf += 1
        name = "s_" + (name or f"sb{self.nbuf}")
        return Buf(self.stack.enter_context(self.nc.sbuf_tensor(name, list(shape), dt)), name)

    def ps(self, shape, dt=F32, name=None):
        self.nbuf += 1
        name = "p_" + (name or f"ps{self.nbuf}")
        return Buf(self.stack.enter_context(self.nc.psum_tensor(name, list(shape), dt)), name)

    def _wait(self, e, tok):
        if tok is None:
            return
        sem, val, key = tok
        if key == e and e in SELF_ORDERED:
            return
        if self.seen[e].get(key, 0) >= val:
            return
        self.eng[e].wait_ge(sem, val)
        self.seen[e][key] = val

    def _deps(self, e, reads, writes):
        for b in reads:
            if b is not None:
                self._wait(e, b.last_w)
        for b in writes:
            if b is not None:
                self._wait(e, b.last_w)
                for r in b.reads.values():
                    self._wait(e, r)

    def _mark(self, tok, reads, writes):
        for b in reads:
            if b is not None:
                b.reads[tok[2]] = tok
        for b in writes:
            if b is not None:
                b.last_w = tok
                b.reads = {}

    def op(self, e, fn, reads=(), writes=()):
        self._deps(e, reads, writes)
        ins = fn(self.eng[e])
        self.cnt[e] += 1
        ins.then_inc(self.sem[e], 1)
        self._mark((self.sem[e], self.cnt[e], e), reads, writes)
        return ins

    def dma(self, q, out_b, out_ap, in_b, in_ap, **kw):
        self._deps(q, [in_b], [out_b])
        owner = out_b if out_b is not None else in_b
        if owner.dsem is None:
            owner.dsem = self.stack.enter_context(self.nc.semaphore("d_" + owner.name))
            self.dbufs.append(owner)
        ins = self.eng[q].dma_start(out=out_ap, in_=in_ap, **kw)
        owner.dcnt += 16
        ins.then_inc(owner.dsem, 16)
        tok = (owner.dsem, owner.dcnt, "d_" + owner.name)
        self._mark(tok, [in_b], [out_b])
        return tok

    def finish(self, e="sp"):
        for b in self.dbufs:
            self._wait(e, (b.dsem, b.dcnt, "d_" + b.name))

    def mm(self, ob, oap, lb, lap, rb, rap, start=True, stop=True):
        return self.op("pe", lambda e: e.matmul(oap, lhsT=lap, rhs=rap, start=start, stop=stop), [lb, rb], [ob])

    def act(self, ob, oap, ib, iap, func, extra=(), eng="act", **kw):
        return self.op(eng, lambda e: e.activation(out=oap, in_=iap, func=func, **kw), [ib, *extra], [ob])

    def ts(self, eng, ob, oap, ib, iap, s1, s2, op0, op1=None, extra=()):
        if op1 is None:
            return self.op(eng, lambda e: e.tensor_scalar(out=oap, in0=iap, scalar1=s1, scalar2=None, op0=op0), [ib, *extra], [ob])
        return self.op(eng, lambda e: e.tensor_scalar(out=oap, in0=iap, scalar1=s1, scalar2=s2, op0=op0, op1=op1), [ib, *extra], [ob])

    def tt(self, eng, ob, oap, ab, aap, bb, bap, op):
        return self.op(eng, lambda e: e.tensor_tensor(out=oap, in0=aap, in1=bap, op=op), [ab, bb], [ob])

    def stt(self, ob, oap, ab, aap, scalar, bb, bap, op0, op1, extra=()):
        return self.op("dve", lambda e: e.scalar_tensor_tensor(out=oap, in0=aap, scalar=scalar, in1=bap, op0=op0, op1=op1),
                       [ab, bb, *extra], [ob])

    def copy(self, eng, ob, oap, ib, iap):
        if eng == "act":
            return self.op("act", lambda e: e.copy(out=oap, in_=iap), [ib], [ob])
        return self.op(eng, lambda e: e.tensor_copy(out=oap, in_=iap), [ib], [ob])

    def memset(self, eng, ob, oap, val):
        return self.op(eng, lambda e: e.memset(oap, val), [], [ob])


TRACE = False
LAST_EXEC_NS = [None]


def _run(nc, in_maps):
    if TRACE:
        res = run_bass_kernel_spmd(nc, in_maps, core_ids=list(range(NCORES)), trace=True)
        LAST_EXEC_NS[0] = res.exec_time_ns
    else:
        res = run_bass_kernel_spmd(nc, in_maps, core_ids=list(range(NCORES)))
    return res.results


def wview(w_ap, n0, n1):
    return w_ap.rearrange("(k p) n -> p k n", p=128)[:, :, n0:n1]


class Rot:
    def __init__(self, items):
        self.items = items
        self.i = 0

    def next(self):
        b = self.items[self.i % len(self.items)]
        self.i += 1
        return b


def emit_mod(K, nc, dram, wst, pmod, consts):
    cT = K.sb([128, 16], F32, "cT")
    sc = K.sb([128, 16], F32, "sc")
    abT = K.sb([128, 48], F32, "abT")
    gT = K.sb([128, 16], F32, "gT")
    mod = K.sb([128, 48], F32, "mod")
    gs = K.sb([128, 16], F32, "gs")
    K.dma("sp", cT, cT[:], None, dram["cT"][:, :])
    K.dma("sp", abT, abT[:], None, dram["ada_bT"][:, :])
    K.dma("sp", gT, gT[:], None, dram["gT"][:, :])
    K.act(sc, sc[:], cT, cT[:], AF.Silu)
    NB = 6144 // 256
    pend = None
    for nb in range(NB + 1):
        cur = pend
        if nb < NB:
            st = wst.next()
            for h in range(4):
                K.dma("sp" if h % 2 == 0 else "pool", st, st[:, 4 * h:4 * h + 4, :], None,
                      wview(dram["ada_w"], nb * 256, nb * 256 + 256)[:, 4 * h:4 * h + 4, :])
            pend = (st, nb)
        if cur is not None:
            st, b = cur
            for jj in range(2):
                j = 2 * b + jj
                for k in range(16):
                    K.mm(pmod, pmod[:, j:j + 1], st, st[:, k, 128 * jj:128 * jj + 128], sc, sc[:, k:k + 1],
                         start=(k == 0), stop=(k == 15))
    K.tt("dve", mod, mod[:], pmod, pmod[:, 0:48], abT, abT[:], ALU.add)
    K.stt(gs, gs[:], mod, mod[:, 16:32], 1.0, gT, gT[:], ALU.add, ALU.mult)
    return mod, gs


def emit_rstd(K, ob, oap, ib, iap, inv_n):
    K.ts("dve", ob, oap, ib, iap, inv_n, EPS, ALU.mult, ALU.add)
    K.act(ob, oap, ob, oap, AF.Sqrt)
    K.op("dve", lambda e: e.reciprocal(out=oap, in_=oap), [ob], [ob])


def front_plan(N, kinds):
    chunks = []
    nbf = nf = 0
    for ci, kd in enumerate(kinds):
        n0 = ci * 128
        m = min(128, N - n0)
        isbf = (kd == "bf") or (isinstance(kd, tuple))
        if isbf:
            chunks.append((n0, m, kd, "bf", nbf)); nbf += m
        else:
            chunks.append((n0, m, kd, "f", nf)); nf += m
    return chunks, nbf, nf


def emit_front_core(K, nc, dram, N, kinds, rope, xsrc, outs, T=TPC):
    chunks, nbf, nf = front_plan(N, kinds)
    NT = T // 512
    ones = K.sb([128, 128], F32, "ones")
    K.memset("pool", ones, ones[:], 1.0)
    wst = Rot([K.sb([128, 16, 256], F32, f"wst{i}") for i in range(2)])
    wbs = Rot([K.sb([128, 16, 256], BF16, f"wb{i}") for i in range(2)])
    pmod = K.ps([128, 64], F32, "pmod")
    acc = [K.ps([128, 512], F32, f"acc{i}") for i in range(4)]
    mod, gs = emit_mod(K, nc, dram, wst, pmod, None)

    xTb = K.sb([128, 16, T], BF16, "xTb")
    rstd = K.sb([128, T], F32, "rstd")
    sq = K.sb([128, T], F32, "sq")
    pend = xsrc(0)
    for k in range(16):
        xs = pend
        if k + 1 < 16:
            pend = xsrc(k + 1)
        K.copy("dve", xTb, xTb[:, k, :], xs, xs[:, :])
        K.act(sq, sq[:], xs, xs[:, :], AF.Square)
        for t in range(NT):
            K.mm(acc[t], acc[t][:], ones, ones[:], sq, sq[:, 512 * t:512 * t + 512], start=(k == 0), stop=(k == 15))
    for t in range(NT):
        emit_rstd(K, rstd, rstd[:, 512 * t:512 * t + 512], acc[t], acc[t][:], 1.0 / D)

    if rope:
        cosT = K.sb([128, T], F32, "cosT")
        sinT = K.sb([128, T], F32, "sinT")
        K.dma("sp", cosT, cosT[:], None, dram["cosT"][:, :])
        K.dma("sp", sinT, sinT[:], None, dram["sinT"][:, :])
        swp = K.sb([128, 128], F32, "swp")
        K.dma("sp", swp, swp[:], None, dram["swapM"][:, :])
    qkg = K.sb([128, 2], F32, "qkg")
    K.dma("sp", qkg, qkg[:], None, dram["qkg"][:, :])
    psq = K.ps([128, 512], F32, "psq")
    psw = K.ps([128, 512], F32, "psw")
    bvec = K.sb([128, 64], F32, "bvec")
    tmpR = Rot([K.sb([128, 512], F32, f"tmp{i}") for i in range(2)])
    resR = Rot([K.sb([128, 512], F32, f"res{i}") for i in range(2)])
    resbR = Rot([K.sb([128, 512], BF16, f"resb{i}") for i in range(2)])
    sqkR = Rot([K.sb([128, 512], F32, f"sqk{i}") for i in range(2)])
    qnR = Rot([K.sb([128, 512], F32, f"qn{i}") for i in range(2)])
    accR = Rot(acc)

    NB = (N + 255) // 256

    def load_w(nb):
        st = wst.next()
        n0 = nb * 256
        w = min(256, N - n0)
        for h in range(4):
            K.dma("sp" if h % 2 == 0 else "pool", st, st[:, 4 * h:4 * h + 4, 0:w], None,
                  wview(dram["w_in"], n0, n0 + w)[:, 4 * h:4 * h + 4, :])
        return st

    pend = load_w(0)
    for nb in range(NB):
        st = pend
        if nb + 1 < NB:
            pend = load_w(nb + 1)
        n0 = nb * 256
        w = min(256, N - n0)
        wb = wbs.next()
        for k in range(16):
            if k % 2 == 0:
                K.ts("dve", wb, wb[:, k, 0:w], st, st[:, k, 0:w], gs[:, k:k + 1], None, ALU.mult, extra=[gs])
            else:
                K.act(wb, wb[:, k, 0:w], st, st[:, k, 0:w], AF.Identity, extra=[gs], scale=gs[:, k:k + 1])
        for jj in range((w + 127) // 128):
            ci = 2 * nb + jj
            cn0, m, kd, okind, orow = chunks[ci]
            c0 = 128 * jj
            for k in range(16):
                K.mm(pmod, pmod[0:m, 48:49], st, st[:, k, c0:c0 + m], mod, mod[:, k:k + 1], start=(k == 0), stop=(k == 15))
            bcol = bvec[0:m, ci:ci + 1]
            K.copy("dve", bvec, bcol, pmod, pmod[0:m, 48:49])
            for t in range(NT):
                ts_ = slice(512 * t, 512 * t + 512)
                a = accR.next()
                for k in range(16):
                    K.mm(a, a[0:m, :], wb, wb[:, k, c0:c0 + m], xTb, xTb[:, k, ts_], start=(k == 0), stop=(k == 15))
                tmp = tmpR.next()
                K.tt("dve", tmp, tmp[0:m, :], a, a[0:m, :], rstd, rstd[0:m, ts_], ALU.mult)
                if kd == "silu":
                    res = resR.next()
                    K.act(res, res[0:m, :], tmp, tmp[0:m, :], AF.Silu, extra=[bvec], bias=bcol)
                    K.dma("pool", None, outs["f"][orow:orow + m, ts_], res, res[0:m, :])
                elif kd == "f32":
                    res = resR.next()
                    K.act(res, res[0:m, :], tmp, tmp[0:m, :], AF.Identity, extra=[bvec], bias=bcol)
                    K.dma("pool", None, outs["f"][orow:orow + m, ts_], res, res[0:m, :])
                elif kd == "bf":
                    resb = resbR.next()
                    K.act(resb, resb[0:m, :], tmp, tmp[0:m, :], AF.Identity, extra=[bvec], bias=bcol)
                    K.dma("pool", None, outs["bf"][orow:orow + m, ts_], resb, resb[0:m, :])
                else:
                    _, gcol, do_rope = kd
                    res = resR.next()
                    K.act(res, res[:], tmp, tmp[:], AF.Identity, extra=[bvec], bias=bcol)
                    sqk = sqkR.next()
                    K.act(sqk, sqk[:], res, res[:], AF.Square)
                    K.mm(psq, psq[:], ones, ones[:], sqk, sqk[:])
                    emit_rstd(K, sqk, sqk[:], psq, psq[:], 1.0 / 128)
                    qn = qnR.next()
                    K.stt(qn, qn[:], res, res[:], qkg[:, gcol:gcol + 1], sqk, sqk[:], ALU.mult, ALU.mult, extra=[qkg])
                    resb = resbR.next()
                    if do_rope:
                        K.mm(psw, psw[:], swp, swp[:], qn, qn[:])
                        K.tt("dve", sqk, sqk[:], psw, psw[:], sinT, sinT[:, ts_], ALU.mult)
                        K.tt("dve", qn, qn[:], qn, qn[:], cosT, cosT[:, ts_], ALU.mult)
                        K.tt("dve", resb, resb[:], qn, qn[:], sqk, sqk[:], ALU.add)
                    else:
                        K.copy("dve", resb, resb[:], qn, qn[:])
                    K.dma("pool", None, outs["bf"][orow:orow + m, ts_], resb, resb[:])
    return mod


KINDS_E = [("qk", 0, True)] * 8 + [("qk", 1, True)] * 2 + ["bf"] * 2 + ["silu"] * 8 + ["bf"] * 8 + ["silu"] * 8
N_E = 4608
KINDS_O = [("qk", 0, False)] * 8 + [("qk", 1, False)] * 8 + ["bf"] * 8 + ["silu"] * 8 + ["silu"] * 8 + ["f32"] * 12 + ["f32"]
N_O = 6688


def front_dram(nc, N, rope, T=TPC):
    d = {}
    d["cT"] = nc.dram_tensor("cT", [128, 16], F32, kind="ExternalInput").ap()
    d["ada_bT"] = nc.dram_tensor("ada_bT", [128, 48], F32, kind="ExternalInput").ap()
    d["gT"] = nc.dram_tensor("gT", [128, 16], F32, kind="ExternalInput").ap()
    d["ada_w"] = nc.dram_tensor("ada_w", [D, 3 * D], F32, kind="ExternalInput").ap()
    d["w_in"] = nc.dram_tensor("w_in", [D, N], F32, kind="ExternalInput").ap()
    d["qkg"] = nc.dram_tensor("qkg", [128, 2], F32, kind="ExternalInput").ap()
    if rope:
        d["cosT"] = nc.dram_tensor("cosT", [128, T], F32, kind="ExternalInput").ap()
        d["sinT"] = nc.dram_tensor("sinT", [128, T], F32, kind="ExternalInput").ap()
        d["swapM"] = nc.dram_tensor("swapM", [128, 128], F32, kind="ExternalInput").ap()
    return d


def build_front(N, kinds, rope):
    nc = bass.Bass("TRN2", target_bir_lowering=False)
    dram = front_dram(nc, N, rope)
    xT = nc.dram_tensor("xT", [D, TPC], F32, kind="ExternalInput").ap()
    chunks, nbf, nf = front_plan(N, kinds)
    outs = {"bf": nc.dram_tensor("obf", [nbf, TPC], BF16, kind="ExternalOutput").ap(),
            "f": nc.dram_tensor("of", [nf, TPC], F32, kind="ExternalOutput").ap()}
    modo = nc.dram_tensor("modo", [128, 48], F32, kind="ExternalOutput").ap()
    with ExitStack() as st:
        K = KB(nc, st)
        xst = Rot([K.sb([128, TPC], F32, f"xst{i}") for i in range(2)])

        def xsrc(k):
            b = xst.next()
            K.dma("sp", b, b[:, 0:TPC // 2], None, xT[128 * k:128 * k + 128, 0:TPC // 2])
            K.dma("pool", b, b[:, TPC // 2:], None, xT[128 * k:128 * k + 128, TPC // 2:])
            return b
        mod = emit_front_core(K, nc, dram, N, kinds, rope, xsrc, outs)
        K.dma("pool", None, modo[:, :], mod, mod[:])
        K.finish()
    return nc


def pj(v, ncol):
    return np.ascontiguousarray(np.asarray(v, np.float32).reshape(ncol, 128).T)


def rope_tables():
    t = np.arange(SEQ)
    row = (t // 64).astype(np.float32)
    col = (t % 64).astype(np.float32)
    inv = (np.float32(10000.0) ** (-np.arange(32, dtype=np.float32) / np.float32(32))).astype(np.float32)
    ang = np.concatenate([row[:, None] * inv, col[:, None] * inv], axis=-1).astype(np.float32)
    cos, sin = np.cos(ang).astype(np.float32), np.sin(ang).astype(np.float32)
    cosT = np.repeat(cos, 2, axis=1).T
    sgn = np.tile(np.array([-1.0, 1.0], np.float32), 64)
    sinT = (np.repeat(sin, 2, axis=1) * sgn).T
    return np.ascontiguousarray(cosT), np.ascontiguousarray(sinT)


def swap_matrix():
    m = np.zeros((128, 128), np.float32)
    for i in range(64):
        m[2 * i + 1, 2 * i] = 1.0
        m[2 * i, 2 * i + 1] = 1.0
    return m


def front_common_inputs(c, ada_w, ada_b, norm_g, w_in, qn, kn):
    return {"cT": pj(c.reshape(-1), 16), "ada_bT": pj(ada_b.reshape(-1), 48), "gT": pj(norm_g.reshape(-1), 16),
            "ada_w": np.ascontiguousarray(ada_w), "w_in": np.ascontiguousarray(w_in),
            "qkg": np.ascontiguousarray(np.stack([qn.reshape(-1), kn.reshape(-1)], axis=1).astype(np.float32))}


def run_front(layer, xT, c, norm_g, ada_w, ada_b, w_in, q_norm, k_norm):
    rope = (layer == 0)
    nc = build_front(N_E if layer == 0 else N_O, KINDS_E if layer == 0 else KINDS_O, rope)
    base = front_common_inputs(c, ada_w[0], ada_b[0], norm_g[0], w_in[0], q_norm[0], k_norm[0])
    if rope:
        cosT, sinT = rope_tables()
        base["swapM"] = swap_matrix()
    maps = []
    for i in range(NCORES):
        sl = slice(i * TPC, (i + 1) * TPC)
        m = dict(base)
        m["xT"] = np.ascontiguousarray(xT[:, sl])
        if rope:
            m["cosT"] = np.ascontiguousarray(cosT[:, sl])
            m["sinT"] = np.ascontiguousarray(sinT[:, sl])
        maps.append(m)
    res = _run(nc, maps)
    obf = np.concatenate([r["obf"] for r in res], axis=1)
    of = np.concatenate([r["of"] for r in res], axis=1)
    return obf, of, res[0]["modo"]


def build_attn():
    nc = bass.Bass("TRN2", target_bir_lowering=False)
    qT = nc.dram_tensor("qT", [128, SEQ], BF16, kind="ExternalInput").ap()
    kT = nc.dram_tensor("kT", [128, SEQ], BF16, kind="ExternalInput").ap()
    vP = nc.dram_tensor("vP", [128, SEQ], BF16, kind="ExternalInput").ap()
    oT = nc.dram_tensor("oT", [128, SEQ], F32, kind="ExternalOutput").ap()
    scale = 128.0 ** -0.5
    with ExitStack() as st:
        K = KB(nc, st)
        q = K.sb([128, SEQ], BF16, "q")
        k = K.sb([128, SEQ], BF16, "k")
        v = K.sb([128, SEQ], BF16, "v")
        for h in range(4):
            sl = slice(h * 4096, (h + 1) * 4096)
            K.dma("sp", k, k[:, sl], None, kT[:, sl])
            K.dma("pool", q, q[:, sl], None, qT[:, sl])
            K.dma("sp", v, v[:, sl], None, vP[:, sl])
        ones = K.sb([128, 128], BF16, "ones")
        K.memset("pool", ones, ones[:], 1.0)
        pS = [K.ps([128, 512], F32, f"pS{i}") for i in range(3)]
        pO = [K.ps([128, 512], F32, f"pO{i}") for i in range(2)]
        pZ = [K.ps([128, 512], F32, f"pZ{i}") for i in range(2)]
        pT = Rot([K.sb([128, 512], BF16, f"pT{i}") for i in range(3)])
        rec = Rot([K.sb([128, 512], F32, f"rec{i}") for i in range(2)])
        osb = Rot([K.sb([128, 512], F32, f"osb{i}") for i in range(2)])
        NQ, NKB = SEQ // 512, SEQ // 128
        seq = [(a, b) for a in range(NQ) for b in range(NKB)]
        LOOK = 2

        def issue_S(idx):
            qt, kb = seq[idx]
            ps = pS[idx % 3]
            K.mm(ps, ps[:], k, k[:, kb * 128:(kb + 1) * 128], q, q[:, qt * 512:(qt + 1) * 512])

        for i in range(LOOK):
            issue_S(i)
        for idx, (qt, kb) in enumerate(seq):
            if idx + LOOK < len(seq):
                issue_S(idx + LOOK)
            ps = pS[idx % 3]
            p = pT.next()
            K.act(p, p[:], ps, ps[:], AF.Exp, scale=scale)
            o, z = pO[qt % 2], pZ[qt % 2]
            K.mm(o, o[:], v, v[:, kb * 128:(kb + 1) * 128], p, p[:], start=(kb == 0), stop=(kb == NKB - 1))
            K.mm(z, z[:], ones, ones[:], p, p[:], start=(kb == 0), stop=(kb == NKB - 1))
            if kb == NKB - 1:
                r = rec.next()
                K.op("dve", lambda e: e.reciprocal(out=r[:], in_=z[:]), [z], [r])
                ob = osb.next()
                K.tt("dve", ob, ob[:], o, o[:], r, r[:], ALU.mult)
                K.dma("pool", None, oT[:, qt * 512:(qt + 1) * 512], ob, ob[:])
        K.finish()
    return nc


def run_attn(obf):
    nc = build_attn()
    maps = []
    for i in range(NCORES):
        g = i // 4
        vT = obf[1280 + 128 * g:1280 + 128 * g + 128, :]
        vP = np.ascontiguousarray(vT.reshape(128, SEQ // 128, 128).transpose(2, 1, 0).reshape(128, SEQ))
        maps.append({"qT": np.ascontiguousarray(obf[128 * i:128 * i + 128, :]),
                     "kT": np.ascontiguousarray(obf[1024 + 128 * g:1024 + 128 * g + 128, :]),
                     "vP": vP})
    res = _run(nc, maps)
    return np.concatenate([r["oT"] for r in res], axis=0)


S5T = 32
S5NC = SEQ // S5T
TWO_PI = 2.0 * math.pi
S5_OFF = TWO_PI * 128


def s5_consts():
    s = np.arange(32, dtype=np.float32)
    et = np.concatenate([-s, 31 - s, s, s + 1, np.array([32.0], np.float32)]).astype(np.float32)
    ET = np.ascontiguousarray(np.broadcast_to(et, (64, 129)))
    mask = np.zeros((128, 4, 512), np.float32)
    for kb in range(4):
        for sl in range(8):
            sg = 8 * kb + sl
            for t in range(32):
                if sg <= t:
                    mask[sl * 16:(sl + 1) * 16, kb, t * 16:(t + 1) * 16] = 1.0
    ident = np.eye(64, dtype=np.float32)
    return ET, mask, ident


def build_s5(NPR=16):
    nc = bass.Bass("TRN2", target_bir_lowering=False)
    U = nc.dram_tensor("U", [NPR, 4, 128, S5NC], BF16, kind="ExternalInput").ap()
    PRM = nc.dram_tensor("PRM", [NPR, 64, 67], F32, kind="ExternalInput").ap()
    ETd = nc.dram_tensor("ET", [64, 129], F32, kind="ExternalInput").ap()
    MKd = nc.dram_tensor("MK", [128, 4, 512], F32, kind="ExternalInput").ap()
    IDd = nc.dram_tensor("ID", [64, 64], F32, kind="ExternalInput").ap()
    Y = nc.dram_tensor("Y", [NPR, 4, 128, S5NC], F32, kind="ExternalOutput").ap()
    with ExitStack() as st:
        K = KB(nc, st)
        ET = K.sb([64, 129], F32, "ET"); K.dma("sp", ET, ET[:], None, ETd[:, :])
        MK = K.sb([128, 4, 512], F32, "MK"); K.dma("sp", MK, MK[:], None, MKd[:, :, :])
        ID = K.sb([64, 64], F32, "ID"); K.dma("sp", ID, ID[:], None, IDd[:, :])
        uR = Rot([K.sb([128, 4, S5NC], BF16, f"u{i}") for i in range(3)])
        prR = Rot([K.sb([64, 67], F32, f"prm{i}") for i in range(3)])
        sc = K.sb([64, 16], F32, "sc")
        mag = K.sb([64, 129], F32, "mag")
        ang = K.sb([64, 129], F32, "ang")
        ang2 = K.sb([64, 129], F32, "ang2")
        angi = K.sb([64, 129], mybir.dt.int32, "angi")
        PWr = K.sb([64, 129], F32, "PWr")
        PWi = K.sb([64, 129], F32, "PWi")
        bb = K.sb([64, 32], F32, "bb")
        t1 = K.sb([64, 512], F32, "t1")
        t2 = K.sb([64, 512], F32, "t2")
        BLr = K.sb([64, 512], F32, "BLr"); BLi = K.sb([64, 512], F32, "BLi")
        WSr = K.sb([64, 512], F32, "WSr"); WSi = K.sb([64, 512], F32, "WSi")
        CLr = K.sb([64, 512], F32, "CLr"); CLm = K.sb([64, 512], F32, "CLm")
        WOr = [K.sb([64, 512], BF16, f"WOr{i}") for i in range(2)]; WOm = [K.sb([64, 512], BF16, f"WOm{i}") for i in range(2)]
        Msb = [K.sb([128, 4, 512], BF16, f"Msb{i}") for i in range(2)]
        WSs = K.sb([128, 4, 128], BF16, "WSs")
        Pre = [K.sb([64, S5NC], F32, f"Pre{i}") for i in range(2)]
        Pim = [K.sb([64, S5NC], F32, f"Pim{i}") for i in range(2)]
        Hre = K.sb([64, S5NC], BF16, "Hre"); Him = K.sb([64, S5NC], BF16, "Him")
        K.memset("pool", Hre, Hre[:], 0.0); K.memset("pool", Him, Him[:], 0.0)
        Ac = [K.sb([64, 8], F32, f"Ac{i}") for i in range(2)]
        ysb = Rot([K.sb([128, S5NC], F32, f"ysb{i}") for i in range(2)])
        pM = K.ps([128, 512], F32, "pM")
        pW = K.ps([128, 128], F32, "pW")
        pEr = K.ps([64, 512], F32, "pEr"); pEi = K.ps([64, 512], F32, "pEi")
        pY = [K.ps([128, 512], F32, f"pY{i}") for i in range(2)]

        def v3(ap):
            return ap.rearrange("p (s c) -> p s c", c=16)

        def bs(buf, a):
            return buf[:, a:a + 32].unsqueeze(2).to_broadcast([64, 32, 16])

        def bc(buf, a):
            return buf[:, a:a + 16].unsqueeze(1).to_broadcast([64, 32, 16])

        def table(outr, outi, a, xb, xr, xi, neg_im):
            K.tt("dve", t1, v3(t1[:, :]), PWr, bs(PWr, a), xb, bc(xb, xr), ALU.mult)
            K.tt("dve", t2, v3(t2[:, :]), PWi, bs(PWi, a), xb, bc(xb, xi), ALU.mult)
            K.tt("dve", outr, v3(outr[:, :]), t1, v3(t1[:, :]), t2, v3(t2[:, :]), ALU.subtract)
            K.tt("dve", t1, v3(t1[:, :]), PWr, bs(PWr, a), xb, bc(xb, xi), ALU.mult)
            K.tt("dve", t2, v3(t2[:, :]), PWi, bs(PWi, a), xb, bc(xb, xr), ALU.mult)
            if neg_im:
                K.stt(outi, v3(outi[:, :]), t1, v3(t1[:, :]), -1.0, t2, v3(t2[:, :]), ALU.mult, ALU.subtract)
            else:
                K.tt("dve", outi, v3(outi[:, :]), t1, v3(t1[:, :]), t2, v3(t2[:, :]), ALU.add)

        ups = {}

        def load(pr):
            u = uR.next(); p = prR.next()
            K.dma("sp", u, u[:], None, U[pr].rearrange("k p j -> p k j"))
            K.dma("sp", p, p[:], None, PRM[pr])
            ups[pr] = (u, p)

        def P1a(pr):
            u, p = ups[pr]
            K.act(sc, sc[:, 0:1], p, p[:, 2:3], AF.Exp)
            K.tt("dve", sc, sc[:, 1:2], p, p[:, 0:1], sc, sc[:, 0:1], ALU.mult)
            K.tt("dve", sc, sc[:, 2:3], p, p[:, 1:2], sc, sc[:, 0:1], ALU.mult)
            K.act(mag, mag[:], ET, ET[:], AF.Exp, extra=[sc], scale=sc[:, 1:2])
            K.ts("dve", ang, ang[:], ET, ET[:], sc[:, 2:3], S5_OFF, ALU.mult, ALU.add, extra=[sc])
            K.ts("dve", ang2, ang2[:], ang, ang[:], 1.0 / TWO_PI, None, ALU.mult)
            K.copy("dve", angi, angi[:], ang2, ang2[:])
            K.copy("dve", ang2, ang2[:], angi, angi[:])
            K.stt(ang, ang[:], ang2, ang2[:], -TWO_PI, ang, ang[:], ALU.mult, ALU.add)
            K.act(PWi, PWi[:], ang, ang[:], AF.Sin, scale=0.5)
            K.act(ang2, ang2[:], ang, ang[:], AF.Sin, scale=0.25)
            K.tt("dve", ang2, ang2[:], ang2, ang2[:], ang2, ang2[:], ALU.mult)
            K.ts("dve", ang2, ang2[:], ang2, ang2[:], -2.0, 1.0, ALU.mult, ALU.add)
            K.tt("dve", PWr, PWr[:], PWi, PWi[:], PWi, PWi[:], ALU.mult)
            K.ts("dve", PWr, PWr[:], PWr, PWr[:], -2.0, 1.0, ALU.mult, ALU.add)
            K.stt(PWi, PWi[:], PWi, PWi[:], 2.0, ang2, ang2[:], ALU.mult, ALU.mult)
            K.tt("dve", PWr, PWr[:], PWr, PWr[:], mag, mag[:], ALU.mult)
            K.tt("dve", PWi, PWi[:], PWi, PWi[:], mag, mag[:], ALU.mult)
            K.ts("dve", sc, sc[:, 3:4], PWr, PWr[:, 65:66], 1.0, None, ALU.subtract)
            K.tt("dve", sc, sc[:, 4:5], p, p[:, 0:1], p, p[:, 0:1], ALU.mult)
            K.stt(sc, sc[:, 4:5], p, p[:, 1:2], p[:, 1:2], sc, sc[:, 4:5], ALU.mult, ALU.add)
            K.op("dve", lambda e: e.reciprocal(out=sc[:, 4:5], in_=sc[:, 4:5]), [sc], [sc])
            K.tt("dve", sc, sc[:, 5:6], sc, sc[:, 3:4], p, p[:, 0:1], ALU.mult)
            K.stt(sc, sc[:, 5:6], PWi, PWi[:, 65:66], p[:, 1:2], sc, sc[:, 5:6], ALU.mult, ALU.add, extra=[p])
            K.tt("dve", sc, sc[:, 5:6], sc, sc[:, 5:6], sc, sc[:, 4:5], ALU.mult)
            K.tt("dve", sc, sc[:, 6:7], sc, sc[:, 3:4], p, p[:, 1:2], ALU.mult)
            K.stt(sc, sc[:, 6:7], PWi, PWi[:, 65:66], p[:, 0:1], sc, sc[:, 6:7], ALU.mult, ALU.subtract, extra=[p])
            K.tt("dve", sc, sc[:, 6:7], sc, sc[:, 6:7], sc, sc[:, 4:5], ALU.mult)
            K.ts("dve", t1, t1[:, 0:16], p, p[:, 19:35], sc[:, 6:7], None, ALU.mult, extra=[sc])
            K.stt(bb, bb[:, 0:16], p, p[:, 3:19], sc[:, 5:6], t1, t1[:, 0:16], ALU.mult, ALU.subtract, extra=[sc])
            K.ts("dve", t1, t1[:, 0:16], p, p[:, 3:19], sc[:, 6:7], None, ALU.mult, extra=[sc])
            K.stt(bb, bb[:, 16:32], p, p[:, 19:35], sc[:, 5:6], t1, t1[:, 0:16], ALU.mult, ALU.add, extra=[sc])
            table(BLr, BLi, 0, bb, 0, 16, False)
            table(WSr, WSi, 32, bb, 0, 16, False)
            table(CLr, CLm, 64, p, 35, 51, True)
            table(WOr[pr % 2], WOm[pr % 2], 96, p, 35, 51, True)
            K.copy("dve", Ac[pr % 2], Ac[pr % 2][:, 0:1], PWr, PWr[:, 128:129])
            K.copy("dve", Ac[pr % 2], Ac[pr % 2][:, 1:2], PWi, PWi[:, 128:129])
            K.ts("dve", Ac[pr % 2], Ac[pr % 2][:, 2:3], PWi, PWi[:, 128:129], -1.0, None, ALU.mult)

        def P1b(pr):
            u, p = ups[pr]
            for kb in range(4):
                ks = slice(kb * 128, (kb + 1) * 128)
                K.mm(pM, pM[:], BLr, BLr[:, ks], CLr, CLr[:, :], start=True, stop=False)
                K.mm(pM, pM[:], BLi, BLi[:, ks], CLm, CLm[:, :], start=False, stop=True)
                K.tt("dve", Msb[pr % 2], Msb[pr % 2][:, kb, :], pM, pM[:], MK, MK[:, kb, :], ALU.mult)
                K.mm(pW, pW[:, 0:64], WSr, WSr[:, ks], ID, ID[:], start=True, stop=True)
                K.mm(pW, pW[:, 64:128], WSi, WSi[:, ks], ID, ID[:], start=True, stop=True)
                K.copy("act", WSs, WSs[:, kb, :], pW, pW[:])

        def P2a(pr):
            u, p = ups[pr]
            for kb in range(4):
                K.mm(pEr, pEr[:], WSs, WSs[:, kb, 0:64], u, u[:, kb, :], start=(kb == 0), stop=(kb == 3))
            for kb in range(4):
                K.mm(pEi, pEi[:], WSs, WSs[:, kb, 64:128], u, u[:, kb, :], start=(kb == 0), stop=(kb == 3))
            K.copy("act", Pre[0], Pre[0][:], pEr, pEr[:])
            K.copy("dve", Pim[0], Pim[0][:], pEi, pEi[:])

        def P2b(pr):
            A_ = Ac[pr % 2]
            cur = 0
            d = 1
            while d < S5NC:
                re, im, nre, nim = Pre[cur], Pim[cur], Pre[1 - cur], Pim[1 - cur]
                n = S5NC
                K.stt(nre, nre[:, d:n], re, re[:, 0:n - d], A_[:, 0:1], re, re[:, d:n], ALU.mult, ALU.add, extra=[A_])
                K.stt(nre, nre[:, d:n], im, im[:, 0:n - d], A_[:, 2:3], nre, nre[:, d:n], ALU.mult, ALU.add, extra=[A_])
                K.stt(nim, nim[:, d:n], im, im[:, 0:n - d], A_[:, 0:1], im, im[:, d:n], ALU.mult, ALU.add, extra=[A_])
                K.stt(nim, nim[:, d:n], re, re[:, 0:n - d], A_[:, 1:2], nim, nim[:, d:n], ALU.mult, ALU.add, extra=[A_])
                K.copy("act", nre, nre[:, 0:d], re, re[:, 0:d])
                K.copy("act", nim, nim[:, 0:d], im, im[:, 0:d])
                cur = 1 - cur
                d *= 2
                if d < S5NC:
                    K.tt("dve", A_, A_[:, 3:4], A_, A_[:, 0:1], A_, A_[:, 0:1], ALU.mult)
                    K.stt(A_, A_[:, 3:4], A_, A_[:, 1:2], A_[:, 2:3], A_, A_[:, 3:4], ALU.mult, ALU.add)
                    K.stt(A_, A_[:, 4:5], A_, A_[:, 0:1], 2.0, A_, A_[:, 1:2], ALU.mult, ALU.mult)
                    K.copy("dve", A_, A_[:, 0:1], A_, A_[:, 3:4])
                    K.copy("dve", A_, A_[:, 1:2], A_, A_[:, 4:5])
                    K.ts("dve", A_, A_[:, 2:3], A_, A_[:, 4:5], -1.0, None, ALU.mult)
            return cur

        def P2c(pr, cur):
            u, p = ups[pr]
            K.copy("act", Hre, Hre[:, 1:S5NC], Pre[cur], Pre[cur][:, 0:S5NC - 1])
            K.copy("dve", Him, Him[:, 1:S5NC], Pim[cur], Pim[cur][:, 0:S5NC - 1])
            for tb in range(4):
                tsl = slice(tb * 128, (tb + 1) * 128)
                py = pY[tb % 2]
                for kb in range(tb + 1):
                    K.mm(py, py[:], Msb[pr % 2], Msb[pr % 2][:, kb, tsl], u, u[:, kb, :], start=(kb == 0), stop=False)
                K.mm(py, py[:], WOr[pr % 2], WOr[pr % 2][:, tsl], Hre, Hre[:], start=False, stop=False)
                K.mm(py, py[:], WOm[pr % 2], WOm[pr % 2][:, tsl], Him, Him[:], start=False, stop=True)
                yb = ysb.next()
                K.copy("act" if tb % 2 == 0 else "dve", yb, yb[:], py, py[:])
                K.dma("pool", None, Y[pr, tb], yb, yb[:])

        load(0)
        if NPR > 1:
            load(1)
        P1a(0); P1b(0)
        for pr in range(NPR):
            if pr + 2 < NPR:
                load(pr + 2)
            P2a(pr)
            if pr + 1 < NPR:
                P1a(pr + 1)
            cur_ = P2b(pr)
            if pr + 1 < NPR:
                P1b(pr + 1)
            P2c(pr, cur_)
        K.finish()
    return nc


def s5_pack_u(uT):
    outs = []
    for i in range(NCORES):
        U = np.empty((16, 4, 128, S5NC), uT.dtype)
        for gl in range(8):
            g = 8 * i + gl
            ug = uT[16 * g:16 * g + 16, :]
            for d in range(2):
                x = ug[:, ::-1] if d == 1 else ug
                x = x.reshape(16, S5NC, 32).transpose(2, 0, 1)
                U[2 * gl + d] = x.reshape(4, 128, S5NC)
        outs.append(U)
    return outs


def s5_unpack_y(Ys):
    yf = np.empty((1024, SEQ), np.float32); yb = np.empty((1024, SEQ), np.float32)
    for i in range(NCORES):
        for gl in range(8):
            g = 8 * i + gl
            for d in range(2):
                y = Ys[i][2 * gl + d].reshape(32, 16, S5NC).transpose(1, 2, 0).reshape(16, SEQ)
                if d == 0:
                    yf[16 * g:16 * g + 16] = y
                else:
                    yb[16 * g:16 * g + 16] = y[:, ::-1]
    return yf, yb


def s5_params(lam_re, lam_im, log_step, b_re, b_im, c_re, c_im):
    outs = []
    for i in range(NCORES):
        P = np.empty((16, 64, 67), np.float32)
        for gl in range(8):
            g = 8 * i + gl
            for d in range(2):
                pr = 2 * gl + d
                P[pr, :, 0] = lam_re[d, g]; P[pr, :, 1] = lam_im[d, g]; P[pr, :, 2] = log_step[d, g]
                P[pr, :, 3:19] = b_re[d, g]; P[pr, :, 19:35] = b_im[d, g]
                P[pr, :, 35:51] = c_re[d, g].T; P[pr, :, 51:67] = c_im[d, g].T
        outs.append(P)
    return outs


def run_s5(uT, lam_re, lam_im, log_step, b_re, b_im, c_re, c_im):
    nc = build_s5()
    ET, MK, ID = s5_consts()
    Us = s5_pack_u(uT)
    Ps = s5_params(lam_re, lam_im, log_step, b_re, b_im, c_re, c_im)
    maps = [{"U": Us[i], "PRM": Ps[i], "ET": ET, "MK": MK, "ID": ID} for i in range(NCORES)]
    res = _run(nc, maps)
    return s5_unpack_y([r["Y"] for r in res])


def build_post(layer, T=TPC):
    nc = bass.Bass("TRN2", target_bir_lowering=False)
    def di(name, shape, dt=F32):
        return nc.dram_tensor(name, shape, dt, kind="ExternalInput").ap()
    xT = di("xT", [D, T]); w_out = di("w_out", [D, D]); modd = di("mod", [128, 48])
    aT = di("aT", [1024, T]); agT = di("agT", [1024, T])
    if layer == 0:
        yfT = di("yfT", [1024, T]); ybT = di("ybT", [1024, T]); uT = di("uT", [1024, T], BF16); gbT = di("gbT", [1024, T])
        s5d = di("s5d", [128, 8]); w_glu = di("w_glu", [1024, 2048]); bglu = di("bglu", [128, 16])
    else:
        ysT = di("ysT", [1024, T]); ys2T = di("ys2T", [1024, T]); zT = di("zT", [1024, T]); nw = di("nw", [128, 8])
    oT = nc.dram_tensor("oT", [D, T], F32, kind="ExternalOutput").ap()
    NT = T // 512
    with ExitStack() as st:
        K = KB(nc, st)
        mod = K.sb([128, 48], F32, "mod"); K.dma("sp", mod, mod[:], None, modd[:, :])
        wo = K.sb([128, 16, D], BF16, "wo")
        stg = Rot([K.sb([128, D], F32, f"stg{i}") for i in range(2)])
        cast_i = [0]

        def load_w(dst, w_ap, nk):
            for k in range(nk):
                s_ = stg.next()
                K.dma("sp" if k % 2 == 0 else "pool", s_, s_[:], None, w_ap[128 * k:128 * k + 128, :])
                e = ("dve", "act")[cast_i[0] % 2]; cast_i[0] += 1
                K.copy(e, dst, dst[:, k, :], s_, s_[:])
        if layer == 0:
            wg = K.sb([128, 8, 2048], BF16, "wg")
            load_w(wg, w_glu, 8)
            sd = K.sb([128, 8], F32, "sd"); K.dma("sp", sd, sd[:], None, s5d[:, :])
            bg = K.sb([128, 16], F32, "bg"); K.dma("sp", bg, bg[:], None, bglu[:, :])
        else:
            nws = K.sb([128, 8], F32, "nws"); K.dma("sp", nws, nws[:], None, nw[:, :])
            ones = K.sb([128, 128], F32, "ones"); K.memset("pool", ones, ones[:], 1.0)
        load_w(wo, w_out, 16)
        A = Rot([K.sb([128, 16, 512], BF16, f"A{i}") for i in range(2)])
        ld = {nm: Rot([K.sb([128, 512], F32, f"ld_{nm}{i}") for i in range(3)]) for nm in ("a", "b", "c", "x")}
        ldu = Rot([K.sb([128, 512], BF16, f"ldu{i}") for i in range(2)])
        tmp = {nm: Rot([K.sb([128, 512], F32, f"tm_{nm}{i}") for i in range(2)]) for nm in ("p", "q", "r")}
        osb = Rot([K.sb([128, 512], F32, f"osb{i}") for i in range(3)])
        pacc = Rot([K.ps([128, 512], F32, f"pa{i}") for i in range(4)])
        pss = K.ps([128, 512], F32, "pss")
        if layer == 0:
            gy = Rot([K.sb([128, 8, 512], BF16, f"gy{i}") for i in range(2)])
        else:
            yz = Rot([K.sb([128, 8, 512], F32, f"yz{i}") for i in range(2)])
            rs = K.sb([128, 512], F32, "rs")

        def load(nm, ap, q="sp"):
            b = ld[nm].next()
            K.dma(q, b, b[:], None, ap)
            return b

        for t in range(NT):
            ts_ = slice(512 * t, 512 * t + 512)
            Ab = A.next()
            if layer == 0:
                g = gy.next()
                for k in range(8):
                    rows = slice(128 * k, 128 * k + 128)
                    yf = load("a", yfT[rows, ts_]); yb = load("b", ybT[rows, ts_], "pool")
                    ub = ldu.next(); K.dma("sp", ub, ub[:], None, uT[rows, ts_])
                    y = tmp["p"].next()
                    K.stt(y, y[:], ub, ub[:], sd[:, k:k + 1], yf, yf[:], ALU.mult, ALU.add, extra=[sd])
                    K.tt("dve", y, y[:], y, y[:], yb, yb[:], ALU.add)
                    q_ = tmp["q"].next()
                    K.act(q_, q_[:], y, y[:], AF.Square)
                    K.ts("dve", q_, q_[:], q_, q_[:], 0.044715, 1.0, ALU.mult, ALU.add)
                    K.tt("dve", q_, q_[:], q_, q_[:], y, y[:], ALU.mult)
                    K.act(q_, q_[:], q_, q_[:], AF.Sigmoid, scale=1.5957691216057308)
                    K.tt("dve", g, g[:, k, :], q_, q_[:], y, y[:], ALU.mult)
                for n in range(8):
                    pv, pg = pacc.next(), pacc.next()
                    for k in range(8):
                        K.mm(pv, pv[:], wg, wg[:, k, 128 * n:128 * n + 128], g, g[:, k, :], start=(k == 0), stop=(k == 7))
                    for k in range(8):
                        K.mm(pg, pg[:], wg, wg[:, k, 1024 + 128 * n:1024 + 128 * n + 128], g, g[:, k, :], start=(k == 0), stop=(k == 7))
                    sg = tmp["r"].next()
                    K.act(sg, sg[:], pg, pg[:], AF.Sigmoid, extra=[bg], bias=bg[:, 8 + n:9 + n])
                    gb = load("c", gbT[128 * n:128 * n + 128, ts_])
                    v1 = tmp["p"].next()
                    K.stt(v1, v1[:], pv, pv[:], bg[:, n:n + 1], sg, sg[:], ALU.add, ALU.mult, extra=[bg])
                    K.tt("dve", Ab, Ab[:, 8 + n, :], v1, v1[:], gb, gb[:], ALU.mult)
            else:
                yzb = yz.next()
                for k in range(8):
                    rows = slice(128 * k, 128 * k + 128)
                    ys = load("a", ysT[rows, ts_]); z = load("b", zT[rows, ts_], "pool"); y2 = load("c", ys2T[rows, ts_])
                    K.tt("dve", ys, ys[:], ys, ys[:], y2, y2[:], ALU.add)
                    K.tt("dve", yzb, yzb[:, k, :], ys, ys[:], z, z[:], ALU.mult)
                    sq = tmp["q"].next()
                    K.act(sq, sq[:], yzb, yzb[:, k, :], AF.Square)
                    K.mm(pss, pss[:], ones, ones[:], sq, sq[:], start=(k == 0), stop=(k == 7))
                emit_rstd(K, rs, rs[:], pss, pss[:], 1.0 / 1024)
                for k in range(8):
                    K.stt(Ab, Ab[:, 8 + k, :], yzb, yzb[:, k, :], nws[:, k:k + 1], rs, rs[:], ALU.mult, ALU.mult, extra=[nws])
            for n in range(8):
                a = load("a", aT[128 * n:128 * n + 128, ts_]); ag = load("b", agT[128 * n:128 * n + 128, ts_], "pool")
                K.tt("dve", Ab, Ab[:, n, :], a, a[:], ag, ag[:], ALU.mult)
            for n in range(16):
                po = pacc.next()
                for k in range(16):
                    K.mm(po, po[:], wo, wo[:, k, 128 * n:128 * n + 128], Ab, Ab[:, k, :], start=(k == 0), stop=(k == 15))
                xb = load("x", xT[128 * n:128 * n + 128, ts_])
                ob = osb.next()
                K.stt(ob, ob[:], po, po[:], mod[:, 32 + n:33 + n], xb, xb[:], ALU.mult, ALU.add, extra=[mod])
                K.dma("pool", None, oT[128 * n:128 * n + 128, ts_], ob, ob[:])
        K.finish()
    return nc


def run_post(layer, xT, mod, w_out, aT, agT, **kw):
    nc = build_post(layer)
    maps = []
    for i in range(NCORES):
        sl = slice(i * TPC, (i + 1) * TPC)
        m = {"xT": np.ascontiguousarray(xT[:, sl]), "w_out": np.ascontiguousarray(w_out), "mod": np.ascontiguousarray(mod),
             "aT": np.ascontiguousarray(aT[:, sl]), "agT": np.ascontiguousarray(agT[:, sl])}
        if layer == 0:
            for nm in ("yfT", "ybT", "uT", "gbT"):
                m[nm] = np.ascontiguousarray(kw[nm][:, sl])
            m["s5d"] = pj(kw["s5_d"], 8); m["w_glu"] = np.ascontiguousarray(kw["w_glu"]); m["bglu"] = pj(kw["b_glu"], 16)
        else:
            for nm in ("ysT", "ys2T", "zT"):
                m[nm] = np.ascontiguousarray(kw[nm][:, sl])
            m["nw"] = pj(kw["norm_w"], 8)
        maps.append(m)
    res = _run(nc, maps)
    return np.concatenate([r["oT"] for r in res], axis=1)


GW = 64
NROW = SEQ // GW


def na_bias_tables(rpb_h):
    col = np.arange(GW)
    cs = np.clip(col - 8, 0, GW - 16)
    cmask = (col[None, :] >= cs[:, None]) & (col[None, :] < cs[:, None] + 16)
    dc = np.clip(col[None, :] - col[:, None], -15, 15) + 15
    out = np.empty((8, GW, 8, GW), np.float32)
    for v in range(8):
        off = 7 - v
        for kr in range(8):
            b = rpb_h[off + kr][dc]
            b = np.where(cmask, b, np.float32(-1e30)).astype(np.float32)
            out[v, :, kr, :] = b.T
    return out


def build_na():
    nc = bass.Bass("TRN2", target_bir_lowering=False)
    qT = nc.dram_tensor("qT", [128, SEQ], BF16, kind="ExternalInput").ap()
    kT = nc.dram_tensor("kT", [128, SEQ], BF16, kind="ExternalInput").ap()
    vR = nc.dram_tensor("vR", [64, NROW * 128], BF16, kind="ExternalInput").ap()
    bT = nc.dram_tensor("bT", [64, 8, 512], F32, kind="ExternalInput").ap()
    oT = nc.dram_tensor("oT", [128, SEQ], F32, kind="ExternalOutput").ap()
    scale = 128.0 ** -0.5
    with ExitStack() as st:
        K = KB(nc, st)
        q = K.sb([128, SEQ], BF16, "q"); k = K.sb([128, SEQ], BF16, "k"); v = K.sb([64, NROW * 128], BF16, "v")
        bias = K.sb([64, 8, 512], F32, "bias")
        for h in range(4):
            sl = slice(h * 4096, (h + 1) * 4096)
            K.dma("sp", k, k[:, sl], None, kT[:, sl])
            K.dma("pool", q, q[:, sl], None, qT[:, sl])
        for h in range(4):
            sl = slice(h * 8192, (h + 1) * 8192)
            K.dma("sp", v, v[:, sl], None, vR[:, sl])
        K.dma("sp", bias, bias[:], None, bT[:, :, :])
        ones = K.sb([64, 128], BF16, "ones"); K.memset("pool", ones, ones[:], 1.0)
        pS = Rot([K.ps([64, 512], F32, f"pS{i}") for i in range(3)])
        pO = Rot([K.ps([128, 512], F32, f"pO{i}") for i in range(2)])
        pZ = Rot([K.ps([128, 512], F32, f"pZ{i}") for i in range(2)])
        tS = Rot([K.sb([64, 512], F32, f"tS{i}") for i in range(3)])
        pT = Rot([K.sb([64, 512], BF16, f"pT{i}") for i in range(3)])
        rec = Rot([K.sb([128, 512], F32, f"rec{i}") for i in range(2)])
        osb = Rot([K.sb([128, 512], F32, f"osb{i}") for i in range(2)])
        def row_info(r):
            rs = min(max(r - 4, 0), NROW - 8)
            var = r if r < 4 else (4 if r <= 252 else 4 + (r - 252))
            return rs, var

        pS_l = pS.items

        def issue_S(r):
            rs, _ = row_info(r)
            ps = pS_l[r % 3]
            qs = slice(r * 64, r * 64 + 64)
            for j in range(8):
                K.mm(ps, ps[:, j * 64:(j + 1) * 64], k, k[:, (rs + j) * 64:(rs + j) * 64 + 64], q, q[:, qs])

        LOOK = 2
        for r in range(LOOK):
            issue_S(r)
        o = z = None
        for r in range(NROW):
            rr = r % 8
            if rr == 0:
                o, z = pO.next(), pZ.next()
            if r + LOOK < NROW:
                issue_S(r + LOOK)
            rs, var = row_info(r)
            ps = pS_l[r % 3]
            t = tS.next()
            K.stt(t, t[:], ps, ps[:], scale, bias, bias[:, var, :], ALU.mult, ALU.add)
            p = pT.next()
            K.act(p, p[:], t, t[:], AF.Exp)
            cs_ = slice(rr * 64, rr * 64 + 64)
            for j in range(8):
                K.mm(o, o[:, cs_], v, v[:, (rs + j) * 128:(rs + j) * 128 + 128], p, p[:, j * 64:(j + 1) * 64], start=(j == 0), stop=(j == 7))
            for j in range(8):
                K.mm(z, z[:, cs_], ones, ones[:], p, p[:, j * 64:(j + 1) * 64], start=(j == 0), stop=(j == 7))
            if rr == 7:
                r0 = r - 7
                rc = rec.next()
                K.op("dve", lambda e: e.reciprocal(out=rc[:], in_=z[:]), [z], [rc])
                ob = osb.next()
                K.tt("dve", ob, ob[:], o, o[:], rc, rc[:], ALU.mult)
                K.dma("pool", None, oT[:, r0 * 64:r0 * 64 + 512], ob, ob[:])
        K.finish()
    return nc


def run_na(obf, rpb):
    nc = build_na()
    maps = []
    for i in range(NCORES):
        vT = obf[2048 + 128 * i:2048 + 128 * i + 128, :]
        vR = np.ascontiguousarray(vT.reshape(128, NROW, 64).transpose(2, 1, 0).reshape(64, NROW * 128))
        bt = na_bias_tables(np.asarray(rpb[i], np.float32))
        bT = np.ascontiguousarray(bt.transpose(1, 0, 2, 3).reshape(64, 8, 512))
        maps.append({"qT": np.ascontiguousarray(obf[128 * i:128 * i + 128, :]),
                     "kT": np.ascontiguousarray(obf[1024 + 128 * i:1024 + 128 * i + 128, :]), "vR": vR, "bT": bT})
    res = _run(nc, maps)
    return np.concatenate([r["oT"] for r in res], axis=0)


NCH = SEQ // 128


def build_ssd(nchunks=NCH, ndirs=2):
    nc = bass.Bass("TRN2", target_bir_lowering=False)
    XBC = nc.dram_tensor("XBC", [2, 3, 128, SEQ + 4], F32, kind="ExternalInput").ap()
    CW = nc.dram_tensor("CW", [2, 3, 128, 6], F32, kind="ExternalInput").ap()
    DTR = nc.dram_tensor("DTR", [2, 2, 128, NCH], F32, kind="ExternalInput").ap()
    SCd = nc.dram_tensor("SC", [2, 128, 6], F32, kind="ExternalInput").ap()
    TRI = nc.dram_tensor("TRI", [128, 128], F32, kind="ExternalInput").ap()
    SUd = nc.dram_tensor("SU", [128, 128], F32, kind="ExternalInput").ap()
    IDd = nc.dram_tensor("ID", [128, 128], BF16, kind="ExternalInput").ap()
    Y = nc.dram_tensor("Y", [2, 128, NCH, 128], F32, kind="ExternalOutput").ap()
    PIECE = 2048
    with ExitStack() as st:
        K = KB(nc, st)
        tri = K.sb([128, 128], F32, "tri"); K.dma("sp", tri, tri[:], None, TRI[:, :])
        su = K.sb([128, 128], F32, "su"); K.dma("sp", su, su[:], None, SUd[:, :])
        idb = K.sb([128, 128], BF16, "idb"); K.dma("sp", idb, idb[:], None, IDd[:, :])
        ones = K.sb([128, 128], F32, "ones"); K.memset("pool", ones, ones[:], 1.0)
        stage = Rot([K.sb([128, PIECE + 4], F32, f"stage{i}") for i in range(2)])
        acc = K.sb([128, PIECE], F32, "acc")
        xfm = K.sb([128, PIECE], BF16, "xfm")
        BT = K.sb([128, SEQ], BF16, "BT"); CT = K.sb([128, SEQ], BF16, "CT")
        Xt = K.sb([128, NCH, 128], BF16, "Xt"); Bt = K.sb([128, NCH, 128], BF16, "Bt")
        cw = K.sb([128, 3, 6], F32, "cw")
        scs = K.sb([128, 6], F32, "scs")
        dtr = K.sb([128, 2, NCH], F32, "dtr")
        dt = K.sb([128, 2, NCH], F32, "dt"); adt = K.sb([128, 2, NCH], F32, "adt")
        cum = K.sb([128, 2, NCH], F32, "cum"); ecum = K.sb([128, 2, NCH], F32, "ecum")
        dts = K.sb([128, 2, NCH], F32, "dts"); etot = K.sb([128, 2, NCH], F32, "etot")
        na = K.sb([128, 2], F32, "na")
        stT = [K.sb([128, 64], F32, f"stT{h}") for h in range(2)]
        stB = [[K.sb([128, 64], BF16, f"stB{h}_{i}") for i in range(2)] for h in range(2)]
        pSC = K.ps([128, 512], F32, "pSC")
        pSEG = [K.ps([128, 512], F32, f"pSEG{h}") for h in range(2)]
        pYD = [K.ps([128, 512], F32, f"pYD{i}") for i in range(2)]
        pYO = K.ps([128, 512], F32, "pYO")
        pST = [K.ps([128, 512], F32, f"pST{i}") for i in range(2)]

        class _PA:
            def __init__(self):
                self.i = 0

            def next(self):
                self.i += 1
                return pSC if self.i % 2 else pYO
        pA = _PA()
        scm = Rot([K.sb([128, 512], F32, f"scm{i}") for i in range(2)])
        lseg = Rot([K.sb([128, 512], F32, f"lseg{i}") for i in range(4)])
        Lm = Rot([K.sb([128, 512], F32, f"Lm{i}") for i in range(4)])
        Gt = Rot([K.sb([128, 512], BF16, f"Gt{i}") for i in range(4)])
        xd = Rot([K.sb([128, 4, 64], BF16, f"xd{i}") for i in range(4)])
        xdd = Rot([K.sb([128, 4, 64], BF16, f"xdd{i}") for i in range(4)])
        ysb = Rot([K.sb([128, 512], F32, f"ysb{i}") for i in range(3)])
        ws = K.sb([128, 2], F32, "ws")
        for d in range(ndirs):
            K.dma("sp", cw, cw[:], None, CW[d].rearrange("g p t -> p g t"))
            K.dma("sp", scs, scs[:], None, SCd[d])
            K.dma("sp", dtr, dtr[:], None, DTR[d].rearrange("h p c -> p h c"))
            import os
            for g in range(0 if os.environ.get('SKIP_CONV') else 3):
                for pc in range(SEQ // PIECE):
                    sg = stage.next()
                    K.dma("sp" if pc % 2 == 0 else "pool", sg, sg[:], None, XBC[d, g, :, pc * PIECE:pc * PIECE + PIECE + 4])
                    K.ts("dve", acc, acc[:], sg, sg[:, 0:PIECE], cw[:, g, 0:1], None, ALU.mult, extra=[cw])
                    for j in range(1, 5):
                        K.stt(acc, acc[:], sg, sg[:, j:j + PIECE], cw[:, g, j:j + 1], acc, acc[:], ALU.mult, ALU.add, extra=[cw])
                    dst, dbuf = ((xfm[:, :], xfm), (BT[:, pc * PIECE:(pc + 1) * PIECE], BT), (CT[:, pc * PIECE:(pc + 1) * PIECE], CT))[g]
                    K.act(dbuf, dst, acc, acc[:], AF.Silu, extra=[cw], bias=cw[:, g, 5:6])
                    if g < 2:
                        for cc in range(PIECE // 128):
                            c = pc * (PIECE // 128) + cc
                            pa = pA.next()
                            src = xfm[:, cc * 128:(cc + 1) * 128] if g == 0 else BT[:, c * 128:(c + 1) * 128]
                            K.mm(pa, pa[:, 0:128], dbuf, src, idb, idb[:])
                            tgt = Xt if g == 0 else Bt
                            K.copy("act" if cc % 2 == 0 else "pool" if False else "dve", tgt, tgt[:, c, :], pa, pa[:, 0:128])
            import os
            for h in range(0 if os.environ.get('SKIP_DT') else 2):
                K.ts("dve", cum, cum[:, h, :], dtr, dtr[:, h, :], scs[:, h:h + 1], None, ALU.add, extra=[scs])
                K.act(ecum, ecum[:, h, :], cum, cum[:, h, :], AF.Exp)
                K.ts("dve", ecum, ecum[:, h, :], ecum, ecum[:, h, :], 1.0, None, ALU.add)
                K.ts("dve", dt, dt[:, h, :], cum, cum[:, h, :], 0.0, 0.3, ALU.max, ALU.add)
                for _it in range(6):
                    K.act(dts, dts[:, h, :], dt, dt[:, h, :], AF.Exp, scale=-1.0)
                    K.tt("dve", dts, dts[:, h, :], dts, dts[:, h, :], ecum, ecum[:, h, :], ALU.mult)
                    K.stt(dt, dt[:, h, :], dts, dts[:, h, :], -1.0, dt, dt[:, h, :], ALU.add, ALU.add)
                K.act(na, na[:, h:h + 1], scs, scs[:, 2 + h:3 + h], AF.Exp)
                K.ts("dve", na, na[:, h:h + 1], na, na[:, h:h + 1], -1.0, None, ALU.mult)
                K.ts("dve", adt, adt[:, h, :], dt, dt[:, h, :], na[:, h:h + 1], None, ALU.mult, extra=[na])
                pa = pA.next()
                K.mm(pa, pa[:, 0:128], tri, tri[:], adt, adt[:, h, :])
                K.copy("dve", cum, cum[:, h, :], pa, pa[:, 0:128])
                K.act(ecum, ecum[:, h, :], cum, cum[:, h, :], AF.Exp)
                pb = pA.next()
                K.mm(pb, pb[:, 0:128], ones, ones[:], adt, adt[:, h, :])
                K.copy("dve", dts, dts[:, h, :], pb, pb[:, 0:128])
                K.act(etot, etot[:, h, :], dts, dts[:, h, :], AF.Exp)
                K.tt("dve", dts, dts[:, h, :], dts, dts[:, h, :], cum, cum[:, h, :], ALU.subtract)
                K.act(dts, dts[:, h, :], dts, dts[:, h, :], AF.Exp)
                K.tt("dve", dts, dts[:, h, :], dts, dts[:, h, :], dt, dt[:, h, :], ALU.mult)
                K.memset("pool", stT[h], stT[h][:], 0.0)
                K.memset("pool", stB[h][0], stB[h][0][:], 0.0)
                K.memset("pool", stB[h][1], stB[h][1][:], 0.0)
            G = 4
            NG = nchunks // G

            def g4(ap, inner):
                return ap.rearrange("p (g i) -> p g i", g=G)

            def indep(k):
                c0 = k * G
                for g in range(G):
                    c = c0 + g
                    K.mm(pSC, pSC[:, g * 128:(g + 1) * 128], BT, BT[:, c * 128:(c + 1) * 128], CT, CT[:, c * 128:(c + 1) * 128])
                sm = scm.next()
                K.tt("dve", sm, g4(sm[:, :], 128), pSC, g4(pSC[:, :], 128), tri, tri[:, :].unsqueeze(1).to_broadcast([128, G, 128]), ALU.mult)
                pyd, pst = pYD[k % 2], pST[k % 2]
                for h in range(2):
                    hs = slice(64 * h, 64 * h + 64)
                    lsb = lseg.next()
                    K.tt("dve", lsb, g4(lsb[:, :], 128), su, su[:, :].unsqueeze(1).to_broadcast([128, G, 128]),
                         adt, adt[:, h, c0:c0 + G].unsqueeze(2).to_broadcast([128, G, 128]), ALU.mult)
                    for g in range(G):
                        K.mm(pSEG[h], pSEG[h][:, g * 128:(g + 1) * 128], lsb, lsb[:, g * 128:(g + 1) * 128], tri, tri[:])
                    L = Lm.next()
                    K.act(L, L[:], pSEG[h], pSEG[h][:], AF.Exp)
                    Gb = Gt.next()
                    K.tt("dve", Gb, Gb[:], L, L[:], sm, sm[:], ALU.mult)
                    x1 = xd.next()
                    K.tt("dve", x1, x1[:], Xt, Xt[:, c0:c0 + G, hs], dt, dt[:, h, c0:c0 + G].unsqueeze(2).to_broadcast([128, G, 64]), ALU.mult)
                    x2 = xdd.next()
                    K.tt("dve", x2, x2[:], Xt, Xt[:, c0:c0 + G, hs], dts, dts[:, h, c0:c0 + G].unsqueeze(2).to_broadcast([128, G, 64]), ALU.mult)
                    for g in range(G):
                        K.mm(pyd, pyd[:, g * 128 + 64 * h:g * 128 + 64 * h + 64], Gb, Gb[:, g * 128:(g + 1) * 128], x1, x1[:, g, :])
                    for g in range(G):
                        K.mm(pst, pst[:, g * 128 + 64 * h:g * 128 + 64 * h + 64], Bt, Bt[:, c0 + g, :], x2, x2[:, g, :])

            def dep(k):
                c0 = k * G
                pyd, pst = pYD[k % 2], pST[k % 2]
                for g in range(G):
                    c = c0 + g
                    for h in range(2):
                        col = slice(g * 128 + 64 * h, g * 128 + 64 * h + 64)
                        sb_old = stB[h][stp[h] % 2]
                        sb_new = stB[h][(stp[h] + 1) % 2]
                        K.mm(pYO, pYO[:, col], CT, CT[:, c * 128:(c + 1) * 128], sb_old, sb_old[:])
                        K.stt(sb_new, sb_new[:], stT[h], stT[h][:], etot[:, h, c:c + 1], pst, pst[:, col], ALU.mult, ALU.add, extra=[etot])
                        K.stt(stT[h], stT[h][:], stT[h], stT[h][:], etot[:, h, c:c + 1], pst, pst[:, col], ALU.mult, ALU.add, extra=[etot])
                        stp[h] += 1
                yb = ysb.next()
                K.tt("dve", yb, yb[:, :].rearrange("p (g h j) -> p g h j", g=G, h=2), pYO, pYO[:, :].rearrange("p (g h j) -> p g h j", g=G, h=2),
                     ecum, ecum[:, :, c0:c0 + G].rearrange("p h g -> p g h").unsqueeze(3).to_broadcast([128, G, 2, 64]), ALU.mult)
                K.tt("dve", yb, yb[:], yb, yb[:], pyd, pyd[:], ALU.add)
                if d == 0:
                    for h in range(2):
                        hs = slice(64 * h, 64 * h + 64)
                        K.stt(yb, g4(yb[:, :], 128)[:, :, hs], Xt, Xt[:, c0:c0 + G, hs], scs[:, 4 + h:5 + h], yb, g4(yb[:, :], 128)[:, :, hs],
                              ALU.mult, ALU.add, extra=[scs])
                K.dma("pool", None, Y[d, :, c0:c0 + G, :], yb, g4(yb[:, :], 128))

            stp = [0, 0]
            if NG > 0:
                indep(0)
            for k in range(NG):
                if k + 1 < NG:
                    indep(k + 1)
                dep(k)
        K.finish()
    return nc


def run_ssd(of1, conv_w, conv_b, dt_bias, a_log, ssd_d):
    nc = build_ssd()
    xbc = of1[2048:3584]
    dtr = of1[3584:3616]
    tri = np.triu(np.ones((128, 128), np.float32))
    su = np.tril(np.ones((128, 128), np.float32), -1)
    idm = np.eye(128, dtype=np.float32).astype(NPBF)
    maps = []
    for i in range(NCORES):
        g = i // 4
        rows = [slice(128 * i, 128 * i + 128), slice(1024 + 128 * g, 1024 + 128 * g + 128), slice(1280 + 128 * g, 1280 + 128 * g + 128)]
        XBC = np.zeros((2, 3, 128, SEQ + 4), np.float32)
        CWm = np.empty((2, 3, 128, 6), np.float32)
        DTR = np.empty((2, 2, 128, NCH), np.float32)
        SC = np.empty((2, 128, 6), np.float32)
        for d in range(2):
            for gi, rs in enumerate(rows):
                src = xbc[rs]
                XBC[d, gi, :, 2:SEQ + 2] = src[:, ::-1] if d == 1 else src
                w = conv_w[:, rs].T
                CWm[d, gi, :, 0:5] = w[:, ::-1] if d == 1 else w
                CWm[d, gi, :, 5] = conv_b[rs]
            for h in range(2):
                hd = 2 * i + h
                s_ = dtr[d * 16 + hd]
                s_ = s_[::-1] if d == 1 else s_
                DTR[d, h] = s_.reshape(NCH, 128).T
                SC[d, :, h] = dt_bias[d, hd]; SC[d, :, 2 + h] = a_log[d, hd]; SC[d, :, 4 + h] = ssd_d[hd]
        maps.append({"XBC": XBC, "CW": CWm, "DTR": DTR, "SC": SC, "TRI": tri, "SU": su, "ID": idm})
    res = _run(nc, maps)
    yf = np.empty((1024, SEQ), np.float32); yb = np.empty((1024, SEQ), np.float32)
    for i in range(NCORES):
        Yc = res[i]["Y"]
        for d in range(2):
            y = Yc[d].transpose(2, 1, 0).reshape(128, SEQ)
            if d == 0:
                yf[128 * i:128 * i + 128] = y
            else:
                yb[128 * i:128 * i + 128] = y[:, ::-1]
    return yf, yb


def kernel(x, c, e_norm_g, e_ada_w, e_ada_b, e_w_in, e_q_norm, e_k_norm, s5_lam_re, s5_lam_im,
           s5_log_step, s5_b_re, s5_b_im, s5_c_re, s5_c_im, s5_d, s5_w_glu, s5_b_glu, e_w_out,
           o_norm_g, o_ada_w, o_ada_b, o_w_in, o_q_norm, o_k_norm, na_rpb, ssd_conv_w, ssd_conv_b,
           ssd_dt_bias, ssd_a_log, ssd_d, ssd_norm_w, o_w_out):
    f = lambda a: np.asarray(a, np.float32)
    xT = np.ascontiguousarray(f(x)[0].T)
    obf0, of0, mod0 = run_front(0, xT, f(c), f(e_norm_g), f(e_ada_w), f(e_ada_b), f(e_w_in), f(e_q_norm), f(e_k_norm))
    oa = run_attn(obf0)
    yf, yb = run_s5(obf0[1536:2560], f(s5_lam_re)[0], f(s5_lam_im)[0], f(s5_log_step)[0], f(s5_b_re)[0], f(s5_b_im)[0],
                    f(s5_c_re)[0], f(s5_c_im)[0])
    x1T = run_post(0, xT, mod0, f(e_w_out)[0], oa, of0[0:1024], yfT=yf, ybT=yb, uT=obf0[1536:2560], gbT=of0[1024:2048],
                   s5_d=f(s5_d)[0], w_glu=f(s5_w_glu)[0], b_glu=f(s5_b_glu)[0])
    obf1, of1, mod1 = run_front(1, x1T, f(c), f(o_norm_g), f(o_ada_w), f(o_ada_b), f(o_w_in), f(o_q_norm), f(o_k_norm))
    oc = run_na(obf1, f(na_rpb)[0])
    sf, sb_ = run_ssd(of1, f(ssd_conv_w)[0], f(ssd_conv_b)[0], f(ssd_dt_bias)[0], f(ssd_a_log)[0], f(ssd_d)[0])
    x2T = run_post(1, x1T, mod1, f(o_w_out)[0], oc, of1[0:1024], ysT=sf, ys2T=sb_, zT=of1[1024:2048], norm_w=f(ssd_norm_w)[0])
    return np.ascontiguousarray(x2T.T)[None].astype(np.float32)
```

```python
import math
from contextlib import ExitStack
import numpy as np
import ml_dtypes
import concourse.bass as bass
import concourse.mybir as mybir
from concourse.bass_utils import run_bass_kernel_spmd

F32 = mybir.dt.float32
BF16 = mybir.dt.bfloat16
AF = mybir.ActivationFunctionType
ALU = mybir.AluOpType
AX = mybir.AxisListType
NPBF = ml_dtypes.bfloat16

NCORES = 8
D = 2048
SEQ = 16384
TPC = SEQ // NCORES
EPS = 1e-6


class Buf:
    def __init__(self, t, name, onchip=True):
        self.t = t
        self.name = name
        self.onchip = onchip
        self.last_w = None
        self.reads = {}
        self.dsem = None
        self.dcnt = 0

    def __getitem__(self, idx):
        return self.t[idx]


SELF_ORDERED = {"pe"}


class KB:
    def __init__(self, nc, stack):
        self.nc = nc
        self.stack = stack
        self.eng = {"pe": nc.tensor, "dve": nc.vector, "act": nc.scalar, "pool": nc.gpsimd, "sp": nc.sync}
        self.sem, self.cnt, self.seen = {}, {}, {}
        for k in self.eng:
            self.sem[k] = stack.enter_context(nc.semaphore("s_" + k))
            self.cnt[k] = 0
            self.seen[k] = {}
        self.nbuf = 0
        self.dbufs = []

    def sb(self, shape, dt, name=None):
        self.nbuf += 1
        name = "s_" + (name or f"sb{self.nbuf}")
        return Buf(self.stack.enter_context(self.nc.sbuf_tensor(name, list(shape), dt)), name)

    def ps(self, shape, dt=F32, name=None):
        self.nbuf += 1
        name = "p_" + (name or f"ps{self.nbuf}")
        return Buf(self.stack.enter_context(self.nc.psum_tensor(name, list(shape), dt)), name)

    def _wait(self, e, tok):
        if tok is None:
            return
        sem, val, key = tok
        if key == e and e in SELF_ORDERED:
            return
        if self.seen[e].get(key, 0) >= val:
            return
        self.eng[e].wait_ge(sem, val)
        self.seen[e][key] = val

    def _deps(self, e, reads, writes):
        for b in reads:
            if b is not None:
                self._wait(e, b.last_w)
        for b in writes:
            if b is not None:
                self._wait(e, b.last_w)
                for r in b.reads.values():
                    self._wait(e, r)

    def _mark(self, tok, reads, writes):
        for b in reads:
            if b is not None:
                b.reads[tok[2]] = tok
        for b in writes:
            if b is not None:
                b.last_w = tok
                b.reads = {}

    def op(self, e, fn, reads=(), writes=()):
        self._deps(e, reads, writes)
        ins = fn(self.eng[e])
        self.cnt[e] += 1
        ins.then_inc(self.sem[e], 1)
        self._mark((self.sem[e], self.cnt[e], e), reads, writes)
        return ins

    def dma(self, q, out_b, out_ap, in_b, in_ap, **kw):
        self._deps(q, [in_b], [out_b])
        owner = out_b if out_b is not None else in_b
        if owner.dsem is None:
            owner.dsem = self.stack.enter_context(self.nc.semaphore("d_" + owner.name))
            self.dbufs.append(owner)
        ins = self.eng[q].dma_start(out=out_ap, in_=in_ap, **kw)
        owner.dcnt += 16
        ins.then_inc(owner.dsem, 16)
        tok = (owner.dsem, owner.dcnt, "d_" + owner.name)
        self._mark(tok, [in_b], [out_b])
        return tok

    def finish(self, e="sp"):
        for b in self.dbufs:
            self._wait(e, (b.dsem, b.dcnt, "d_" + b.name))

    def mm(self, ob, oap, lb, lap, rb, rap, start=True, stop=True):
        return self.op("pe", lambda e: e.matmul(oap, lhsT=lap, rhs=rap, start=start, stop=stop), [lb, rb], [ob])

    def act(self, ob, oap, ib, iap, func, extra=(), eng="act", **kw):
        return self.op(eng, lambda e: e.activation(out=oap, in_=iap, func=func, **kw), [ib, *extra], [ob])

    def ts(self, eng, ob, oap, ib, iap, s1, s2, op0, op1=None, extra=()):
        if op1 is None:
            return self.op(eng, lambda e: e.tensor_scalar(out=oap, in0=iap, scalar1=s1, scalar2=None, op0=op0), [ib, *extra], [ob])
        return self.op(eng, lambda e: e.tensor_scalar(out=oap, in0=iap, scalar1=s1, scalar2=s2, op0=op0, op1=op1), [ib, *extra], [ob])

    def tt(self, eng, ob, oap, ab, aap, bb, bap, op):
        return self.op(eng, lambda e: e.tensor_tensor(out=oap, in0=aap, in1=bap, op=op), [ab, bb], [ob])

    def stt(self, ob, oap, ab, aap, scalar, bb, bap, op0, op1, extra=()):
        return self.op("dve", lambda e: e.scalar_tensor_tensor(out=oap, in0=aap, scalar=scalar, in1=bap, op0=op0, op1=op1),
                       [ab, bb, *extra], [ob])

    def copy(self, eng, ob, oap, ib, iap):
        if eng == "act":
            return self.op("act", lambda e: e.copy(out=oap, in_=iap), [ib], [ob])
        return self.op(eng, lambda e: e.tensor_copy(out=oap, in_=iap), [ib], [ob])

    def memset(self, eng, ob, oap, val):
        return self.op(eng, lambda e: e.memset(oap, val), [], [ob])


TRACE = False
LAST_EXEC_NS = [None]


def _run(nc, in_maps):
    if TRACE:
        res = run_bass_kernel_spmd(nc, in_maps, core_ids=list(range(NCORES)), trace=True)
        LAST_EXEC_NS[0] = res.exec_time_ns
    else:
        res = run_bass_kernel_spmd(nc, in_maps, core_ids=list(range(NCORES)))
    return res.results


def wview(w_ap, n0, n1):
    return w_ap.rearrange("(k p) n -> p k n", p=128)[:, :, n0:n1]


class Rot:
    def __init__(self, items):
        self.items = items
        self.i = 0

    def next(self):
        b = self.items[self.i % len(self.items)]
        self.i += 1
        return b


def emit_mod(K, nc, dram, wst, pmod, consts):
    cT = K.sb([128, 16], F32, "cT")
    sc = K.sb([128, 16], F32, "sc")
    abT = K.sb([128, 48], F32, "abT")
    gT = K.sb([128, 16], F32, "gT")
    mod = K.sb([128, 48], F32, "mod")
    gs = K.sb([128, 16], F32, "gs")
    K.dma("sp", cT, cT[:], None, dram["cT"][:, :])
    K.dma("sp", abT, abT[:], None, dram["ada_bT"][:, :])
    K.dma("sp", gT, gT[:], None, dram["gT"][:, :])
    K.act(sc, sc[:], cT, cT[:], AF.Silu)
    NB = 6144 // 256
    pend = None
    for nb in range(NB + 1):
        cur = pend
        if nb < NB:
            st = wst.next()
            for h in range(4):
                K.dma("sp" if h % 2 == 0 else "pool", st, st[:, 4 * h:4 * h + 4, :], None,
                      wview(dram["ada_w"], nb * 256, nb * 256 + 256)[:, 4 * h:4 * h + 4, :])
            pend = (st, nb)
        if cur is not None:
            st, b = cur
            for jj in range(2):
                j = 2 * b + jj
                for k in range(16):
                    K.mm(pmod, pmod[:, j:j + 1], st, st[:, k, 128 * jj:128 * jj + 128], sc, sc[:, k:k + 1],
                         start=(k == 0), stop=(k == 15))
    K.tt("dve", mod, mod[:], pmod, pmod[:, 0:48], abT, abT[:], ALU.add)
    K.stt(gs, gs[:], mod, mod[:, 16:32], 1.0, gT, gT[:], ALU.add, ALU.mult)
    return mod, gs


def emit_rstd(K, ob, oap, ib, iap, inv_n):
    K.ts("dve", ob, oap, ib, iap, inv_n, EPS, ALU.mult, ALU.add)
    K.act(ob, oap, ob, oap, AF.Sqrt)
    K.op("dve", lambda e: e.reciprocal(out=oap, in_=oap), [ob], [ob])


def front_plan(N, kinds):
    chunks = []
    nbf = nf = 0
    for ci, kd in enumerate(kinds):
        n0 = ci * 128
        m = min(128, N - n0)
        isbf = (kd == "bf") or (isinstance(kd, tuple))
        if isbf:
            chunks.append((n0, m, kd, "bf", nbf)); nbf += m
        else:
            chunks.append((n0, m, kd, "f", nf)); nf += m
    return chunks, nbf, nf


def emit_front_core(K, nc, dram, N, kinds, rope, xsrc, outs, T=TPC):
    chunks, nbf, nf = front_plan(N, kinds)
    NT = T // 512
    ones = K.sb([128, 128], F32, "ones")
    K.memset("pool", ones, ones[:], 1.0)
    wst = Rot([K.sb([128, 16, 256], F32, f"wst{i}") for i in range(2)])
    wbs = Rot([K.sb([128, 16, 256], BF16, f"wb{i}") for i in range(2)])
    pmod = K.ps([128, 64], F32, "pmod")
    acc = [K.ps([128, 512], F32, f"acc{i}") for i in range(4)]
    mod, gs = emit_mod(K, nc, dram, wst, pmod, None)

    xTb = K.sb([128, 16, T], BF16, "xTb")
    rstd = K.sb([128, T], F32, "rstd")
    sq = K.sb([128, T], F32, "sq")
    pend = xsrc(0)
    for k in range(16):
        xs = pend
        if k + 1 < 16:
            pend = xsrc(k + 1)
        K.copy("dve", xTb, xTb[:, k, :], xs, xs[:, :])
        K.act(sq, sq[:], xs, xs[:, :], AF.Square)
        for t in range(NT):
            K.mm(acc[t], acc[t][:], ones, ones[:], sq, sq[:, 512 * t:512 * t + 512], start=(k == 0), stop=(k == 15))
    for t in range(NT):
        emit_rstd(K, rstd, rstd[:, 512 * t:512 * t + 512], acc[t], acc[t][:], 1.0 / D)

    if rope:
        cosT = K.sb([128, T], F32, "cosT")
        sinT = K.sb([128, T], F32, "sinT")
        K.dma("sp", cosT, cosT[:], None, dram["cosT"][:, :])
        K.dma("sp", sinT, sinT[:], None, dram["sinT"][:, :])
        swp = K.sb([128, 128], F32, "swp")
        K.dma("sp", swp, swp[:], None, dram["swapM"][:, :])
    qkg = K.sb([128, 2], F32, "qkg")
    K.dma("sp", qkg, qkg[:], None, dram["qkg"][:, :])
    psq = K.ps([128, 512], F32, "psq")
    psw = K.ps([128, 512], F32, "psw")
    bvec = K.sb([128, 64], F32, "bvec")
    tmpR = Rot([K.sb([128, 512], F32, f"tmp{i}") for i in range(2)])
    resR = Rot([K.sb([128, 512], F32, f"res{i}") for i in range(2)])
    resbR = Rot([K.sb([128, 512], BF16, f"resb{i}") for i in range(2)])
    sqkR = Rot([K.sb([128, 512], F32, f"sqk{i}") for i in range(2)])
    qnR = Rot([K.sb([128, 512], F32, f"qn{i}") for i in range(2)])
    accR = Rot(acc)

    NB = (N + 255) // 256

    def load_w(nb):
        st = wst.next()
        n0 = nb * 256
        w = min(256, N - n0)
        for h in range(4):
            K.dma("sp" if h % 2 == 0 else "pool", st, st[:, 4 * h:4 * h + 4, 0:w], None,
                  wview(dram["w_in"], n0, n0 + w)[:, 4 * h:4 * h + 4, :])
        return st

    pend = load_w(0)
    for nb in range(NB):
        st = pend
        if nb + 1 < NB:
            pend = load_w(nb + 1)
        n0 = nb * 256
        w = min(256, N - n0)
        wb = wbs.next()
        for k in range(16):
            if k % 2 == 0:
                K.ts("dve", wb, wb[:, k, 0:w], st, st[:, k, 0:w], gs[:, k:k + 1], None, ALU.mult, extra=[gs])
            else:
                K.act(wb, wb[:, k, 0:w], st, st[:, k, 0:w], AF.Identity, extra=[gs], scale=gs[:, k:k + 1])
        for jj in range((w + 127) // 128):
            ci = 2 * nb + jj
            cn0, m, kd, okind, orow = chunks[ci]
            c0 = 128 * jj
            for k in range(16):
                K.mm(pmod, pmod[0:m, 48:49], st, st[:, k, c0:c0 + m], mod, mod[:, k:k + 1], start=(k == 0), stop=(k == 15))
            bcol = bvec[0:m, ci:ci + 1]
            K.copy("dve", bvec, bcol, pmod, pmod[0:m, 48:49])
            for t in range(NT):
                ts_ = slice(512 * t, 512 * t + 512)
                a = accR.next()
                for k in range(16):
                    K.mm(a, a[0:m, :], wb, wb[:, k, c0:c0 + m], xTb, xTb[:, k, ts_], start=(k == 0), stop=(k == 15))
                tmp = tmpR.next()
                K.tt("dve", tmp, tmp[0:m, :], a, a[0:m, :], rstd, rstd[0:m, ts_], ALU.mult)
                if kd == "silu":
                    res = resR.next()
                    K.act(res, res[0:m, :], tmp, tmp[0:m, :], AF.Silu, extra=[bvec], bias=bcol)
                    K.dma("pool", None, outs["f"][orow:orow + m, ts_], res, res[0:m, :])
                elif kd == "f32":
                    res = resR.next()
                    K.act(res, res[0:m, :], tmp, tmp[0:m, :], AF.Identity, extra=[bvec], bias=bcol)
                    K.dma("pool", None, outs["f"][orow:orow + m, ts_], res, res[0:m, :])
                elif kd == "bf":
                    resb = resbR.next()
                    K.act(resb, resb[0:m, :], tmp, tmp[0:m, :], AF.Identity, extra=[bvec], bias=bcol)
                    K.dma("pool", None, outs["bf"][orow:orow + m, ts_], resb, resb[0:m, :])
                else:
                    _, gcol, do_rope = kd
                    res = resR.next()
                    K.act(res, res[:], tmp, tmp[:], AF.Identity, extra=[bvec], bias=bcol)
                    sqk = sqkR.next()
                    K.act(sqk, sqk[:], res, res[:], AF.Square)
                    K.mm(psq, psq[:], ones, ones[:], sqk, sqk[:])
                    emit_rstd(K, sqk, sqk[:], psq, psq[:], 1.0 / 128)
                    qn = qnR.next()
                    K.stt(qn, qn[:], res, res[:], qkg[:, gcol:gcol + 1], sqk, sqk[:], ALU.mult, ALU.mult, extra=[qkg])
                    resb = resbR.next()
                    if do_rope:
                        K.mm(psw, psw[:], swp, swp[:], qn, qn[:])
                        K.tt("dve", sqk, sqk[:], psw, psw[:], sinT, sinT[:, ts_], ALU.mult)
                        K.tt("dve", qn, qn[:], qn, qn[:], cosT, cosT[:, ts_], ALU.mult)
                        K.tt("dve", resb, resb[:], qn, qn[:], sqk, sqk[:], ALU.add)
                    else:
                        K.copy("dve", resb, resb[:], qn, qn[:])
                    K.dma("pool", None, outs["bf"][orow:orow + m, ts_], resb, resb[:])
    return mod


KINDS_E = [("qk", 0, True)] * 8 + [("qk", 1, True)] * 2 + ["bf"] * 2 + ["silu"] * 8 + ["bf"] * 8 + ["silu"] * 8
N_E = 4608
KINDS_O = [("qk", 0, False)] * 8 + [("qk", 1, False)] * 8 + ["bf"] * 8 + ["silu"] * 8 + ["silu"] * 8 + ["f32"] * 12 + ["f32"]
N_O = 6688


def front_dram(nc, N, rope, T=TPC):
    d = {}
    d["cT"] = nc.dram_tensor("cT", [128, 16], F32, kind="ExternalInput").ap()
    d["ada_bT"] = nc.dram_tensor("ada_bT", [128, 48], F32, kind="ExternalInput").ap()
    d["gT"] = nc.dram_tensor("gT", [128, 16], F32, kind="ExternalInput").ap()
    d["ada_w"] = nc.dram_tensor("ada_w", [D, 3 * D], F32, kind="ExternalInput").ap()
    d["w_in"] = nc.dram_tensor("w_in", [D, N], F32, kind="ExternalInput").ap()
    d["qkg"] = nc.dram_tensor("qkg", [128, 2], F32, kind="ExternalInput").ap()
    if rope:
        d["cosT"] = nc.dram_tensor("cosT", [128, T], F32, kind="ExternalInput").ap()
        d["sinT"] = nc.dram_tensor("sinT", [128, T], F32, kind="ExternalInput").ap()
        d["swapM"] = nc.dram_tensor("swapM", [128, 128], F32, kind="ExternalInput").ap()
    return d


def build_front(N, kinds, rope):
    nc = bass.Bass("TRN2", target_bir_lowering=False)
    dram = front_dram(nc, N, rope)
    xT = nc.dram_tensor("xT", [D, TPC], F32, kind="ExternalInput").ap()
    chunks, nbf, nf = front_plan(N, kinds)
    outs = {"bf": nc.dram_tensor("obf", [nbf, TPC], BF16, kind="ExternalOutput").ap(),
            "f": nc.dram_tensor("of", [nf, TPC], F32, kind="ExternalOutput").ap()}
    modo = nc.dram_tensor("modo", [128, 48], F32, kind="ExternalOutput").ap()
    with ExitStack() as st:
        K = KB(nc, st)
        xst = Rot([K.sb([128, TPC], F32, f"xst{i}") for i in range(2)])

        def xsrc(k):
            b = xst.next()
            K.dma("sp", b, b[:, 0:TPC // 2], None, xT[128 * k:128 * k + 128, 0:TPC // 2])
            K.dma("pool", b, b[:, TPC // 2:], None, xT[128 * k:128 * k + 128, TPC // 2:])
            return b
        mod = emit_front_core(K, nc, dram, N, kinds, rope, xsrc, outs)
        K.dma("pool", None, modo[:, :], mod, mod[:])
        K.finish()
    return nc


def pj(v, ncol):
    return np.ascontiguousarray(np.asarray(v, np.float32).reshape(ncol, 128).T)


def rope_tables():
    t = np.arange(SEQ)
    row = (t // 64).astype(np.float32)
    col = (t % 64).astype(np.float32)
    inv = (np.float32(10000.0) ** (-np.arange(32, dtype=np.float32) / np.float32(32))).astype(np.float32)
    ang = np.concatenate([row[:, None] * inv, col[:, None] * inv], axis=-1).astype(np.float32)
    cos, sin = np.cos(ang).astype(np.float32), np.sin(ang).astype(np.float32)
    cosT = np.repeat(cos, 2, axis=1).T
    sgn = np.tile(np.array([-1.0, 1.0], np.float32), 64)
    sinT = (np.repeat(sin, 2, axis=1) * sgn).T
    return np.ascontiguousarray(cosT), np.ascontiguousarray(sinT)


def swap_matrix():
    m = np.zeros((128, 128), np.float32)
    for i in range(64):
        m[2 * i + 1, 2 * i] = 1.0
        m[2 * i, 2 * i + 1] = 1.0
    return m


def front_common_inputs(c, ada_w, ada_b, norm_g, w_in, qn, kn):
    return {"cT": pj(c.reshape(-1), 16), "ada_bT": pj(ada_b.reshape(-1), 48), "gT": pj(norm_g.reshape(-1), 16),
            "ada_w": np.ascontiguousarray(ada_w), "w_in": np.ascontiguousarray(w_in),
            "qkg": np.ascontiguousarray(np.stack([qn.reshape(-1), kn.reshape(-1)], axis=1).astype(np.float32))}


def run_front(layer, xT, c, norm_g, ada_w, ada_b, w_in, q_norm, k_norm):
    rope = (layer == 0)
    nc = build_front(N_E if layer == 0 else N_O, KINDS_E if layer == 0 else KINDS_O, rope)
    base = front_common_inputs(c, ada_w[0], ada_b[0], norm_g[0], w_in[0], q_norm[0], k_norm[0])
    if rope:
        cosT, sinT = rope_tables()
        base["swapM"] = swap_matrix()
    maps = []
    for i in range(NCORES):
        sl = slice(i * TPC, (i + 1) * TPC)
        m = dict(base)
        m["xT"] = np.ascontiguousarray(xT[:, sl])
        if rope:
            m["cosT"] = np.ascontiguousarray(cosT[:, sl])
            m["sinT"] = np.ascontiguousarray(sinT[:, sl])
        maps.append(m)
    res = _run(nc, maps)
    obf = np.concatenate([r["obf"] for r in res], axis=1)
    of = np.concatenate([r["of"] for r in res], axis=1)
    return obf, of, res[0]["modo"]


def build_attn():
    nc = bass.Bass("TRN2", target_bir_lowering=False)
    qT = nc.dram_tensor("qT", [128, SEQ], BF16, kind="ExternalInput").ap()
    kT = nc.dram_tensor("kT", [128, SEQ], BF16, kind="ExternalInput").ap()
    vP = nc.dram_tensor("vP", [128, SEQ], BF16, kind="ExternalInput").ap()
    oT = nc.dram_tensor("oT", [128, SEQ], F32, kind="ExternalOutput").ap()
    scale = 128.0 ** -0.5
    with ExitStack() as st:
        K = KB(nc, st)
        q = K.sb([128, SEQ], BF16, "q")
        k = K.sb([128, SEQ], BF16, "k")
        v = K.sb([128, SEQ], BF16, "v")
        for h in range(4):
            sl = slice(h * 4096, (h + 1) * 4096)
            K.dma("sp", k, k[:, sl], None, kT[:, sl])
            K.dma("pool", q, q[:, sl], None, qT[:, sl])
            K.dma("sp", v, v[:, sl], None, vP[:, sl])
        ones = K.sb([128, 128], BF16, "ones")
        K.memset("pool", ones, ones[:], 1.0)
        pS = [K.ps([128, 512], F32, f"pS{i}") for i in range(3)]
        pO = [K.ps([128, 512], F32, f"pO{i}") for i in range(2)]
        pZ = [K.ps([128, 512], F32, f"pZ{i}") for i in range(2)]
        pT = Rot([K.sb([128, 512], BF16, f"pT{i}") for i in range(3)])
        rec = Rot([K.sb([128, 512], F32, f"rec{i}") for i in range(2)])
        osb = Rot([K.sb([128, 512], F32, f"osb{i}") for i in range(2)])
        NQ, NKB = SEQ // 512, SEQ // 128
        seq = [(a, b) for a in range(NQ) for b in range(NKB)]
        LOOK = 2

        def issue_S(idx):
            qt, kb = seq[idx]
            ps = pS[idx % 3]
            K.mm(ps, ps[:], k, k[:, kb * 128:(kb + 1) * 128], q, q[:, qt * 512:(qt + 1) * 512])

        for i in range(LOOK):
            issue_S(i)
        for idx, (qt, kb) in enumerate(seq):
            if idx + LOOK < len(seq):
                issue_S(idx + LOOK)
            ps = pS[idx % 3]
            p = pT.next()
            K.act(p, p[:], ps, ps[:], AF.Exp, scale=scale)
            o, z = pO[qt % 2], pZ[qt % 2]
            K.mm(o, o[:], v, v[:, kb * 128:(kb + 1) * 128], p, p[:], start=(kb == 0), stop=(kb == NKB - 1))
            K.mm(z, z[:], ones, ones[:], p, p[:], start=(kb == 0), stop=(kb == NKB - 1))
            if kb == NKB - 1:
                r = rec.next()
                K.op("dve", lambda e: e.reciprocal(out=r[:], in_=z[:]), [z], [r])
                ob = osb.next()
                K.tt("dve", ob, ob[:], o, o[:], r, r[:], ALU.mult)
                K.dma("pool", None, oT[:, qt * 512:(qt + 1) * 512], ob, ob[:])
        K.finish()
    return nc


def run_attn(obf):
    nc = build_attn()
    maps = []
    for i in range(NCORES):
        g = i // 4
        vT = obf[1280 + 128 * g:1280 + 128 * g + 128, :]
        vP = np.ascontiguousarray(vT.reshape(128, SEQ // 128, 128).transpose(2, 1, 0).reshape(128, SEQ))
        maps.append({"qT": np.ascontiguousarray(obf[128 * i:128 * i + 128, :]),
                     "kT": np.ascontiguousarray(obf[1024 + 128 * g:1024 + 128 * g + 128, :]),
                     "vP": vP})
    res = _run(nc, maps)
    return np.concatenate([r["oT"] for r in res], axis=0)


S5T = 32
S5NC = SEQ // S5T
TWO_PI = 2.0 * math.pi
S5_OFF = TWO_PI * 128


def s5_consts():
    s = np.arange(32, dtype=np.float32)
    et = np.concatenate([-s, 31 - s, s, s + 1, np.array([32.0], np.float32)]).astype(np.float32)
    ET = np.ascontiguousarray(np.broadcast_to(et, (64, 129)))
    mask = np.zeros((128, 4, 512), np.float32)
    for kb in range(4):
        for sl in range(8):
            sg = 8 * kb + sl
            for t in range(32):
                if sg <= t:
                    mask[sl * 16:(sl + 1) * 16, kb, t * 16:(t + 1) * 16] = 1.0
    ident = np.eye(64, dtype=np.float32)
    return ET, mask, ident


def s5_program(K, nc, U, PRM, ETd, MKd, IDd, Y, NPR, shared=False):
    cpe = "dve" if shared else "act"
    ET = K.sb([64, 129], F32, "ET"); K.dma("sp", ET, ET[:], None, ETd[:, :])
    MK = K.sb([128, 4, 512], F32, "MK"); K.dma("sp", MK, MK[:], None, MKd[:, :, :])
    ID = K.sb([64, 64], F32, "ID"); K.dma("sp", ID, ID[:], None, IDd[:, :])
    uR = Rot([K.sb([128, 4, S5NC], BF16, f"u{i}") for i in range(3)])
    prR = Rot([K.sb([64, 67], F32, f"prm{i}") for i in range(3)])
    sc = K.sb([64, 16], F32, "sc")
    mag = K.sb([64, 129], F32, "mag")
    ang = K.sb([64, 129], F32, "ang")
    ang2 = K.sb([64, 129], F32, "ang2")
    angi = K.sb([64, 129], mybir.dt.int32, "angi")
    PWr = K.sb([64, 129], F32, "PWr")
    PWi = K.sb([64, 129], F32, "PWi")
    bb = K.sb([64, 32], F32, "bb")
    t1 = K.sb([64, 512], F32, "t1")
    t2 = K.sb([64, 512], F32, "t2")
    BLr = K.sb([64, 512], F32, "BLr"); BLi = K.sb([64, 512], F32, "BLi")
    WSr = K.sb([64, 512], F32, "WSr"); WSi = K.sb([64, 512], F32, "WSi")
    CLr = K.sb([64, 512], F32, "CLr"); CLm = K.sb([64, 512], F32, "CLm")
    WOr = [K.sb([64, 512], BF16, f"WOr{i}") for i in range(2)]; WOm = [K.sb([64, 512], BF16, f"WOm{i}") for i in range(2)]
    Msb = [K.sb([128, 4, 512], BF16, f"Msb{i}") for i in range(2)]
    WSs = K.sb([128, 4, 128], BF16, "WSs")
    Pre = [K.sb([64, S5NC], F32, f"Pre{i}") for i in range(2)]
    Pim = [K.sb([64, S5NC], F32, f"Pim{i}") for i in range(2)]
    Hre = K.sb([64, S5NC], BF16, "Hre"); Him = K.sb([64, S5NC], BF16, "Him")
    K.memset("pool", Hre, Hre[:], 0.0); K.memset("pool", Him, Him[:], 0.0)
    Ac = [K.sb([64, 8], F32, f"Ac{i}") for i in range(2)]
    ysb = Rot([K.sb([128, S5NC], F32, f"ysb{i}") for i in range(2)])
    pM = K.ps([128, 512], F32, "pM")
    if shared:
        pW = K.ps([128, 512], F32, "pWE"); pEr = pW
    else:
        pW = K.ps([128, 128], F32, "pW")
        pEr = K.ps([64, 512], F32, "pEr")
    pEi = K.ps([64, 512], F32, "pEi")
    pY = [pM, pM] if shared else [K.ps([128, 512], F32, f"pY{i}") for i in range(2)]

    def v3(ap):
        return ap.rearrange("p (s c) -> p s c", c=16)

    def bs(buf, a):
        return buf[:, a:a + 32].unsqueeze(2).to_broadcast([64, 32, 16])

    def bc(buf, a):
        return buf[:, a:a + 16].unsqueeze(1).to_broadcast([64, 32, 16])

    def table(outr, outi, a, xb, xr, xi, neg_im):
        K.tt("dve", t1, v3(t1[:, :]), PWr, bs(PWr, a), xb, bc(xb, xr), ALU.mult)
        K.tt("dve", t2, v3(t2[:, :]), PWi, bs(PWi, a), xb, bc(xb, xi), ALU.mult)
        K.tt("dve", outr, v3(outr[:, :]), t1, v3(t1[:, :]), t2, v3(t2[:, :]), ALU.subtract)
        K.tt("dve", t1, v3(t1[:, :]), PWr, bs(PWr, a), xb, bc(xb, xi), ALU.mult)
        K.tt("dve", t2, v3(t2[:, :]), PWi, bs(PWi, a), xb, bc(xb, xr), ALU.mult)
        if neg_im:
            K.stt(outi, v3(outi[:, :]), t1, v3(t1[:, :]), -1.0, t2, v3(t2[:, :]), ALU.mult, ALU.subtract)
        else:
            K.tt("dve", outi, v3(outi[:, :]), t1, v3(t1[:, :]), t2, v3(t2[:, :]), ALU.add)

    ups = {}

    def load(pr):
        u = uR.next(); p = prR.next()
        K.dma("sp", u, u[:], None, U[pr].rearrange("k p j -> p k j"))
        K.dma("sp", p, p[:], None, PRM[pr])
        ups[pr] = (u, p)

    def P1a(pr):
        u, p = ups[pr]
        K.act(sc, sc[:, 0:1], p, p[:, 2:3], AF.Exp)
        K.tt("dve", sc, sc[:, 1:2], p, p[:, 0:1], sc, sc[:, 0:1], ALU.mult)
        K.tt("dve", sc, sc[:, 2:3], p, p[:, 1:2], sc, sc[:, 0:1], ALU.mult)
        K.act(mag, mag[:], ET, ET[:], AF.Exp, extra=[sc], scale=sc[:, 1:2])
        K.ts("dve", ang, ang[:], ET, ET[:], sc[:, 2:3], S5_OFF, ALU.mult, ALU.add, extra=[sc])
        K.ts("dve", ang2, ang2[:], ang, ang[:], 1.0 / TWO_PI, None, ALU.mult)
        K.copy("dve", angi, angi[:], ang2, ang2[:])
        K.copy("dve", ang2, ang2[:], angi, angi[:])
        K.stt(ang, ang[:], ang2, ang2[:], -TWO_PI, ang, ang[:], ALU.mult, ALU.add)
        K.act(PWi, PWi[:], ang, ang[:], AF.Sin, scale=0.5)
        K.act(ang2, ang2[:], ang, ang[:], AF.Sin, scale=0.25)
        K.tt("dve", ang2, ang2[:], ang2, ang2[:], ang2, ang2[:], ALU.mult)
        K.ts("dve", ang2, ang2[:], ang2, ang2[:], -2.0, 1.0, ALU.mult, ALU.add)
        K.tt("dve", PWr, PWr[:], PWi, PWi[:], PWi, PWi[:], ALU.mult)
        K.ts("dve", PWr, PWr[:], PWr, PWr[:], -2.0, 1.0, ALU.mult, ALU.add)
        K.stt(PWi, PWi[:], PWi, PWi[:], 2.0, ang2, ang2[:], ALU.mult, ALU.mult)
        K.tt("dve", PWr, PWr[:], PWr, PWr[:], mag, mag[:], ALU.mult)
        K.tt("dve", PWi, PWi[:], PWi, PWi[:], mag, mag[:], ALU.mult)
        K.ts("dve", sc, sc[:, 3:4], PWr, PWr[:, 65:66], 1.0, None, ALU.subtract)
        K.tt("dve", sc, sc[:, 4:5], p, p[:, 0:1], p, p[:, 0:1], ALU.mult)
        K.stt(sc, sc[:, 4:5], p, p[:, 1:2], p[:, 1:2], sc, sc[:, 4:5], ALU.mult, ALU.add)
        K.op("dve", lambda e: e.reciprocal(out=sc[:, 4:5], in_=sc[:, 4:5]), [sc], [sc])
        K.tt("dve", sc, sc[:, 5:6], sc, sc[:, 3:4], p, p[:, 0:1], ALU.mult)
        K.stt(sc, sc[:, 5:6], PWi, PWi[:, 65:66], p[:, 1:2], sc, sc[:, 5:6], ALU.mult, ALU.add, extra=[p])
        K.tt("dve", sc, sc[:, 5:6], sc, sc[:, 5:6], sc, sc[:, 4:5], ALU.mult)
        K.tt("dve", sc, sc[:, 6:7], sc, sc[:, 3:4], p, p[:, 1:2], ALU.mult)
        K.stt(sc, sc[:, 6:7], PWi, PWi[:, 65:66], p[:, 0:1], sc, sc[:, 6:7], ALU.mult, ALU.subtract, extra=[p])
        K.tt("dve", sc, sc[:, 6:7], sc, sc[:, 6:7], sc, sc[:, 4:5], ALU.mult)
        K.ts("dve", t1, t1[:, 0:16], p, p[:, 19:35], sc[:, 6:7], None, ALU.mult, extra=[sc])
        K.stt(bb, bb[:, 0:16], p, p[:, 3:19], sc[:, 5:6], t1, t1[:, 0:16], ALU.mult, ALU.subtract, extra=[sc])
        K.ts("dve", t1, t1[:, 0:16], p, p[:, 3:19], sc[:, 6:7], None, ALU.mult, extra=[sc])
        K.stt(bb, bb[:, 16:32], p, p[:, 19:35], sc[:, 5:6], t1, t1[:, 0:16], ALU.mult, ALU.add, extra=[sc])
        table(BLr, BLi, 0, bb, 0, 16, False)
        table(WSr, WSi, 32, bb, 0, 16, False)
        table(CLr, CLm, 64, p, 35, 51, True)
        table(WOr[pr % 2], WOm[pr % 2], 96, p, 35, 51, True)
        K.copy("dve", Ac[pr % 2], Ac[pr % 2][:, 0:1], PWr, PWr[:, 128:129])
        K.copy("dve", Ac[pr % 2], Ac[pr % 2][:, 1:2], PWi, PWi[:, 128:129])
        K.ts("dve", Ac[pr % 2], Ac[pr % 2][:, 2:3], PWi, PWi[:, 128:129], -1.0, None, ALU.mult)

    def P1b(pr):
        u, p = ups[pr]
        for kb in range(4):
            ks = slice(kb * 128, (kb + 1) * 128)
            K.mm(pM, pM[:], BLr, BLr[:, ks], CLr, CLr[:, :], start=True, stop=False)
            K.mm(pM, pM[:], BLi, BLi[:, ks], CLm, CLm[:, :], start=False, stop=True)
            K.tt("dve", Msb[pr % 2], Msb[pr % 2][:, kb, :], pM, pM[:], MK, MK[:, kb, :], ALU.mult)
            K.mm(pW, pW[:, 0:64], WSr, WSr[:, ks], ID, ID[:], start=True, stop=True)
            K.mm(pW, pW[:, 64:128], WSi, WSi[:, ks], ID, ID[:], start=True, stop=True)
            K.copy(cpe, WSs, WSs[:, kb, :], pW, pW[:, 0:128])

    def P2a(pr):
        u, p = ups[pr]
        for kb in range(4):
            K.mm(pEr, pEr[0:64, :], WSs, WSs[:, kb, 0:64], u, u[:, kb, :], start=(kb == 0), stop=(kb == 3))
        for kb in range(4):
            K.mm(pEi, pEi[:], WSs, WSs[:, kb, 64:128], u, u[:, kb, :], start=(kb == 0), stop=(kb == 3))
        K.copy(cpe, Pre[0], Pre[0][:], pEr, pEr[0:64, :])
        K.copy("dve", Pim[0], Pim[0][:], pEi, pEi[:])

    def P2b(pr):
        A_ = Ac[pr % 2]
        cur = 0
        d = 1
        while d < S5NC:
            re, im, nre, nim = Pre[cur], Pim[cur], Pre[1 - cur], Pim[1 - cur]
            n = S5NC
            K.stt(nre, nre[:, d:n], re, re[:, 0:n - d], A_[:, 0:1], re, re[:, d:n], ALU.mult, ALU.add, extra=[A_])
            K.stt(nre, nre[:, d:n], im, im[:, 0:n - d], A_[:, 2:3], nre, nre[:, d:n], ALU.mult, ALU.add, extra=[A_])
            K.stt(nim, nim[:, d:n], im, im[:, 0:n - d], A_[:, 0:1], im, im[:, d:n], ALU.mult, ALU.add, extra=[A_])
            K.stt(nim, nim[:, d:n], re, re[:, 0:n - d], A_[:, 1:2], nim, nim[:, d:n], ALU.mult, ALU.add, extra=[A_])
            K.copy(cpe, nre, nre[:, 0:d], re, re[:, 0:d])
            K.copy(cpe, nim, nim[:, 0:d], im, im[:, 0:d])
            cur = 1 - cur
            d *= 2
            if d < S5NC:
                K.tt("dve", A_, A_[:, 3:4], A_, A_[:, 0:1], A_, A_[:, 0:1], ALU.mult)
                K.stt(A_, A_[:, 3:4], A_, A_[:, 1:2], A_[:, 2:3], A_, A_[:, 3:4], ALU.mult, ALU.add)
                K.stt(A_, A_[:, 4:5], A_, A_[:, 0:1], 2.0, A_, A_[:, 1:2], ALU.mult, ALU.mult)
                K.copy("dve", A_, A_[:, 0:1], A_, A_[:, 3:4])
                K.copy("dve", A_, A_[:, 1:2], A_, A_[:, 4:5])
                K.ts("dve", A_, A_[:, 2:3], A_, A_[:, 4:5], -1.0, None, ALU.mult)
        return cur

    def P2c(pr, cur):
        u, p = ups[pr]
        K.copy(cpe, Hre, Hre[:, 1:S5NC], Pre[cur], Pre[cur][:, 0:S5NC - 1])
        K.copy("dve", Him, Him[:, 1:S5NC], Pim[cur], Pim[cur][:, 0:S5NC - 1])
        for tb in range(4):
            tsl = slice(tb * 128, (tb + 1) * 128)
            py = pY[tb % 2]
            for kb in range(tb + 1):
                K.mm(py, py[:], Msb[pr % 2], Msb[pr % 2][:, kb, tsl], u, u[:, kb, :], start=(kb == 0), stop=False)
            K.mm(py, py[:], WOr[pr % 2], WOr[pr % 2][:, tsl], Hre, Hre[:], start=False, stop=False)
            K.mm(py, py[:], WOm[pr % 2], WOm[pr % 2][:, tsl], Him, Him[:], start=False, stop=True)
            yb = ysb.next()
            K.copy(cpe if tb % 2 == 0 else "dve", yb, yb[:], py, py[:])
            K.dma("pool", None, Y[pr, tb], yb, yb[:])

    return load, P1a, P1b, P2a, P2b, P2c


def build_s5(NPR=16):
    nc = bass.Bass("TRN2", target_bir_lowering=False)
    U = nc.dram_tensor("U", [NPR, 4, 128, S5NC], BF16, kind="ExternalInput").ap()
    PRM = nc.dram_tensor("PRM", [NPR, 64, 67], F32, kind="ExternalInput").ap()
    ETd = nc.dram_tensor("ET", [64, 129], F32, kind="ExternalInput").ap()
    MKd = nc.dram_tensor("MK", [128, 4, 512], F32, kind="ExternalInput").ap()
    IDd = nc.dram_tensor("ID", [64, 64], F32, kind="ExternalInput").ap()
    Y = nc.dram_tensor("Y", [NPR, 4, 128, S5NC], F32, kind="ExternalOutput").ap()
    with ExitStack() as st:
        K = KB(nc, st)
        load, P1a, P1b, P2a, P2b, P2c = s5_program(K, nc, U, PRM, ETd, MKd, IDd, Y, NPR)
        load(0)
        if NPR > 1:
            load(1)
        P1a(0); P1b(0)
        for pr in range(NPR):
            if pr + 2 < NPR:
                load(pr + 2)
            P2a(pr)
            if pr + 1 < NPR:
                P1a(pr + 1)
            cur_ = P2b(pr)
            if pr + 1 < NPR:
                P1b(pr + 1)
            P2c(pr, cur_)
        K.finish()
    return nc


def s5_pack_u(uT):
    outs = []
    for i in range(NCORES):
        U = np.empty((16, 4, 128, S5NC), uT.dtype)
        for gl in range(8):
            g = 8 * i + gl
            ug = uT[16 * g:16 * g + 16, :]
            for d in range(2):
                x = ug[:, ::-1] if d == 1 else ug
                x = x.reshape(16, S5NC, 32).transpose(2, 0, 1)
                U[2 * gl + d] = x.reshape(4, 128, S5NC)
        outs.append(U)
    return outs


def s5_unpack_y(Ys):
    yf = np.empty((1024, SEQ), np.float32); yb = np.empty((1024, SEQ), np.float32)
    for i in range(NCORES):
        for gl in range(8):
            g = 8 * i + gl
            for d in range(2):
                y = Ys[i][2 * gl + d].reshape(32, 16, S5NC).transpose(1, 2, 0).reshape(16, SEQ)
                if d == 0:
                    yf[16 * g:16 * g + 16] = y
                else:
                    yb[16 * g:16 * g + 16] = y[:, ::-1]
    return yf, yb


def s5_params(lam_re, lam_im, log_step, b_re, b_im, c_re, c_im):
    outs = []
    for i in range(NCORES):
        P = np.empty((16, 64, 67), np.float32)
        for gl in range(8):
            g = 8 * i + gl
            for d in range(2):
                pr = 2 * gl + d
                P[pr, :, 0] = lam_re[d, g]; P[pr, :, 1] = lam_im[d, g]; P[pr, :, 2] = log_step[d, g]
                P[pr, :, 3:19] = b_re[d, g]; P[pr, :, 19:35] = b_im[d, g]
                P[pr, :, 35:51] = c_re[d, g].T; P[pr, :, 51:67] = c_im[d, g].T
        outs.append(P)
    return outs


def run_s5(uT, lam_re, lam_im, log_step, b_re, b_im, c_re, c_im):
    nc = build_s5()
    ET, MK, ID = s5_consts()
    Us = s5_pack_u(uT)
    Ps = s5_params(lam_re, lam_im, log_step, b_re, b_im, c_re, c_im)
    maps = [{"U": Us[i], "PRM": Ps[i], "ET": ET, "MK": MK, "ID": ID} for i in range(NCORES)]
    res = _run(nc, maps)
    return s5_unpack_y([r["Y"] for r in res])


def build_post(layer, T=TPC):
    nc = bass.Bass("TRN2", target_bir_lowering=False)
    def di(name, shape, dt=F32):
        return nc.dram_tensor(name, shape, dt, kind="ExternalInput").ap()
    xT = di("xT", [D, T]); w_out = di("w_out", [D, D]); modd = di("mod", [128, 48])
    aT = di("aT", [1024, T]); agT = di("agT", [1024, T])
    if layer == 0:
        yfT = di("yfT", [1024, T]); ybT = di("ybT", [1024, T]); uT = di("uT", [1024, T], BF16); gbT = di("gbT", [1024, T])
        s5d = di("s5d", [128, 8]); w_glu = di("w_glu", [1024, 2048]); bglu = di("bglu", [128, 16])
    else:
        ysT = di("ysT", [1024, T]); ys2T = di("ys2T", [1024, T]); zT = di("zT", [1024, T]); nw = di("nw", [128, 8])
    oT = nc.dram_tensor("oT", [D, T], F32, kind="ExternalOutput").ap()
    NT = T // 512
    with ExitStack() as st:
        K = KB(nc, st)
        mod = K.sb([128, 48], F32, "mod"); K.dma("sp", mod, mod[:], None, modd[:, :])
        wo = K.sb([128, 16, D], BF16, "wo")
        stg = Rot([K.sb([128, D], F32, f"stg{i}") for i in range(2)])
        cast_i = [0]

        def load_w(dst, w_ap, nk):
            for k in range(nk):
                s_ = stg.next()
                K.dma("sp" if k % 2 == 0 else "pool", s_, s_[:], None, w_ap[128 * k:128 * k + 128, :])
                e = ("dve", "act")[cast_i[0] % 2]; cast_i[0] += 1
                K.copy(e, dst, dst[:, k, :], s_, s_[:])
        if layer == 0:
            wg = K.sb([128, 8, 2048], BF16, "wg")
            load_w(wg, w_glu, 8)
            sd = K.sb([128, 8], F32, "sd"); K.dma("sp", sd, sd[:], None, s5d[:, :])
            bg = K.sb([128, 16], F32, "bg"); K.dma("sp", bg, bg[:], None, bglu[:, :])
        else:
            nws = K.sb([128, 8], F32, "nws"); K.dma("sp", nws, nws[:], None, nw[:, :])
            ones = K.sb([128, 128], F32, "ones"); K.memset("pool", ones, ones[:], 1.0)
        load_w(wo, w_out, 16)
        A = Rot([K.sb([128, 16, 512], BF16, f"A{i}") for i in range(2)])
        ld = {nm: Rot([K.sb([128, 512], F32, f"ld_{nm}{i}") for i in range(3)]) for nm in ("a", "b", "c", "x")}
        ldu = Rot([K.sb([128, 512], BF16, f"ldu{i}") for i in range(2)])
        tmp = {nm: Rot([K.sb([128, 512], F32, f"tm_{nm}{i}") for i in range(2)]) for nm in ("p", "q", "r")}
        osb = Rot([K.sb([128, 512], F32, f"osb{i}") for i in range(3)])
        pacc = Rot([K.ps([128, 512], F32, f"pa{i}") for i in range(4)])
        pss = K.ps([128, 512], F32, "pss")
        if layer == 0:
            gy = Rot([K.sb([128, 8, 512], BF16, f"gy{i}") for i in range(2)])
        else:
            yz = Rot([K.sb([128, 8, 512], F32, f"yz{i}") for i in range(2)])
            rs = K.sb([128, 512], F32, "rs")

        def load(nm, ap, q="sp"):
            b = ld[nm].next()
            K.dma(q, b, b[:], None, ap)
            return b

        for t in range(NT):
            ts_ = slice(512 * t, 512 * t + 512)
            Ab = A.next()
            if layer == 0:
                g = gy.next()
                for k in range(8):
                    rows = slice(128 * k, 128 * k + 128)
                    yf = load("a", yfT[rows, ts_]); yb = load("b", ybT[rows, ts_], "pool")
                    ub = ldu.next(); K.dma("sp", ub, ub[:], None, uT[rows, ts_])
                    y = tmp["p"].next()
                    K.stt(y, y[:], ub, ub[:], sd[:, k:k + 1], yf, yf[:], ALU.mult, ALU.add, extra=[sd])
                    K.tt("dve", y, y[:], y, y[:], yb, yb[:], ALU.add)
                    q_ = tmp["q"].next()
                    K.act(q_, q_[:], y, y[:], AF.Square)
                    K.ts("dve", q_, q_[:], q_, q_[:], 0.044715, 1.0, ALU.mult, ALU.add)
                    K.tt("dve", q_, q_[:], q_, q_[:], y, y[:], ALU.mult)
                    K.act(q_, q_[:], q_, q_[:], AF.Sigmoid, scale=1.5957691216057308)
                    K.tt("dve", g, g[:, k, :], q_, q_[:], y, y[:], ALU.mult)
                for n in range(8):
                    pv, pg = pacc.next(), pacc.next()
                    for k in range(8):
                        K.mm(pv, pv[:], wg, wg[:, k, 128 * n:128 * n + 128], g, g[:, k, :], start=(k == 0), stop=(k == 7))
                    for k in range(8):
                        K.mm(pg, pg[:], wg, wg[:, k, 1024 + 128 * n:1024 + 128 * n + 128], g, g[:, k, :], start=(k == 0), stop=(k == 7))
                    sg = tmp["r"].next()
                    K.act(sg, sg[:], pg, pg[:], AF.Sigmoid, extra=[bg], bias=bg[:, 8 + n:9 + n])
                    gb = load("c", gbT[128 * n:128 * n + 128, ts_])
                    v1 = tmp["p"].next()
                    K.stt(v1, v1[:], pv, pv[:], bg[:, n:n + 1], sg, sg[:], ALU.add, ALU.mult, extra=[bg])
                    K.tt("dve", Ab, Ab[:, 8 + n, :], v1, v1[:], gb, gb[:], ALU.mult)
            else:
                yzb = yz.next()
                for k in range(8):
                    rows = slice(128 * k, 128 * k + 128)
                    ys = load("a", ysT[rows, ts_]); z = load("b", zT[rows, ts_], "pool"); y2 = load("c", ys2T[rows, ts_])
                    K.tt("dve", ys, ys[:], ys, ys[:], y2, y2[:], ALU.add)
                    K.tt("dve", yzb, yzb[:, k, :], ys, ys[:], z, z[:], ALU.mult)
                    sq = tmp["q"].next()
                    K.act(sq, sq[:], yzb, yzb[:, k, :], AF.Square)
                    K.mm(pss, pss[:], ones, ones[:], sq, sq[:], start=(k == 0), stop=(k == 7))
                emit_rstd(K, rs, rs[:], pss, pss[:], 1.0 / 1024)
                for k in range(8):
                    K.stt(Ab, Ab[:, 8 + k, :], yzb, yzb[:, k, :], nws[:, k:k + 1], rs, rs[:], ALU.mult, ALU.mult, extra=[nws])
            for n in range(8):
                a = load("a", aT[128 * n:128 * n + 128, ts_]); ag = load("b", agT[128 * n:128 * n + 128, ts_], "pool")
                K.tt("dve", Ab, Ab[:, n, :], a, a[:], ag, ag[:], ALU.mult)
            for n in range(16):
                po = pacc.next()
                for k in range(16):
                    K.mm(po, po[:], wo, wo[:, k, 128 * n:128 * n + 128], Ab, Ab[:, k, :], start=(k == 0), stop=(k == 15))
                xb = load("x", xT[128 * n:128 * n + 128, ts_])
                ob = osb.next()
                K.stt(ob, ob[:], po, po[:], mod[:, 32 + n:33 + n], xb, xb[:], ALU.mult, ALU.add, extra=[mod])
                K.dma("pool", None, oT[128 * n:128 * n + 128, ts_], ob, ob[:])
        K.finish()
    return nc


def run_post(layer, xT, mod, w_out, aT, agT, **kw):
    nc = build_post(layer)
    maps = []
    for i in range(NCORES):
        sl = slice(i * TPC, (i + 1) * TPC)
        m = {"xT": np.ascontiguousarray(xT[:, sl]), "w_out": np.ascontiguousarray(w_out), "mod": np.ascontiguousarray(mod),
             "aT": np.ascontiguousarray(aT[:, sl]), "agT": np.ascontiguousarray(agT[:, sl])}
        if layer == 0:
            for nm in ("yfT", "ybT", "uT", "gbT"):
                m[nm] = np.ascontiguousarray(kw[nm][:, sl])
            m["s5d"] = pj(kw["s5_d"], 8); m["w_glu"] = np.ascontiguousarray(kw["w_glu"]); m["bglu"] = pj(kw["b_glu"], 16)
        else:
            for nm in ("ysT", "ys2T", "zT"):
                m[nm] = np.ascontiguousarray(kw[nm][:, sl])
            m["nw"] = pj(kw["norm_w"], 8)
        maps.append(m)
    res = _run(nc, maps)
    return np.concatenate([r["oT"] for r in res], axis=1)


GW = 64
NROW = SEQ // GW


def na_bias_tables(rpb_h):
    col = np.arange(GW)
    cs = np.clip(col - 8, 0, GW - 16)
    cmask = (col[None, :] >= cs[:, None]) & (col[None, :] < cs[:, None] + 16)
    dc = np.clip(col[None, :] - col[:, None], -15, 15) + 15
    out = np.empty((8, GW, 8, GW), np.float32)
    for v in range(8):
        off = 7 - v
        for kr in range(8):
            b = rpb_h[off + kr][dc]
            b = np.where(cmask, b, np.float32(-1e30)).astype(np.float32)
            out[v, :, kr, :] = b.T
    return out


def build_na():
    nc = bass.Bass("TRN2", target_bir_lowering=False)
    qT = nc.dram_tensor("qT", [128, SEQ], BF16, kind="ExternalInput").ap()
    kT = nc.dram_tensor("kT", [128, SEQ], BF16, kind="ExternalInput").ap()
    vR = nc.dram_tensor("vR", [64, NROW * 128], BF16, kind="ExternalInput").ap()
    bT = nc.dram_tensor("bT", [64, 8, 512], F32, kind="ExternalInput").ap()
    oT = nc.dram_tensor("oT", [128, SEQ], F32, kind="ExternalOutput").ap()
    scale = 128.0 ** -0.5
    with ExitStack() as st:
        K = KB(nc, st)
        q = K.sb([128, SEQ], BF16, "q"); k = K.sb([128, SEQ], BF16, "k"); v = K.sb([64, NROW * 128], BF16, "v")
        bias = K.sb([64, 8, 512], F32, "bias")
        for h in range(4):
            sl = slice(h * 4096, (h + 1) * 4096)
            K.dma("sp", k, k[:, sl], None, kT[:, sl])
            K.dma("pool", q, q[:, sl], None, qT[:, sl])
        for h in range(4):
            sl = slice(h * 8192, (h + 1) * 8192)
            K.dma("sp", v, v[:, sl], None, vR[:, sl])
        K.dma("sp", bias, bias[:], None, bT[:, :, :])
        ones = K.sb([64, 128], BF16, "ones"); K.memset("pool", ones, ones[:], 1.0)
        pS = Rot([K.ps([64, 512], F32, f"pS{i}") for i in range(3)])
        pO = Rot([K.ps([128, 512], F32, f"pO{i}") for i in range(2)])
        pZ = Rot([K.ps([128, 512], F32, f"pZ{i}") for i in range(2)])
        tS = Rot([K.sb([64, 512], F32, f"tS{i}") for i in range(3)])
        pT = Rot([K.sb([64, 512], BF16, f"pT{i}") for i in range(3)])
        rec = Rot([K.sb([128, 512], F32, f"rec{i}") for i in range(2)])
        osb = Rot([K.sb([128, 512], F32, f"osb{i}") for i in range(2)])
        def row_info(r):
            rs = min(max(r - 4, 0), NROW - 8)
            var = r if r < 4 else (4 if r <= 252 else 4 + (r - 252))
            return rs, var

        pS_l = pS.items

        def issue_S(r):
            rs, _ = row_info(r)
            ps = pS_l[r % 3]
            qs = slice(r * 64, r * 64 + 64)
            for j in range(8):
                K.mm(ps, ps[:, j * 64:(j + 1) * 64], k, k[:, (rs + j) * 64:(rs + j) * 64 + 64], q, q[:, qs])

        LOOK = 2
        for r in range(LOOK):
            issue_S(r)
        o = z = None
        for r in range(NROW):
            rr = r % 8
            if rr == 0:
                o, z = pO.next(), pZ.next()
            if r + LOOK < NROW:
                issue_S(r + LOOK)
            rs, var = row_info(r)
            ps = pS_l[r % 3]
            t = tS.next()
            K.stt(t, t[:], ps, ps[:], scale, bias, bias[:, var, :], ALU.mult, ALU.add)
            p = pT.next()
            K.act(p, p[:], t, t[:], AF.Exp)
            cs_ = slice(rr * 64, rr * 64 + 64)
            for j in range(8):
                K.mm(o, o[:, cs_], v, v[:, (rs + j) * 128:(rs + j) * 128 + 128], p, p[:, j * 64:(j + 1) * 64], start=(j == 0), stop=(j == 7))
            for j in range(8):
                K.mm(z, z[:, cs_], ones, ones[:], p, p[:, j * 64:(j + 1) * 64], start=(j == 0), stop=(j == 7))
            if rr == 7:
                r0 = r - 7
                rc = rec.next()
                K.op("dve", lambda e: e.reciprocal(out=rc[:], in_=z[:]), [z], [rc])
                ob = osb.next()
                K.tt("dve", ob, ob[:], o, o[:], rc, rc[:], ALU.mult)
                K.dma("pool", None, oT[:, r0 * 64:r0 * 64 + 512], ob, ob[:])
        K.finish()
    return nc


def run_na(obf, rpb):
    nc = build_na()
    maps = []
    for i in range(NCORES):
        vT = obf[2048 + 128 * i:2048 + 128 * i + 128, :]
        vR = np.ascontiguousarray(vT.reshape(128, NROW, 64).transpose(2, 1, 0).reshape(64, NROW * 128))
        bt = na_bias_tables(np.asarray(rpb[i], np.float32))
        bT = np.ascontiguousarray(bt.transpose(1, 0, 2, 3).reshape(64, 8, 512))
        maps.append({"qT": np.ascontiguousarray(obf[128 * i:128 * i + 128, :]),
                     "kT": np.ascontiguousarray(obf[1024 + 128 * i:1024 + 128 * i + 128, :]), "vR": vR, "bT": bT})
    res = _run(nc, maps)
    return np.concatenate([r["oT"] for r in res], axis=0)


NCH = SEQ // 128


def build_ssd(nchunks=NCH, ndirs=2):
    nc = bass.Bass("TRN2", target_bir_lowering=False)
    XBC = nc.dram_tensor("XBC", [2, 3, 128, SEQ + 4], F32, kind="ExternalInput").ap()
    CW = nc.dram_tensor("CW", [2, 3, 128, 6], F32, kind="ExternalInput").ap()
    DTR = nc.dram_tensor("DTR", [2, 2, 128, NCH], F32, kind="ExternalInput").ap()
    SCd = nc.dram_tensor("SC", [2, 128, 6], F32, kind="ExternalInput").ap()
    TRI = nc.dram_tensor("TRI", [128, 128], F32, kind="ExternalInput").ap()
    SUd = nc.dram_tensor("SU", [128, 128], F32, kind="ExternalInput").ap()
    IDd = nc.dram_tensor("ID", [128, 128], BF16, kind="ExternalInput").ap()
    Y = nc.dram_tensor("Y", [2, 128, NCH, 128], F32, kind="ExternalOutput").ap()
    PIECE = 2048
    with ExitStack() as st:
        K = KB(nc, st)
        tri = K.sb([128, 128], F32, "tri"); K.dma("sp", tri, tri[:], None, TRI[:, :])
        su = K.sb([128, 128], F32, "su"); K.dma("sp", su, su[:], None, SUd[:, :])
        idb = K.sb([128, 128], BF16, "idb"); K.dma("sp", idb, idb[:], None, IDd[:, :])
        ones = K.sb([128, 128], F32, "ones"); K.memset("pool", ones, ones[:], 1.0)
        stage = Rot([K.sb([128, PIECE + 4], F32, f"stage{i}") for i in range(2)])
        acc = K.sb([128, PIECE], F32, "acc")
        xfm = K.sb([128, PIECE], BF16, "xfm")
        BT = K.sb([128, SEQ], BF16, "BT"); CT = K.sb([128, SEQ], BF16, "CT")
        Xt = K.sb([128, NCH, 128], BF16, "Xt"); Bt = K.sb([128, NCH, 128], BF16, "Bt")
        cw = K.sb([128, 3, 6], F32, "cw")
        scs = K.sb([128, 6], F32, "scs")
        dtr = K.sb([128, 2, NCH], F32, "dtr")
        dt = K.sb([128, 2, NCH], F32, "dt"); adt = K.sb([128, 2, NCH], F32, "adt")
        cum = K.sb([128, 2, NCH], F32, "cum"); ecum = K.sb([128, 2, NCH], F32, "ecum")
        dts = K.sb([128, 2, NCH], F32, "dts"); etot = K.sb([128, 2, NCH], F32, "etot")
        na = K.sb([128, 2], F32, "na")
        stT = [K.sb([128, 64], F32, f"stT{h}") for h in range(2)]
        stB = [[K.sb([128, 64], BF16, f"stB{h}_{i}") for i in range(2)] for h in range(2)]
        pSC = K.ps([128, 512], F32, "pSC")
        pSEG = [K.ps([128, 512], F32, f"pSEG{h}") for h in range(2)]
        pYD = [K.ps([128, 512], F32, f"pYD{i}") for i in range(2)]
        pYO = K.ps([128, 512], F32, "pYO")
        pST = [K.ps([128, 512], F32, f"pST{i}") for i in range(2)]

        class _PA:
            def __init__(self):
                self.i = 0

            def next(self):
                self.i += 1
                return pSC if self.i % 2 else pYO
        pA = _PA()
        scm = Rot([K.sb([128, 512], F32, f"scm{i}") for i in range(2)])
        lseg = Rot([K.sb([128, 512], F32, f"lseg{i}") for i in range(4)])
        Lm = Rot([K.sb([128, 512], F32, f"Lm{i}") for i in range(4)])
        Gt = Rot([K.sb([128, 512], BF16, f"Gt{i}") for i in range(4)])
        xd = Rot([K.sb([128, 4, 64], BF16, f"xd{i}") for i in range(4)])
        xdd = Rot([K.sb([128, 4, 64], BF16, f"xdd{i}") for i in range(4)])
        ysb = Rot([K.sb([128, 512], F32, f"ysb{i}") for i in range(3)])
        ws = K.sb([128, 2], F32, "ws")
        for d in range(ndirs):
            K.dma("sp", cw, cw[:], None, CW[d].rearrange("g p t -> p g t"))
            K.dma("sp", scs, scs[:], None, SCd[d])
            K.dma("sp", dtr, dtr[:], None, DTR[d].rearrange("h p c -> p h c"))
            import os
            for g in range(0 if os.environ.get('SKIP_CONV') else 3):
                for pc in range(SEQ // PIECE):
                    sg = stage.next()
                    K.dma("sp" if pc % 2 == 0 else "pool", sg, sg[:], None, XBC[d, g, :, pc * PIECE:pc * PIECE + PIECE + 4])
                    K.ts("dve", acc, acc[:], sg, sg[:, 0:PIECE], cw[:, g, 0:1], None, ALU.mult, extra=[cw])
                    for j in range(1, 5):
                        K.stt(acc, acc[:], sg, sg[:, j:j + PIECE], cw[:, g, j:j + 1], acc, acc[:], ALU.mult, ALU.add, extra=[cw])
                    dst, dbuf = ((xfm[:, :], xfm), (BT[:, pc * PIECE:(pc + 1) * PIECE], BT), (CT[:, pc * PIECE:(pc + 1) * PIECE], CT))[g]
                    K.act(dbuf, dst, acc, acc[:], AF.Silu, extra=[cw], bias=cw[:, g, 5:6])
                    if g < 2:
                        for cc in range(PIECE // 128):
                            c = pc * (PIECE // 128) + cc
                            pa = pA.next()
                            src = xfm[:, cc * 128:(cc + 1) * 128] if g == 0 else BT[:, c * 128:(c + 1) * 128]
                            K.mm(pa, pa[:, 0:128], dbuf, src, idb, idb[:])
                            tgt = Xt if g == 0 else Bt
                            K.copy("act" if cc % 2 == 0 else "pool" if False else "dve", tgt, tgt[:, c, :], pa, pa[:, 0:128])
            import os
            for h in range(0 if os.environ.get('SKIP_DT') else 2):
                K.ts("dve", cum, cum[:, h, :], dtr, dtr[:, h, :], scs[:, h:h + 1], None, ALU.add, extra=[scs])
                K.act(ecum, ecum[:, h, :], cum, cum[:, h, :], AF.Exp)
                K.ts("dve", ecum, ecum[:, h, :], ecum, ecum[:, h, :], 1.0, None, ALU.add)
                K.ts("dve", dt, dt[:, h, :], cum, cum[:, h, :], 0.0, 0.3, ALU.max, ALU.add)
                for _it in range(6):
                    K.act(dts, dts[:, h, :], dt, dt[:, h, :], AF.Exp, scale=-1.0)
                    K.tt("dve", dts, dts[:, h, :], dts, dts[:, h, :], ecum, ecum[:, h, :], ALU.mult)
                    K.stt(dt, dt[:, h, :], dts, dts[:, h, :], -1.0, dt, dt[:, h, :], ALU.add, ALU.add)
                K.act(na, na[:, h:h + 1], scs, scs[:, 2 + h:3 + h], AF.Exp)
                K.ts("dve", na, na[:, h:h + 1], na, na[:, h:h + 1], -1.0, None, ALU.mult)
                K.ts("dve", adt, adt[:, h, :], dt, dt[:, h, :], na[:, h:h + 1], None, ALU.mult, extra=[na])
                pa = pA.next()
                K.mm(pa, pa[:, 0:128], tri, tri[:], adt, adt[:, h, :])
                K.copy("dve", cum, cum[:, h, :], pa, pa[:, 0:128])
                K.act(ecum, ecum[:, h, :], cum, cum[:, h, :], AF.Exp)
                pb = pA.next()
                K.mm(pb, pb[:, 0:128], ones, ones[:], adt, adt[:, h, :])
                K.copy("dve", dts, dts[:, h, :], pb, pb[:, 0:128])
                K.act(etot, etot[:, h, :], dts, dts[:, h, :], AF.Exp)
                K.tt("dve", dts, dts[:, h, :], dts, dts[:, h, :], cum, cum[:, h, :], ALU.subtract)
                K.act(dts, dts[:, h, :], dts, dts[:, h, :], AF.Exp)
                K.tt("dve", dts, dts[:, h, :], dts, dts[:, h, :], dt, dt[:, h, :], ALU.mult)
                K.memset("pool", stT[h], stT[h][:], 0.0)
                K.memset("pool", stB[h][0], stB[h][0][:], 0.0)
                K.memset("pool", stB[h][1], stB[h][1][:], 0.0)
            G = 4
            NG = nchunks // G

            def g4(ap, inner):
                return ap.rearrange("p (g i) -> p g i", g=G)

            def indep(k):
                c0 = k * G
                for g in range(G):
                    c = c0 + g
                    K.mm(pSC, pSC[:, g * 128:(g + 1) * 128], BT, BT[:, c * 128:(c + 1) * 128], CT, CT[:, c * 128:(c + 1) * 128])
                sm = scm.next()
                K.tt("dve", sm, g4(sm[:, :], 128), pSC, g4(pSC[:, :], 128), tri, tri[:, :].unsqueeze(1).to_broadcast([128, G, 128]), ALU.mult)
                pyd, pst = pYD[k % 2], pST[k % 2]
                for h in range(2):
                    hs = slice(64 * h, 64 * h + 64)
                    lsb = lseg.next()
                    K.tt("dve", lsb, g4(lsb[:, :], 128), su, su[:, :].unsqueeze(1).to_broadcast([128, G, 128]),
                         adt, adt[:, h, c0:c0 + G].unsqueeze(2).to_broadcast([128, G, 128]), ALU.mult)
                    for g in range(G):
                        K.mm(pSEG[h], pSEG[h][:, g * 128:(g + 1) * 128], lsb, lsb[:, g * 128:(g + 1) * 128], tri, tri[:])
                    L = Lm.next()
                    K.act(L, L[:], pSEG[h], pSEG[h][:], AF.Exp)
                    Gb = Gt.next()
                    K.tt("dve", Gb, Gb[:], L, L[:], sm, sm[:], ALU.mult)
                    x1 = xd.next()
                    K.tt("dve", x1, x1[:], Xt, Xt[:, c0:c0 + G, hs], dt, dt[:, h, c0:c0 + G].unsqueeze(2).to_broadcast([128, G, 64]), ALU.mult)
                    x2 = xdd.next()
                    K.tt("dve", x2, x2[:], Xt, Xt[:, c0:c0 + G, hs], dts, dts[:, h, c0:c0 + G].unsqueeze(2).to_broadcast([128, G, 64]), ALU.mult)
                    for g in range(G):
                        K.mm(pyd, pyd[:, g * 128 + 64 * h:g * 128 + 64 * h + 64], Gb, Gb[:, g * 128:(g + 1) * 128], x1, x1[:, g, :])
                    for g in range(G):
                        K.mm(pst, pst[:, g * 128 + 64 * h:g * 128 + 64 * h + 64], Bt, Bt[:, c0 + g, :], x2, x2[:, g, :])

            def dep(k):
                c0 = k * G
                pyd, pst = pYD[k % 2], pST[k % 2]
                for g in range(G):
                    c = c0 + g
                    for h in range(2):
                        col = slice(g * 128 + 64 * h, g * 128 + 64 * h + 64)
                        sb_old = stB[h][stp[h] % 2]
                        sb_new = stB[h][(stp[h] + 1) % 2]
                        K.mm(pYO, pYO[:, col], CT, CT[:, c * 128:(c + 1) * 128], sb_old, sb_old[:])
                        K.stt(sb_new, sb_new[:], stT[h], stT[h][:], etot[:, h, c:c + 1], pst, pst[:, col], ALU.mult, ALU.add, extra=[etot])
                        K.stt(stT[h], stT[h][:], stT[h], stT[h][:], etot[:, h, c:c + 1], pst, pst[:, col], ALU.mult, ALU.add, extra=[etot])
                        stp[h] += 1
                yb = ysb.next()
                K.tt("dve", yb, yb[:, :].rearrange("p (g h j) -> p g h j", g=G, h=2), pYO, pYO[:, :].rearrange("p (g h j) -> p g h j", g=G, h=2),
                     ecum, ecum[:, :, c0:c0 + G].rearrange("p h g -> p g h").unsqueeze(3).to_broadcast([128, G, 2, 64]), ALU.mult)
                K.tt("dve", yb, yb[:], yb, yb[:], pyd, pyd[:], ALU.add)
                if d == 0:
                    for h in range(2):
                        hs = slice(64 * h, 64 * h + 64)
                        K.stt(yb, g4(yb[:, :], 128)[:, :, hs], Xt, Xt[:, c0:c0 + G, hs], scs[:, 4 + h:5 + h], yb, g4(yb[:, :], 128)[:, :, hs],
                              ALU.mult, ALU.add, extra=[scs])
                K.dma("pool", None, Y[d, :, c0:c0 + G, :], yb, g4(yb[:, :], 128))

            stp = [0, 0]
            if NG > 0:
                indep(0)
            for k in range(NG):
                if k + 1 < NG:
                    indep(k + 1)
                dep(k)
        K.finish()
    return nc


def run_ssd(of1, conv_w, conv_b, dt_bias, a_log, ssd_d):
    nc = build_ssd()
    xbc = of1[2048:3584]
    dtr = of1[3584:3616]
    tri = np.triu(np.ones((128, 128), np.float32))
    su = np.tril(np.ones((128, 128), np.float32), -1)
    idm = np.eye(128, dtype=np.float32).astype(NPBF)
    maps = []
    for i in range(NCORES):
        g = i // 4
        rows = [slice(128 * i, 128 * i + 128), slice(1024 + 128 * g, 1024 + 128 * g + 128), slice(1280 + 128 * g, 1280 + 128 * g + 128)]
        XBC = np.zeros((2, 3, 128, SEQ + 4), np.float32)
        CWm = np.empty((2, 3, 128, 6), np.float32)
        DTR = np.empty((2, 2, 128, NCH), np.float32)
        SC = np.empty((2, 128, 6), np.float32)
        for d in range(2):
            for gi, rs in enumerate(rows):
                src = xbc[rs]
                XBC[d, gi, :, 2:SEQ + 2] = src[:, ::-1] if d == 1 else src
                w = conv_w[:, rs].T
                CWm[d, gi, :, 0:5] = w[:, ::-1] if d == 1 else w
                CWm[d, gi, :, 5] = conv_b[rs]
            for h in range(2):
                hd = 2 * i + h
                s_ = dtr[d * 16 + hd]
                s_ = s_[::-1] if d == 1 else s_
                DTR[d, h] = s_.reshape(NCH, 128).T
                SC[d, :, h] = dt_bias[d, hd]; SC[d, :, 2 + h] = a_log[d, hd]; SC[d, :, 4 + h] = ssd_d[hd]
        maps.append({"XBC": XBC, "CW": CWm, "DTR": DTR, "SC": SC, "TRI": tri, "SU": su, "ID": idm})
    res = _run(nc, maps)
    yf = np.empty((1024, SEQ), np.float32); yb = np.empty((1024, SEQ), np.float32)
    for i in range(NCORES):
        Yc = res[i]["Y"]
        for d in range(2):
            y = Yc[d].transpose(2, 1, 0).reshape(128, SEQ)
            if d == 0:
                yf[128 * i:128 * i + 128] = y
            else:
                yb[128 * i:128 * i + 128] = y[:, ::-1]
    return yf, yb


def kernel(x, c, e_norm_g, e_ada_w, e_ada_b, e_w_in, e_q_norm, e_k_norm, s5_lam_re, s5_lam_im,
           s5_log_step, s5_b_re, s5_b_im, s5_c_re, s5_c_im, s5_d, s5_w_glu, s5_b_glu, e_w_out,
           o_norm_g, o_ada_w, o_ada_b, o_w_in, o_q_norm, o_k_norm, na_rpb, ssd_conv_w, ssd_conv_b,
           ssd_dt_bias, ssd_a_log, ssd_d, ssd_norm_w, o_w_out):
    f = lambda a: np.asarray(a, np.float32)
    xT = np.ascontiguousarray(f(x)[0].T)
    obf0, of0, mod0 = run_front(0, xT, f(c), f(e_norm_g), f(e_ada_w), f(e_ada_b), f(e_w_in), f(e_q_norm), f(e_k_norm))
    oa, yf, yb = run_mix0(obf0, f(s5_lam_re)[0], f(s5_lam_im)[0], f(s5_log_step)[0], f(s5_b_re)[0], f(s5_b_im)[0],
                          f(s5_c_re)[0], f(s5_c_im)[0])
    x1T = run_post(0, xT, mod0, f(e_w_out)[0], oa, of0[0:1024], yfT=yf, ybT=yb, uT=obf0[1536:2560], gbT=of0[1024:2048],
                   s5_d=f(s5_d)[0], w_glu=f(s5_w_glu)[0], b_glu=f(s5_b_glu)[0])
    obf1, of1, mod1 = run_front(1, x1T, f(c), f(o_norm_g), f(o_ada_w), f(o_ada_b), f(o_w_in), f(o_q_norm), f(o_k_norm))
    oc = run_na(obf1, f(na_rpb)[0])
    sf, sb_ = run_ssd(of1, f(ssd_conv_w)[0], f(ssd_conv_b)[0], f(ssd_dt_bias)[0], f(ssd_a_log)[0], f(ssd_d)[0])
    x2T = run_post(1, x1T, mod1, f(o_w_out)[0], oc, of1[0:1024], ysT=sf, ys2T=sb_, zT=of1[1024:2048], norm_w=f(ssd_norm_w)[0])
    return np.ascontiguousarray(x2T.T)[None].astype(np.float32)


def build_mix0(NPR=16):
    nc = bass.Bass("TRN2", target_bir_lowering=False)
    qT = nc.dram_tensor("qT", [128, SEQ], BF16, kind="ExternalInput").ap()
    kT = nc.dram_tensor("kT", [128, SEQ], BF16, kind="ExternalInput").ap()
    vP = nc.dram_tensor("vP", [128, SEQ], BF16, kind="ExternalInput").ap()
    oT = nc.dram_tensor("oT", [128, SEQ], F32, kind="ExternalOutput").ap()
    U = nc.dram_tensor("U", [NPR, 4, 128, S5NC], BF16, kind="ExternalInput").ap()
    PRM = nc.dram_tensor("PRM", [NPR, 64, 67], F32, kind="ExternalInput").ap()
    ETd = nc.dram_tensor("ET", [64, 129], F32, kind="ExternalInput").ap()
    MKd = nc.dram_tensor("MK", [128, 4, 512], F32, kind="ExternalInput").ap()
    IDd = nc.dram_tensor("ID", [64, 64], F32, kind="ExternalInput").ap()
    Y = nc.dram_tensor("Y", [NPR, 4, 128, S5NC], F32, kind="ExternalOutput").ap()
    scale = 128.0 ** -0.5
    with ExitStack() as st:
        K = KB(nc, st)
        q = K.sb([128, SEQ], BF16, "q")
        k = K.sb([128, SEQ], BF16, "k")
        v = K.sb([128, SEQ], BF16, "v")
        for h in range(4):
            sl = slice(h * 4096, (h + 1) * 4096)
            K.dma("sp", k, k[:, sl], None, kT[:, sl])
            K.dma("pool", q, q[:, sl], None, qT[:, sl])
            K.dma("sp", v, v[:, sl], None, vP[:, sl])
        ones = K.sb([128, 128], BF16, "ones")
        K.memset("pool", ones, ones[:], 1.0)
        pS = [K.ps([128, 512], F32, f"pS{i}") for i in range(3)]
        pO = K.ps([128, 512], F32, "pO")
        pZ = K.ps([128, 512], F32, "pZ")
        pT = Rot([K.sb([128, 512], BF16, f"pT{i}") for i in range(3)])
        rec = Rot([K.sb([128, 512], F32, f"rec{i}") for i in range(2)])
        osb = Rot([K.sb([128, 512], F32, f"osb{i}") for i in range(2)])
        load, P1a, P1b, P2a, P2b, P2c = s5_program(K, nc, U, PRM, ETd, MKd, IDd, Y, NPR, shared=True)
        NQ, NKB = SEQ // 512, SEQ // 128
        seq = [(a_, b_) for a_ in range(NQ) for b_ in range(NKB)]
        curs = {}

        def s5_hook(t):
            pr, half = t // 2, t % 2
            if pr >= NPR:
                return
            if half == 0:
                if pr + 2 < NPR:
                    load(pr + 2)
                P2a(pr)
                if pr + 1 < NPR:
                    P1a(pr + 1)
                curs[pr] = P2b(pr)
            else:
                if pr + 1 < NPR:
                    P1b(pr + 1)
                P2c(pr, curs[pr])

        load(0)
        if NPR > 1:
            load(1)
        P1a(0); P1b(0)

        def issue_S(idx):
            qt, kb = seq[idx]
            ps = pS[idx % 3]
            K.mm(ps, ps[:], k, k[:, kb * 128:(kb + 1) * 128], q, q[:, qt * 512:(qt + 1) * 512])

        issue_S(0); issue_S(1)
        for idx, (qt, kb) in enumerate(seq):
            if kb == 0:
                s5_hook(qt)
            if idx + 2 < len(seq):
                issue_S(idx + 2)
            ps = pS[idx % 3]
            p = pT.next()
            K.act(p, p[:], ps, ps[:], AF.Exp, scale=scale)
            K.mm(pO, pO[:], v, v[:, kb * 128:(kb + 1) * 128], p, p[:], start=(kb == 0), stop=(kb == NKB - 1))
            K.mm(pZ, pZ[:], ones, ones[:], p, p[:], start=(kb == 0), stop=(kb == NKB - 1))
            if kb == NKB - 1:
                r = rec.next()
                K.op("dve", lambda e: e.reciprocal(out=r[:], in_=pZ[:]), [pZ], [r])
                ob = osb.next()
                K.tt("dve", ob, ob[:], pO, pO[:], r, r[:], ALU.mult)
                K.dma("pool", None, oT[:, qt * 512:(qt + 1) * 512], ob, ob[:])
        K.finish()
    return nc


def run_mix0(obf, lam_re, lam_im, log_step, b_re, b_im, c_re, c_im):
    nc = build_mix0()
    ET, MK, ID = s5_consts()
    Us = s5_pack_u(obf[1536:2560])
    Ps = s5_params(lam_re, lam_im, log_step, b_re, b_im, c_re, c_im)
    maps = []
    for i in range(NCORES):
        g = i // 4
        vT = obf[1280 + 128 * g:1280 + 128 * g + 128, :]
        vP = np.ascontiguousarray(vT.reshape(128, SEQ // 128, 128).transpose(2, 1, 0).reshape(128, SEQ))
        maps.append({"qT": np.ascontiguousarray(obf[128 * i:128 * i + 128, :]),
                     "kT": np.ascontiguousarray(obf[1024 + 128 * g:1024 + 128 * g + 128, :]),
                     "vP": vP, "U": Us[i], "PRM": Ps[i], "ET": ET, "MK": MK, "ID": ID})
    res = _run(nc, maps)
    oa = np.concatenate([r["oT"] for r in res], axis=0)
    yf, yb = s5_unpack_y([r["Y"] for r in res])
    return oa, yf, yb
```

```python
import math
from contextlib import ExitStack
import numpy as np
import ml_dtypes
import concourse.bass as bass
import concourse.mybir as mybir
from concourse.bass_utils import run_bass_kernel_spmd

F32 = mybir.dt.float32
BF16 = mybir.dt.bfloat16
AF = mybir.ActivationFunctionType
ALU = mybir.AluOpType
AX = mybir.AxisListType
NPBF = ml_dtypes.bfloat16

NCORES = 8
D = 2048
SEQ = 16384
TPC = SEQ // NCORES
EPS = 1e-6


class Buf:
    def __init__(self, t, name, onchip=True):
        self.t = t
        self.name = name
        self.onchip = onchip
        self.last_w = None
        self.reads = {}
        self.dsem = None
        self.dcnt = 0

    def __getitem__(self, idx):
        return self.t[idx]


SELF_ORDERED = {"pe"}


class KB:
    def __init__(self, nc, stack):
        self.nc = nc
        self.stack = stack
        self.eng = {"pe": nc.tensor, "dve": nc.vector, "act": nc.scalar, "pool": nc.gpsimd, "sp": nc.sync}
        self.sem, self.cnt, self.seen = {}, {}, {}
        for k in self.eng:
            self.sem[k] = stack.enter_context(nc.semaphore("s_" + k))
            self.cnt[k] = 0
            self.seen[k] = {}
        self.nbuf = 0
        self.dbufs = []

    def sb(self, shape, dt, name=None):
        self.nbuf += 1
        name = "s_" + (name or f"sb{self.nbuf}")
        return Buf(self.stack.enter_context(self.nc.sbuf_tensor(name, list(shape), dt)), name)

    def ps(self, shape, dt=F32, name=None):
        self.nbuf += 1
        name = "p_" + (name or f"ps{self.nbuf}")
        return Buf(self.stack.enter_context(self.nc.psum_tensor(name, list(shape), dt)), name)

    def _wait(self, e, tok):
        if tok is None:
            return
        sem, val, key = tok
        if key == e and e in SELF_ORDERED:
            return
        if self.seen[e].get(key, 0) >= val:
            return
        self.eng[e].wait_ge(sem, val)
        self.seen[e][key] = val

    def _deps(self, e, reads, writes):
        for b in reads:
            if b is not None:
                self._wait(e, b.last_w)
        for b in writes:
            if b is not None:
                self._wait(e, b.last_w)
                for r in b.reads.values():
                    self._wait(e, r)

    def _mark(self, tok, reads, writes):
        for b in reads:
            if b is not None:
                b.reads[tok[2]] = tok
        for b in writes:
            if b is not None:
                b.last_w = tok
                b.reads = {}

    def op(self, e, fn, reads=(), writes=()):
        self._deps(e, reads, writes)
        ins = fn(self.eng[e])
        self.cnt[e] += 1
        ins.then_inc(self.sem[e], 1)
        self._mark((self.sem[e], self.cnt[e], e), reads, writes)
        return ins

    def dma(self, q, out_b, out_ap, in_b, in_ap, **kw):
        self._deps(q, [in_b], [])
        if out_b is not None:
            lw = out_b.last_w
            own = "d_" + out_b.name
            if lw is None or lw[2] != own or out_b.reads:
                war = dict(out_b.reads)
                if lw is not None and lw[2] != own:
                    war["__w"] = lw
                out_b.war = war
            for r in getattr(out_b, "war", {}).values():
                self._wait(q, r)
        owner = out_b if out_b is not None else in_b
        if owner.dsem is None:
            owner.dsem = self.stack.enter_context(self.nc.semaphore("d_" + owner.name))
            self.dbufs.append(owner)
        ins = self.eng[q].dma_start(out=out_ap, in_=in_ap, **kw)
        owner.dcnt += 16
        ins.then_inc(owner.dsem, 16)
        tok = (owner.dsem, owner.dcnt, "d_" + owner.name)
        self._mark(tok, [in_b], [out_b])
        return tok

    def finish(self, e="sp"):
        for b in self.dbufs:
            self._wait(e, (b.dsem, b.dcnt, "d_" + b.name))

    def mm(self, ob, oap, lb, lap, rb, rap, start=True, stop=True):
        return self.op("pe", lambda e: e.matmul(oap, lhsT=lap, rhs=rap, start=start, stop=stop), [lb, rb], [ob])

    def act(self, ob, oap, ib, iap, func, extra=(), eng="act", **kw):
        return self.op(eng, lambda e: e.activation(out=oap, in_=iap, func=func, **kw), [ib, *extra], [ob])

    def ts(self, eng, ob, oap, ib, iap, s1, s2, op0, op1=None, extra=()):
        if op1 is None:
            return self.op(eng, lambda e: e.tensor_scalar(out=oap, in0=iap, scalar1=s1, scalar2=None, op0=op0), [ib, *extra], [ob])
        return self.op(eng, lambda e: e.tensor_scalar(out=oap, in0=iap, scalar1=s1, scalar2=s2, op0=op0, op1=op1), [ib, *extra], [ob])

    def tt(self, eng, ob, oap, ab, aap, bb, bap, op):
        return self.op(eng, lambda e: e.tensor_tensor(out=oap, in0=aap, in1=bap, op=op), [ab, bb], [ob])

    def stt(self, ob, oap, ab, aap, scalar, bb, bap, op0, op1, extra=()):
        return self.op("dve", lambda e: e.scalar_tensor_tensor(out=oap, in0=aap, scalar=scalar, in1=bap, op0=op0, op1=op1),
                       [ab, bb, *extra], [ob])

    def copy(self, eng, ob, oap, ib, iap):
        if eng == "act":
            return self.op("act", lambda e: e.copy(out=oap, in_=iap), [ib], [ob])
        return self.op(eng, lambda e: e.tensor_copy(out=oap, in_=iap), [ib], [ob])

    def memset(self, eng, ob, oap, val):
        return self.op(eng, lambda e: e.memset(oap, val), [], [ob])


TRACE = False
LAST_EXEC_NS = [None]


def _run(nc, in_maps):
    if TRACE:
        res = run_bass_kernel_spmd(nc, in_maps, core_ids=list(range(NCORES)), trace=True)
        LAST_EXEC_NS[0] = res.exec_time_ns
    else:
        res = run_bass_kernel_spmd(nc, in_maps, core_ids=list(range(NCORES)))
    return res.results


def wview(w_ap, n0, n1):
    return w_ap.rearrange("(k p) n -> p k n", p=128)[:, :, n0:n1]


class Rot:
    def __init__(self, items):
        self.items = items
        self.i = 0

    def next(self):
        b = self.items[self.i % len(self.items)]
        self.i += 1
        return b


def emit_mod(K, nc, dram, wst, pmod, consts):
    cT = K.sb([128, 16], F32, "cT")
    sc = K.sb([128, 16], F32, "sc")
    abT = K.sb([128, 48], F32, "abT")
    gT = K.sb([128, 16], F32, "gT")
    mod = K.sb([128, 48], F32, "mod")
    gs = K.sb([128, 16], F32, "gs")
    K.dma("sp", cT, cT[:], None, dram["cT"][:, :])
    K.dma("sp", abT, abT[:], None, dram["ada_bT"][:, :])
    K.dma("sp", gT, gT[:], None, dram["gT"][:, :])
    K.act(sc, sc[:], cT, cT[:], AF.Silu)
    NB = 6144 // 256
    pend = None
    for nb in range(NB + 1):
        cur = pend
        if nb < NB:
            st = wst.next()
            for h in range(4):
                K.dma("sp" if h % 2 == 0 else "pool", st, st[:, 4 * h:4 * h + 4, :], None,
                      wview(dram["ada_w"], nb * 256, nb * 256 + 256)[:, 4 * h:4 * h + 4, :])
            pend = (st, nb)
        if cur is not None:
            st, b = cur
            for jj in range(2):
                j = 2 * b + jj
                for k in range(16):
                    K.mm(pmod, pmod[:, j:j + 1], st, st[:, k, 128 * jj:128 * jj + 128], sc, sc[:, k:k + 1],
                         start=(k == 0), stop=(k == 15))
    K.tt("dve", mod, mod[:], pmod, pmod[:, 0:48], abT, abT[:], ALU.add)
    K.stt(gs, gs[:], mod, mod[:, 16:32], 1.0, gT, gT[:], ALU.add, ALU.mult)
    return mod, gs


def emit_rstd(K, ob, oap, ib, iap, inv_n):
    K.ts("dve", ob, oap, ib, iap, inv_n, EPS, ALU.mult, ALU.add)
    K.act(ob, oap, ob, oap, AF.Sqrt)
    K.op("dve", lambda e: e.reciprocal(out=oap, in_=oap), [ob], [ob])


def front_plan(N, kinds):
    chunks = []
    nbf = nf = 0
    for ci, kd in enumerate(kinds):
        n0 = ci * 128
        m = min(128, N - n0)
        isbf = (kd == "bf") or (isinstance(kd, tuple))
        if isbf:
            chunks.append((n0, m, kd, "bf", nbf)); nbf += m
        else:
            chunks.append((n0, m, kd, "f", nf)); nf += m
    return chunks, nbf, nf


def emit_front_core(K, nc, dram, N, kinds, rope, xsrc, outs, T=TPC):
    chunks, nbf, nf = front_plan(N, kinds)
    NT = T // 512
    ones = K.sb([128, 128], F32, "ones")
    K.memset("pool", ones, ones[:], 1.0)
    wst = Rot([K.sb([128, 16, 256], F32, f"wst{i}") for i in range(2)])
    wbs = Rot([K.sb([128, 16, 256], BF16, f"wb{i}") for i in range(2)])
    pmod = K.ps([128, 64], F32, "pmod")
    acc = [K.ps([128, 512], F32, f"acc{i}") for i in range(4)]
    mod, gs = emit_mod(K, nc, dram, wst, pmod, None)

    xTb = K.sb([128, 16, T], BF16, "xTb")
    rstd = K.sb([128, T], F32, "rstd")
    sq = K.sb([128, T], F32, "sq")
    pend = xsrc(0)
    for k in range(16):
        xs = pend
        if k + 1 < 16:
            pend = xsrc(k + 1)
        K.copy("dve", xTb, xTb[:, k, :], xs, xs[:, :])
        K.act(sq, sq[:], xs, xs[:, :], AF.Square)
        for t in range(NT):
            K.mm(acc[t], acc[t][:], ones, ones[:], sq, sq[:, 512 * t:512 * t + 512], start=(k == 0), stop=(k == 15))
    for t in range(NT):
        emit_rstd(K, rstd, rstd[:, 512 * t:512 * t + 512], acc[t], acc[t][:], 1.0 / D)

    if rope:
        cosT = K.sb([128, T], F32, "cosT")
        sinT = K.sb([128, T], F32, "sinT")
        K.dma("sp", cosT, cosT[:], None, dram["cosT"][:, :])
        K.dma("sp", sinT, sinT[:], None, dram["sinT"][:, :])
        swp = K.sb([128, 128], F32, "swp")
        K.dma("sp", swp, swp[:], None, dram["swapM"][:, :])
    qkg = K.sb([128, 2], F32, "qkg")
    K.dma("sp", qkg, qkg[:], None, dram["qkg"][:, :])
    psq = K.ps([128, 512], F32, "psq")
    psw = K.ps([128, 512], F32, "psw")
    bvec = K.sb([128, 64], F32, "bvec")
    tmpR = Rot([K.sb([128, 512], F32, f"tmp{i}") for i in range(2)])
    resR = Rot([K.sb([128, 512], F32, f"res{i}") for i in range(2)])
    resbR = Rot([K.sb([128, 512], BF16, f"resb{i}") for i in range(2)])
    sqkR = Rot([K.sb([128, 512], F32, f"sqk{i}") for i in range(2)])
    qnR = Rot([K.sb([128, 512], F32, f"qn{i}") for i in range(2)])
    accR = Rot(acc)

    NB = (N + 255) // 256

    def load_w(nb):
        st = wst.next()
        n0 = nb * 256
        w = min(256, N - n0)
        for h in range(4):
            K.dma("sp" if h % 2 == 0 else "pool", st, st[:, 4 * h:4 * h + 4, 0:w], None,
                  wview(dram["w_in"], n0, n0 + w)[:, 4 * h:4 * h + 4, :])
        return st

    pend = load_w(0)
    for nb in range(NB):
        st = pend
        if nb + 1 < NB:
            pend = load_w(nb + 1)
        n0 = nb * 256
        w = min(256, N - n0)
        wb = wbs.next()
        for k in range(16):
            if k % 2 == 0:
                K.ts("dve", wb, wb[:, k, 0:w], st, st[:, k, 0:w], gs[:, k:k + 1], None, ALU.mult, extra=[gs])
            else:
                K.act(wb, wb[:, k, 0:w], st, st[:, k, 0:w], AF.Identity, extra=[gs], scale=gs[:, k:k + 1])
        for jj in range((w + 127) // 128):
            ci = 2 * nb + jj
            cn0, m, kd, okind, orow = chunks[ci]
            c0 = 128 * jj
            for k in range(16):
                K.mm(pmod, pmod[0:m, 48:49], st, st[:, k, c0:c0 + m], mod, mod[:, k:k + 1], start=(k == 0), stop=(k == 15))
            bcol = bvec[0:m, ci:ci + 1]
            K.copy("dve", bvec, bcol, pmod, pmod[0:m, 48:49])
            for t in range(NT):
                ts_ = slice(512 * t, 512 * t + 512)
                a = accR.next()
                for k in range(16):
                    K.mm(a, a[0:m, :], wb, wb[:, k, c0:c0 + m], xTb, xTb[:, k, ts_], start=(k == 0), stop=(k == 15))
                tmp = tmpR.next()
                K.tt("dve", tmp, tmp[0:m, :], a, a[0:m, :], rstd, rstd[0:m, ts_], ALU.mult)
                if kd == "silu":
                    res = resR.next()
                    K.act(res, res[0:m, :], tmp, tmp[0:m, :], AF.Silu, extra=[bvec], bias=bcol)
                    K.dma("pool", None, outs["f"][orow:orow + m, ts_], res, res[0:m, :])
                elif kd == "f32":
                    res = resR.next()
                    K.act(res, res[0:m, :], tmp, tmp[0:m, :], AF.Identity, extra=[bvec], bias=bcol)
                    K.dma("pool", None, outs["f"][orow:orow + m, ts_], res, res[0:m, :])
                elif kd == "bf":
                    resb = resbR.next()
                    K.act(resb, resb[0:m, :], tmp, tmp[0:m, :], AF.Identity, extra=[bvec], bias=bcol)
                    K.dma("pool", None, outs["bf"][orow:orow + m, ts_], resb, resb[0:m, :])
                else:
                    _, gcol, do_rope = kd
                    res = resR.next()
                    K.act(res, res[:], tmp, tmp[:], AF.Identity, extra=[bvec], bias=bcol)
                    sqk = sqkR.next()
                    K.act(sqk, sqk[:], res, res[:], AF.Square)
                    K.mm(psq, psq[:], ones, ones[:], sqk, sqk[:])
                    emit_rstd(K, sqk, sqk[:], psq, psq[:], 1.0 / 128)
                    qn = qnR.next()
                    K.stt(qn, qn[:], res, res[:], qkg[:, gcol:gcol + 1], sqk, sqk[:], ALU.mult, ALU.mult, extra=[qkg])
                    resb = resbR.next()
                    if do_rope:
                        K.mm(psw, psw[:], swp, swp[:], qn, qn[:])
                        K.tt("dve", sqk, sqk[:], psw, psw[:], sinT, sinT[:, ts_], ALU.mult)
                        K.tt("dve", qn, qn[:], qn, qn[:], cosT, cosT[:, ts_], ALU.mult)
                        K.tt("dve", resb, resb[:], qn, qn[:], sqk, sqk[:], ALU.add)
                    else:
                        K.copy("dve", resb, resb[:], qn, qn[:])
                    K.dma("pool", None, outs["bf"][orow:orow + m, ts_], resb, resb[:])
    return mod


KINDS_E = [("qk", 0, True)] * 8 + [("qk", 1, True)] * 2 + ["bf"] * 2 + ["silu"] * 8 + ["bf"] * 8 + ["silu"] * 8
N_E = 4608
KINDS_O = [("qk", 0, False)] * 8 + [("qk", 1, False)] * 8 + ["bf"] * 8 + ["silu"] * 8 + ["silu"] * 8 + ["f32"] * 12 + ["f32"]
N_O = 6688


def front_dram(nc, N, rope, T=TPC):
    d = {}
    d["cT"] = nc.dram_tensor("cT", [128, 16], F32, kind="ExternalInput").ap()
    d["ada_bT"] = nc.dram_tensor("ada_bT", [128, 48], F32, kind="ExternalInput").ap()
    d["gT"] = nc.dram_tensor("gT", [128, 16], F32, kind="ExternalInput").ap()
    d["ada_w"] = nc.dram_tensor("ada_w", [D, 3 * D], F32, kind="ExternalInput").ap()
    d["w_in"] = nc.dram_tensor("w_in", [D, N], F32, kind="ExternalInput").ap()
    d["qkg"] = nc.dram_tensor("qkg", [128, 2], F32, kind="ExternalInput").ap()
    if rope:
        d["cosT"] = nc.dram_tensor("cosT", [128, T], F32, kind="ExternalInput").ap()
        d["sinT"] = nc.dram_tensor("sinT", [128, T], F32, kind="ExternalInput").ap()
        d["swapM"] = nc.dram_tensor("swapM", [128, 128], F32, kind="ExternalInput").ap()
    return d


def build_front(N, kinds, rope):
    nc = bass.Bass("TRN2", target_bir_lowering=False)
    dram = front_dram(nc, N, rope)
    xT = nc.dram_tensor("xT", [D, TPC], F32, kind="ExternalInput").ap()
    chunks, nbf, nf = front_plan(N, kinds)
    outs = {"bf": nc.dram_tensor("obf", [nbf, TPC], BF16, kind="ExternalOutput").ap(),
            "f": nc.dram_tensor("of", [nf, TPC], F32, kind="ExternalOutput").ap()}
    modo = nc.dram_tensor("modo", [128, 48], F32, kind="ExternalOutput").ap()
    with ExitStack() as st:
        K = KB(nc, st)
        xst = Rot([K.sb([128, TPC], F32, f"xst{i}") for i in range(2)])

        def xsrc(k):
            b = xst.next()
            K.dma("sp", b, b[:, 0:TPC // 2], None, xT[128 * k:128 * k + 128, 0:TPC // 2])
            K.dma("pool", b, b[:, TPC // 2:], None, xT[128 * k:128 * k + 128, TPC // 2:])
            return b
        mod = emit_front_core(K, nc, dram, N, kinds, rope, xsrc, outs)
        K.dma("pool", None, modo[:, :], mod, mod[:])
        K.finish()
    return nc


def pj(v, ncol):
    return np.ascontiguousarray(np.asarray(v, np.float32).reshape(ncol, 128).T)


def rope_tables():
    t = np.arange(SEQ)
    row = (t // 64).astype(np.float32)
    col = (t % 64).astype(np.float32)
    inv = (np.float32(10000.0) ** (-np.arange(32, dtype=np.float32) / np.float32(32))).astype(np.float32)
    ang = np.concatenate([row[:, None] * inv, col[:, None] * inv], axis=-1).astype(np.float32)
    cos, sin = np.cos(ang).astype(np.float32), np.sin(ang).astype(np.float32)
    cosT = np.repeat(cos, 2, axis=1).T
    sgn = np.tile(np.array([-1.0, 1.0], np.float32), 64)
    sinT = (np.repeat(sin, 2, axis=1) * sgn).T
    return np.ascontiguousarray(cosT), np.ascontiguousarray(sinT)


def swap_matrix():
    m = np.zeros((128, 128), np.float32)
    for i in range(64):
        m[2 * i + 1, 2 * i] = 1.0
        m[2 * i, 2 * i + 1] = 1.0
    return m


def front_common_inputs(c, ada_w, ada_b, norm_g, w_in, qn, kn):
    return {"cT": pj(c.reshape(-1), 16), "ada_bT": pj(ada_b.reshape(-1), 48), "gT": pj(norm_g.reshape(-1), 16),
            "ada_w": np.ascontiguousarray(ada_w), "w_in": np.ascontiguousarray(w_in),
            "qkg": np.ascontiguousarray(np.stack([qn.reshape(-1), kn.reshape(-1)], axis=1).astype(np.float32))}


def run_front(layer, xT, c, norm_g, ada_w, ada_b, w_in, q_norm, k_norm):
    rope = (layer == 0)
    nc = build_front(N_E if layer == 0 else N_O, KINDS_E if layer == 0 else KINDS_O, rope)
    base = front_common_inputs(c, ada_w[0], ada_b[0], norm_g[0], w_in[0], q_norm[0], k_norm[0])
    if rope:
        cosT, sinT = rope_tables()
        base["swapM"] = swap_matrix()
    maps = []
    for i in range(NCORES):
        sl = slice(i * TPC, (i + 1) * TPC)
        m = dict(base)
        m["xT"] = np.ascontiguousarray(xT[:, sl])
        if rope:
            m["cosT"] = np.ascontiguousarray(cosT[:, sl])
            m["sinT"] = np.ascontiguousarray(sinT[:, sl])
        maps.append(m)
    res = _run(nc, maps)
    obf = np.concatenate([r["obf"] for r in res], axis=1)
    of = np.concatenate([r["of"] for r in res], axis=1)
    return obf, of, res[0]["modo"]


def build_attn():
    nc = bass.Bass("TRN2", target_bir_lowering=False)
    qT = nc.dram_tensor("qT", [128, SEQ], BF16, kind="ExternalInput").ap()
    kT = nc.dram_tensor("kT", [128, SEQ], BF16, kind="ExternalInput").ap()
    vP = nc.dram_tensor("vP", [128, SEQ], BF16, kind="ExternalInput").ap()
    oT = nc.dram_tensor("oT", [128, SEQ], F32, kind="ExternalOutput").ap()
    scale = 128.0 ** -0.5
    with ExitStack() as st:
        K = KB(nc, st)
        q = K.sb([128, SEQ], BF16, "q")
        k = K.sb([128, SEQ], BF16, "k")
        v = K.sb([128, SEQ], BF16, "v")
        for h in range(4):
            sl = slice(h * 4096, (h + 1) * 4096)
            K.dma("sp", k, k[:, sl], None, kT[:, sl])
            K.dma("pool", q, q[:, sl], None, qT[:, sl])
            K.dma("sp", v, v[:, sl], None, vP[:, sl])
        ones = K.sb([128, 128], BF16, "ones")
        K.memset("pool", ones, ones[:], 1.0)
        pS = [K.ps([128, 512], F32, f"pS{i}") for i in range(3)]
        pO = [K.ps([128, 512], F32, f"pO{i}") for i in range(2)]
        pZ = [K.ps([128, 512], F32, f"pZ{i}") for i in range(2)]
        pT = Rot([K.sb([128, 512], BF16, f"pT{i}") for i in range(3)])
        rec = Rot([K.sb([128, 512], F32, f"rec{i}") for i in range(2)])
        osb = Rot([K.sb([128, 512], F32, f"osb{i}") for i in range(2)])
        NQ, NKB = SEQ // 512, SEQ // 128
        seq = [(a, b) for a in range(NQ) for b in range(NKB)]
        LOOK = 2

        def issue_S(idx):
            qt, kb = seq[idx]
            ps = pS[idx % 3]
            K.mm(ps, ps[:], k, k[:, kb * 128:(kb + 1) * 128], q, q[:, qt * 512:(qt + 1) * 512])

        for i in range(LOOK):
            issue_S(i)
        for idx, (qt, kb) in enumerate(seq):
            if idx + LOOK < len(seq):
                issue_S(idx + LOOK)
            ps = pS[idx % 3]
            p = pT.next()
            K.act(p, p[:], ps, ps[:], AF.Exp, scale=scale)
            o, z = pO[qt % 2], pZ[qt % 2]
            K.mm(o, o[:], v, v[:, kb * 128:(kb + 1) * 128], p, p[:], start=(kb == 0), stop=(kb == NKB - 1))
            K.mm(z, z[:], ones, ones[:], p, p[:], start=(kb == 0), stop=(kb == NKB - 1))
            if kb == NKB - 1:
                r = rec.next()
                K.op("dve", lambda e: e.reciprocal(out=r[:], in_=z[:]), [z], [r])
                ob = osb.next()
                K.tt("dve", ob, ob[:], o, o[:], r, r[:], ALU.mult)
                K.dma("pool", None, oT[:, qt * 512:(qt + 1) * 512], ob, ob[:])
        K.finish()
    return nc


def run_attn(obf):
    nc = build_attn()
    maps = []
    for i in range(NCORES):
        g = i // 4
        vT = obf[1280 + 128 * g:1280 + 128 * g + 128, :]
        vP = np.ascontiguousarray(vT.reshape(128, SEQ // 128, 128).transpose(2, 1, 0).reshape(128, SEQ))
        maps.append({"qT": np.ascontiguousarray(obf[128 * i:128 * i + 128, :]),
                     "kT": np.ascontiguousarray(obf[1024 + 128 * g:1024 + 128 * g + 128, :]),
                     "vP": vP})
    res = _run(nc, maps)
    return np.concatenate([r["oT"] for r in res], axis=0)


S5T = 32
S5NC = SEQ // S5T
TWO_PI = 2.0 * math.pi
S5_OFF = TWO_PI * 128


def s5_consts():
    s = np.arange(32, dtype=np.float32)
    et = np.concatenate([-s, 31 - s, s, s + 1, np.array([32.0], np.float32)]).astype(np.float32)
    ET = np.ascontiguousarray(np.broadcast_to(et, (64, 129)))
    mask = np.zeros((128, 4, 512), np.float32)
    for kb in range(4):
        for sl in range(8):
            sg = 8 * kb + sl
            for t in range(32):
                if sg <= t:
                    mask[sl * 16:(sl + 1) * 16, kb, t * 16:(t + 1) * 16] = 1.0
    ident = np.eye(64, dtype=np.float32)
    return ET, mask, ident


def s5_program(K, nc, U, PRM, ETd, MKd, IDd, Y, NPR, shared=False):
    cpe = "dve" if shared else "act"
    ET = K.sb([64, 129], F32, "ET"); K.dma("sp", ET, ET[:], None, ETd[:, :])
    MK = K.sb([128, 4, 512], F32, "MK"); K.dma("sp", MK, MK[:], None, MKd[:, :, :])
    ID = K.sb([64, 64], F32, "ID"); K.dma("sp", ID, ID[:], None, IDd[:, :])
    uR = Rot([K.sb([128, 4, S5NC], BF16, f"u{i}") for i in range(3)])
    prR = Rot([K.sb([64, 67], F32, f"prm{i}") for i in range(3)])
    sc = K.sb([64, 16], F32, "sc")
    mag = K.sb([64, 129], F32, "mag")
    ang = K.sb([64, 129], F32, "ang")
    ang2 = K.sb([64, 129], F32, "ang2")
    angi = K.sb([64, 129], mybir.dt.int32, "angi")
    PWr = K.sb([64, 129], F32, "PWr")
    PWi = K.sb([64, 129], F32, "PWi")
    bb = K.sb([64, 32], F32, "bb")
    t1 = K.sb([64, 512], F32, "t1")
    t2 = K.sb([64, 512], F32, "t2")
    BLr = K.sb([64, 512], F32, "BLr"); BLi = K.sb([64, 512], F32, "BLi")
    WSr = K.sb([64, 512], F32, "WSr"); WSi = K.sb([64, 512], F32, "WSi")
    CLr = K.sb([64, 512], F32, "CLr"); CLm = K.sb([64, 512], F32, "CLm")
    WOr = [K.sb([64, 512], BF16, f"WOr{i}") for i in range(2)]; WOm = [K.sb([64, 512], BF16, f"WOm{i}") for i in range(2)]
    Msb = [K.sb([128, 4, 512], BF16, f"Msb{i}") for i in range(2)]
    WSs = K.sb([128, 4, 128], BF16, "WSs")
    Pre = [K.sb([64, S5NC], F32, f"Pre{i}") for i in range(2)]
    Pim = [K.sb([64, S5NC], F32, f"Pim{i}") for i in range(2)]
    Hre = K.sb([64, S5NC], BF16, "Hre"); Him = K.sb([64, S5NC], BF16, "Him")
    K.memset("pool", Hre, Hre[:], 0.0); K.memset("pool", Him, Him[:], 0.0)
    Ac = [K.sb([64, 8], F32, f"Ac{i}") for i in range(2)]
    ysb = Rot([K.sb([128, S5NC], F32, f"ysb{i}") for i in range(2)])
    pM = K.ps([128, 512], F32, "pM")
    if shared:
        pW = K.ps([128, 512], F32, "pWE"); pEr = pW
    else:
        pW = K.ps([128, 128], F32, "pW")
        pEr = K.ps([64, 512], F32, "pEr")
    pEi = K.ps([64, 512], F32, "pEi")
    pY = [pM, pM] if shared else [K.ps([128, 512], F32, f"pY{i}") for i in range(2)]

    def v3(ap):
        return ap.rearrange("p (s c) -> p s c", c=16)

    def bs(buf, a):
        return buf[:, a:a + 32].unsqueeze(2).to_broadcast([64, 32, 16])

    def bc(buf, a):
        return buf[:, a:a + 16].unsqueeze(1).to_broadcast([64, 32, 16])

    def table(outr, outi, a, xb, xr, xi, neg_im):
        K.tt("dve", t1, v3(t1[:, :]), PWr, bs(PWr, a), xb, bc(xb, xr), ALU.mult)
        K.tt("dve", t2, v3(t2[:, :]), PWi, bs(PWi, a), xb, bc(xb, xi), ALU.mult)
        K.tt("dve", outr, v3(outr[:, :]), t1, v3(t1[:, :]), t2, v3(t2[:, :]), ALU.subtract)
        K.tt("dve", t1, v3(t1[:, :]), PWr, bs(PWr, a), xb, bc(xb, xi), ALU.mult)
        K.tt("dve", t2, v3(t2[:, :]), PWi, bs(PWi, a), xb, bc(xb, xr), ALU.mult)
        if neg_im:
            K.stt(outi, v3(outi[:, :]), t1, v3(t1[:, :]), -1.0, t2, v3(t2[:, :]), ALU.mult, ALU.subtract)
        else:
            K.tt("dve", outi, v3(outi[:, :]), t1, v3(t1[:, :]), t2, v3(t2[:, :]), ALU.add)

    ups = {}

    def load(pr):
        u = uR.next(); p = prR.next()
        K.dma("sp", u, u[:], None, U[pr].rearrange("k p j -> p k j"))
        K.dma("sp", p, p[:], None, PRM[pr])
        ups[pr] = (u, p)

    def P1a(pr):
        u, p = ups[pr]
        K.act(sc, sc[:, 0:1], p, p[:, 2:3], AF.Exp)
        K.tt("dve", sc, sc[:, 1:2], p, p[:, 0:1], sc, sc[:, 0:1], ALU.mult)
        K.tt("dve", sc, sc[:, 2:3], p, p[:, 1:2], sc, sc[:, 0:1], ALU.mult)
        K.act(mag, mag[:], ET, ET[:], AF.Exp, extra=[sc], scale=sc[:, 1:2])
        K.ts("dve", ang, ang[:], ET, ET[:], sc[:, 2:3], S5_OFF, ALU.mult, ALU.add, extra=[sc])
        K.ts("dve", ang2, ang2[:], ang, ang[:], 1.0 / TWO_PI, None, ALU.mult)
        K.copy("dve", angi, angi[:], ang2, ang2[:])
        K.copy("dve", ang2, ang2[:], angi, angi[:])
        K.stt(ang, ang[:], ang2, ang2[:], -TWO_PI, ang, ang[:], ALU.mult, ALU.add)
        K.act(PWi, PWi[:], ang, ang[:], AF.Sin, scale=0.5)
        K.act(ang2, ang2[:], ang, ang[:], AF.Sin, scale=0.25)
        K.tt("dve", ang2, ang2[:], ang2, ang2[:], ang2, ang2[:], ALU.mult)
        K.ts("dve", ang2, ang2[:], ang2, ang2[:], -2.0, 1.0, ALU.mult, ALU.add)
        K.tt("dve", PWr, PWr[:], PWi, PWi[:], PWi, PWi[:], ALU.mult)
        K.ts("dve", PWr, PWr[:], PWr, PWr[:], -2.0, 1.0, ALU.mult, ALU.add)
        K.stt(PWi, PWi[:], PWi, PWi[:], 2.0, ang2, ang2[:], ALU.mult, ALU.mult)
        K.tt("dve", PWr, PWr[:], PWr, PWr[:], mag, mag[:], ALU.mult)
        K.tt("dve", PWi, PWi[:], PWi, PWi[:], mag, mag[:], ALU.mult)
        K.ts("dve", sc, sc[:, 3:4], PWr, PWr[:, 65:66], 1.0, None, ALU.subtract)
        K.tt("dve", sc, sc[:, 4:5], p, p[:, 0:1], p, p[:, 0:1], ALU.mult)
        K.stt(sc, sc[:, 4:5], p, p[:, 1:2], p[:, 1:2], sc, sc[:, 4:5], ALU.mult, ALU.add)
        K.op("dve", lambda e: e.reciprocal(out=sc[:, 4:5], in_=sc[:, 4:5]), [sc], [sc])
        K.tt("dve", sc, sc[:, 5:6], sc, sc[:, 3:4], p, p[:, 0:1], ALU.mult)
        K.stt(sc, sc[:, 5:6], PWi, PWi[:, 65:66], p[:, 1:2], sc, sc[:, 5:6], ALU.mult, ALU.add, extra=[p])
        K.tt("dve", sc, sc[:, 5:6], sc, sc[:, 5:6], sc, sc[:, 4:5], ALU.mult)
        K.tt("dve", sc, sc[:, 6:7], sc, sc[:, 3:4], p, p[:, 1:2], ALU.mult)
        K.stt(sc, sc[:, 6:7], PWi, PWi[:, 65:66], p[:, 0:1], sc, sc[:, 6:7], ALU.mult, ALU.subtract, extra=[p])
        K.tt("dve", sc, sc[:, 6:7], sc, sc[:, 6:7], sc, sc[:, 4:5], ALU.mult)
        K.ts("dve", t1, t1[:, 0:16], p, p[:, 19:35], sc[:, 6:7], None, ALU.mult, extra=[sc])
        K.stt(bb, bb[:, 0:16], p, p[:, 3:19], sc[:, 5:6], t1, t1[:, 0:16], ALU.mult, ALU.subtract, extra=[sc])
        K.ts("dve", t1, t1[:, 0:16], p, p[:, 3:19], sc[:, 6:7], None, ALU.mult, extra=[sc])
        K.stt(bb, bb[:, 16:32], p, p[:, 19:35], sc[:, 5:6], t1, t1[:, 0:16], ALU.mult, ALU.add, extra=[sc])
        table(BLr, BLi, 0, bb, 0, 16, False)
        table(WSr, WSi, 32, bb, 0, 16, False)
        table(CLr, CLm, 64, p, 35, 51, True)
        table(WOr[pr % 2], WOm[pr % 2], 96, p, 35, 51, True)
        K.copy("dve", Ac[pr % 2], Ac[pr % 2][:, 0:1], PWr, PWr[:, 128:129])
        K.copy("dve", Ac[pr % 2], Ac[pr % 2][:, 1:2], PWi, PWi[:, 128:129])
        K.ts("dve", Ac[pr % 2], Ac[pr % 2][:, 2:3], PWi, PWi[:, 128:129], -1.0, None, ALU.mult)

    def P1b(pr):
        u, p = ups[pr]
        for kb in range(4):
            ks = slice(kb * 128, (kb + 1) * 128)
            K.mm(pM, pM[:], BLr, BLr[:, ks], CLr, CLr[:, :], start=True, stop=False)
            K.mm(pM, pM[:], BLi, BLi[:, ks], CLm, CLm[:, :], start=False, stop=True)
            K.tt("dve", Msb[pr % 2], Msb[pr % 2][:, kb, :], pM, pM[:], MK, MK[:, kb, :], ALU.mult)
            K.mm(pW, pW[:, 0:64], WSr, WSr[:, ks], ID, ID[:], start=True, stop=True)
            K.mm(pW, pW[:, 64:128], WSi, WSi[:, ks], ID, ID[:], start=True, stop=True)
            K.copy(cpe, WSs, WSs[:, kb, :], pW, pW[:, 0:128])

    def P2a(pr):
        u, p = ups[pr]
        for kb in range(4):
            K.mm(pEr, pEr[0:64, :], WSs, WSs[:, kb, 0:64], u, u[:, kb, :], start=(kb == 0), stop=(kb == 3))
        for kb in range(4):
            K.mm(pEi, pEi[:], WSs, WSs[:, kb, 64:128], u, u[:, kb, :], start=(kb == 0), stop=(kb == 3))
        K.copy(cpe, Pre[0], Pre[0][:], pEr, pEr[0:64, :])
        K.copy("dve", Pim[0], Pim[0][:], pEi, pEi[:])

    def P2b(pr):
        A_ = Ac[pr % 2]
        cur = 0
        d = 1
        while d < S5NC:
            re, im, nre, nim = Pre[cur], Pim[cur], Pre[1 - cur], Pim[1 - cur]
            n = S5NC
            K.stt(nre, nre[:, d:n], re, re[:, 0:n - d], A_[:, 0:1], re, re[:, d:n], ALU.mult, ALU.add, extra=[A_])
            K.stt(nre, nre[:, d:n], im, im[:, 0:n - d], A_[:, 2:3], nre, nre[:, d:n], ALU.mult, ALU.add, extra=[A_])
            K.stt(nim, nim[:, d:n], im, im[:, 0:n - d], A_[:, 0:1], im, im[:, d:n], ALU.mult, ALU.add, extra=[A_])
            K.stt(nim, nim[:, d:n], re, re[:, 0:n - d], A_[:, 1:2], nim, nim[:, d:n], ALU.mult, ALU.add, extra=[A_])
            K.copy(cpe, nre, nre[:, 0:d], re, re[:, 0:d])
            K.copy(cpe, nim, nim[:, 0:d], im, im[:, 0:d])
            cur = 1 - cur
            d *= 2
            if d < S5NC:
                K.tt("dve", A_, A_[:, 3:4], A_, A_[:, 0:1], A_, A_[:, 0:1], ALU.mult)
                K.stt(A_, A_[:, 3:4], A_, A_[:, 1:2], A_[:, 2:3], A_, A_[:, 3:4], ALU.mult, ALU.add)
                K.stt(A_, A_[:, 4:5], A_, A_[:, 0:1], 2.0, A_, A_[:, 1:2], ALU.mult, ALU.mult)
                K.copy("dve", A_, A_[:, 0:1], A_, A_[:, 3:4])
                K.copy("dve", A_, A_[:, 1:2], A_, A_[:, 4:5])
                K.ts("dve", A_, A_[:, 2:3], A_, A_[:, 4:5], -1.0, None, ALU.mult)
        return cur

    def P2c(pr, cur):
        u, p = ups[pr]
        K.copy(cpe, Hre, Hre[:, 1:S5NC], Pre[cur], Pre[cur][:, 0:S5NC - 1])
        K.copy("dve", Him, Him[:, 1:S5NC], Pim[cur], Pim[cur][:, 0:S5NC - 1])
        for tb in range(4):
            tsl = slice(tb * 128, (tb + 1) * 128)
            py = pY[tb % 2]
            for kb in range(tb + 1):
                K.mm(py, py[:], Msb[pr % 2], Msb[pr % 2][:, kb, tsl], u, u[:, kb, :], start=(kb == 0), stop=False)
            K.mm(py, py[:], WOr[pr % 2], WOr[pr % 2][:, tsl], Hre, Hre[:], start=False, stop=False)
            K.mm(py, py[:], WOm[pr % 2], WOm[pr % 2][:, tsl], Him, Him[:], start=False, stop=True)
            yb = ysb.next()
            K.copy(cpe if tb % 2 == 0 else "dve", yb, yb[:], py, py[:])
            K.dma("pool", None, Y[pr, tb], yb, yb[:])

    return load, P1a, P1b, P2a, P2b, P2c


def build_s5(NPR=16):
    nc = bass.Bass("TRN2", target_bir_lowering=False)
    U = nc.dram_tensor("U", [NPR, 4, 128, S5NC], BF16, kind="ExternalInput").ap()
    PRM = nc.dram_tensor("PRM", [NPR, 64, 67], F32, kind="ExternalInput").ap()
    ETd = nc.dram_tensor("ET", [64, 129], F32, kind="ExternalInput").ap()
    MKd = nc.dram_tensor("MK", [128, 4, 512], F32, kind="ExternalInput").ap()
    IDd = nc.dram_tensor("ID", [64, 64], F32, kind="ExternalInput").ap()
    Y = nc.dram_tensor("Y", [NPR, 4, 128, S5NC], F32, kind="ExternalOutput").ap()
    with ExitStack() as st:
        K = KB(nc, st)
        load, P1a, P1b, P2a, P2b, P2c = s5_program(K, nc, U, PRM, ETd, MKd, IDd, Y, NPR)
        load(0)
        if NPR > 1:
            load(1)
        P1a(0); P1b(0)
        for pr in range(NPR):
            if pr + 2 < NPR:
                load(pr + 2)
            P2a(pr)
            if pr + 1 < NPR:
                P1a(pr + 1)
            cur_ = P2b(pr)
            if pr + 1 < NPR:
                P1b(pr + 1)
            P2c(pr, cur_)
        K.finish()
    return nc


def s5_pack_u(uT):
    outs = []
    for i in range(NCORES):
        U = np.empty((16, 4, 128, S5NC), uT.dtype)
        for gl in range(8):
            g = 8 * i + gl
            ug = uT[16 * g:16 * g + 16, :]
            for d in range(2):
                x = ug[:, ::-1] if d == 1 else ug
                x = x.reshape(16, S5NC, 32).transpose(2, 0, 1)
                U[2 * gl + d] = x.reshape(4, 128, S5NC)
        outs.append(U)
    return outs


def s5_unpack_y(Ys):
    yf = np.empty((1024, SEQ), np.float32); yb = np.empty((1024, SEQ), np.float32)
    for i in range(NCORES):
        for gl in range(8):
            g = 8 * i + gl
            for d in range(2):
                y = Ys[i][2 * gl + d].reshape(32, 16, S5NC).transpose(1, 2, 0).reshape(16, SEQ)
                if d == 0:
                    yf[16 * g:16 * g + 16] = y
                else:
                    yb[16 * g:16 * g + 16] = y[:, ::-1]
    return yf, yb


def s5_params(lam_re, lam_im, log_step, b_re, b_im, c_re, c_im):
    outs = []
    for i in range(NCORES):
        P = np.empty((16, 64, 67), np.float32)
        for gl in range(8):
            g = 8 * i + gl
            for d in range(2):
                pr = 2 * gl + d
                P[pr, :, 0] = lam_re[d, g]; P[pr, :, 1] = lam_im[d, g]; P[pr, :, 2] = log_step[d, g]
                P[pr, :, 3:19] = b_re[d, g]; P[pr, :, 19:35] = b_im[d, g]
                P[pr, :, 35:51] = c_re[d, g].T; P[pr, :, 51:67] = c_im[d, g].T
        outs.append(P)
    return outs


def run_s5(uT, lam_re, lam_im, log_step, b_re, b_im, c_re, c_im):
    nc = build_s5()
    ET, MK, ID = s5_consts()
    Us = s5_pack_u(uT)
    Ps = s5_params(lam_re, lam_im, log_step, b_re, b_im, c_re, c_im)
    maps = [{"U": Us[i], "PRM": Ps[i], "ET": ET, "MK": MK, "ID": ID} for i in range(NCORES)]
    res = _run(nc, maps)
    return s5_unpack_y([r["Y"] for r in res])


def build_post(layer, T=TPC):
    nc = bass.Bass("TRN2", target_bir_lowering=False)
    def di(name, shape, dt=F32):
        return nc.dram_tensor(name, shape, dt, kind="ExternalInput").ap()
    xT = di("xT", [D, T]); w_out = di("w_out", [D, D]); modd = di("mod", [128, 48])
    aT = di("aT", [1024, T]); agT = di("agT", [1024, T])
    if layer == 0:
        yfT = di("yfT", [1024, T]); ybT = di("ybT", [1024, T]); uT = di("uT", [1024, T], BF16); gbT = di("gbT", [1024, T])
        s5d = di("s5d", [128, 8]); w_glu = di("w_glu", [1024, 2048]); bglu = di("bglu", [128, 16])
    else:
        ysT = di("ysT", [1024, T]); ys2T = di("ys2T", [1024, T]); zT = di("zT", [1024, T]); nw = di("nw", [128, 8])
    oT = nc.dram_tensor("oT", [D, T], F32, kind="ExternalOutput").ap()
    NT = T // 512
    with ExitStack() as st:
        K = KB(nc, st)
        mod = K.sb([128, 48], F32, "mod"); K.dma("sp", mod, mod[:], None, modd[:, :])
        wo = K.sb([128, 16, D], BF16, "wo")
        stg = Rot([K.sb([128, D], F32, f"stg{i}") for i in range(2)])
        cast_i = [0]

        def load_w(dst, w_ap, nk):
            for k in range(nk):
                s_ = stg.next()
                K.dma("sp" if k % 2 == 0 else "pool", s_, s_[:], None, w_ap[128 * k:128 * k + 128, :])
                e = ("dve", "act")[cast_i[0] % 2]; cast_i[0] += 1
                K.copy(e, dst, dst[:, k, :], s_, s_[:])
        if layer == 0:
            wg = K.sb([128, 8, 2048], BF16, "wg")
            load_w(wg, w_glu, 8)
            sd = K.sb([128, 8], F32, "sd"); K.dma("sp", sd, sd[:], None, s5d[:, :])
            bg = K.sb([128, 16], F32, "bg"); K.dma("sp", bg, bg[:], None, bglu[:, :])
        else:
            nws = K.sb([128, 8], F32, "nws"); K.dma("sp", nws, nws[:], None, nw[:, :])
            ones = K.sb([128, 128], F32, "ones"); K.memset("pool", ones, ones[:], 1.0)
        load_w(wo, w_out, 16)
        A = Rot([K.sb([128, 16, 512], BF16, f"A{i}") for i in range(2)])
        ld = {nm: Rot([K.sb([128, 512], F32, f"ld_{nm}{i}") for i in range(3)]) for nm in ("a", "b", "c", "x")}
        ldu = Rot([K.sb([128, 512], BF16, f"ldu{i}") for i in range(2)])
        tmp = {nm: Rot([K.sb([128, 512], F32, f"tm_{nm}{i}") for i in range(2)]) for nm in ("p", "q", "r")}
        osb = Rot([K.sb([128, 512], F32, f"osb{i}") for i in range(3)])
        pacc = Rot([K.ps([128, 512], F32, f"pa{i}") for i in range(4)])
        pss = K.ps([128, 512], F32, "pss")
        if layer == 0:
            gy = Rot([K.sb([128, 8, 512], BF16, f"gy{i}") for i in range(2)])
        else:
            yz = Rot([K.sb([128, 8, 512], F32, f"yz{i}") for i in range(2)])
            rs = K.sb([128, 512], F32, "rs")

        def load(nm, ap, q="sp"):
            b = ld[nm].next()
            K.dma(q, b, b[:], None, ap)
            return b

        for t in range(NT):
            ts_ = slice(512 * t, 512 * t + 512)
            Ab = A.next()
            if layer == 0:
                g = gy.next()
                for k in range(8):
                    rows = slice(128 * k, 128 * k + 128)
                    yf = load("a", yfT[rows, ts_]); yb = load("b", ybT[rows, ts_], "pool")
                    ub = ldu.next(); K.dma("sp", ub, ub[:], None, uT[rows, ts_])
                    y = tmp["p"].next()
                    K.stt(y, y[:], ub, ub[:], sd[:, k:k + 1], yf, yf[:], ALU.mult, ALU.add, extra=[sd])
                    K.tt("dve", y, y[:], y, y[:], yb, yb[:], ALU.add)
                    q_ = tmp["q"].next()
                    K.act(q_, q_[:], y, y[:], AF.Square)
                    K.ts("dve", q_, q_[:], q_, q_[:], 0.044715, 1.0, ALU.mult, ALU.add)
                    K.tt("dve", q_, q_[:], q_, q_[:], y, y[:], ALU.mult)
                    K.act(q_, q_[:], q_, q_[:], AF.Sigmoid, scale=1.5957691216057308)
                    K.tt("dve", g, g[:, k, :], q_, q_[:], y, y[:], ALU.mult)
                for n in range(8):
                    pv, pg = pacc.next(), pacc.next()
                    for k in range(8):
                        K.mm(pv, pv[:], wg, wg[:, k, 128 * n:128 * n + 128], g, g[:, k, :], start=(k == 0), stop=(k == 7))
                    for k in range(8):
                        K.mm(pg, pg[:], wg, wg[:, k, 1024 + 128 * n:1024 + 128 * n + 128], g, g[:, k, :], start=(k == 0), stop=(k == 7))
                    sg = tmp["r"].next()
                    K.act(sg, sg[:], pg, pg[:], AF.Sigmoid, extra=[bg], bias=bg[:, 8 + n:9 + n])
                    gb = load("c", gbT[128 * n:128 * n + 128, ts_])
                    v1 = tmp["p"].next()
                    K.stt(v1, v1[:], pv, pv[:], bg[:, n:n + 1], sg, sg[:], ALU.add, ALU.mult, extra=[bg])
                    K.tt("dve", Ab, Ab[:, 8 + n, :], v1, v1[:], gb, gb[:], ALU.mult)
            else:
                yzb = yz.next()
                for k in range(8):
                    rows = slice(128 * k, 128 * k + 128)
                    ys = load("a", ysT[rows, ts_]); z = load("b", zT[rows, ts_], "pool"); y2 = load("c", ys2T[rows, ts_])
                    K.tt("dve", ys, ys[:], ys, ys[:], y2, y2[:], ALU.add)
                    K.tt("dve", yzb, yzb[:, k, :], ys, ys[:], z, z[:], ALU.mult)
                    sq = tmp["q"].next()
                    K.act(sq, sq[:], yzb, yzb[:, k, :], AF.Square)
                    K.mm(pss, pss[:], ones, ones[:], sq, sq[:], start=(k == 0), stop=(k == 7))
                emit_rstd(K, rs, rs[:], pss, pss[:], 1.0 / 1024)
                for k in range(8):
                    K.stt(Ab, Ab[:, 8 + k, :], yzb, yzb[:, k, :], nws[:, k:k + 1], rs, rs[:], ALU.mult, ALU.mult, extra=[nws])
            for n in range(8):
                a = load("a", aT[128 * n:128 * n + 128, ts_]); ag = load("b", agT[128 * n:128 * n + 128, ts_], "pool")
                K.tt("dve", Ab, Ab[:, n, :], a, a[:], ag, ag[:], ALU.mult)
            for n in range(16):
                po = pacc.next()
                for k in range(16):
                    K.mm(po, po[:], wo, wo[:, k, 128 * n:128 * n + 128], Ab, Ab[:, k, :], start=(k == 0), stop=(k == 15))
                xb = load("x", xT[128 * n:128 * n + 128, ts_])
                ob = osb.next()
                K.stt(ob, ob[:], po, po[:], mod[:, 32 + n:33 + n], xb, xb[:], ALU.mult, ALU.add, extra=[mod])
                K.dma("pool", None, oT[128 * n:128 * n + 128, ts_], ob, ob[:])
        K.finish()
    return nc


def run_post(layer, xT, mod, w_out, aT, agT, **kw):
    nc = build_post(layer)
    maps = []
    for i in range(NCORES):
        sl = slice(i * TPC, (i + 1) * TPC)
        m = {"xT": np.ascontiguousarray(xT[:, sl]), "w_out": np.ascontiguousarray(w_out), "mod": np.ascontiguousarray(mod),
             "aT": np.ascontiguousarray(aT[:, sl]), "agT": np.ascontiguousarray(agT[:, sl])}
        if layer == 0:
            for nm in ("yfT", "ybT", "uT", "gbT"):
                m[nm] = np.ascontiguousarray(kw[nm][:, sl])
            m["s5d"] = pj(kw["s5_d"], 8); m["w_glu"] = np.ascontiguousarray(kw["w_glu"]); m["bglu"] = pj(kw["b_glu"], 16)
        else:
            for nm in ("ysT", "ys2T", "zT"):
                m[nm] = np.ascontiguousarray(kw[nm][:, sl])
            m["nw"] = pj(kw["norm_w"], 8)
        maps.append(m)
    res = _run(nc, maps)
    return np.concatenate([r["oT"] for r in res], axis=1)


GW = 64
NROW = SEQ // GW


def na_bias_tables(rpb_h):
    col = np.arange(GW)
    cs = np.clip(col - 8, 0, GW - 16)
    cmask = (col[None, :] >= cs[:, None]) & (col[None, :] < cs[:, None] + 16)
    dc = np.clip(col[None, :] - col[:, None], -15, 15) + 15
    out = np.empty((8, GW, 8, GW), np.float32)
    for v in range(8):
        off = 7 - v
        for kr in range(8):
            b = rpb_h[off + kr][dc]
            b = np.where(cmask, b, np.float32(-1e30)).astype(np.float32)
            out[v, :, kr, :] = b.T
    return out


def build_na():
    nc = bass.Bass("TRN2", target_bir_lowering=False)
    qT = nc.dram_tensor("qT", [128, SEQ], BF16, kind="ExternalInput").ap()
    kT = nc.dram_tensor("kT", [128, SEQ], BF16, kind="ExternalInput").ap()
    vR = nc.dram_tensor("vR", [64, NROW * 128], BF16, kind="ExternalInput").ap()
    bT = nc.dram_tensor("bT", [64, 8, 512], F32, kind="ExternalInput").ap()
    oT = nc.dram_tensor("oT", [128, SEQ], F32, kind="ExternalOutput").ap()
    scale = 128.0 ** -0.5
    with ExitStack() as st:
        K = KB(nc, st)
        q = K.sb([128, SEQ], BF16, "q"); k = K.sb([128, SEQ], BF16, "k"); v = K.sb([64, NROW * 128], BF16, "v")
        bias = K.sb([64, 8, 512], F32, "bias")
        for h in range(4):
            sl = slice(h * 4096, (h + 1) * 4096)
            K.dma("sp", k, k[:, sl], None, kT[:, sl])
            K.dma("pool", q, q[:, sl], None, qT[:, sl])
        for h in range(4):
            sl = slice(h * 8192, (h + 1) * 8192)
            K.dma("sp", v, v[:, sl], None, vR[:, sl])
        K.dma("sp", bias, bias[:], None, bT[:, :, :])
        ones = K.sb([64, 128], BF16, "ones"); K.memset("pool", ones, ones[:], 1.0)
        pS = Rot([K.ps([64, 512], F32, f"pS{i}") for i in range(3)])
        pO = Rot([K.ps([128, 512], F32, f"pO{i}") for i in range(2)])
        pZ = Rot([K.ps([128, 512], F32, f"pZ{i}") for i in range(2)])
        tS = Rot([K.sb([64, 512], F32, f"tS{i}") for i in range(3)])
        pT = Rot([K.sb([64, 512], BF16, f"pT{i}") for i in range(3)])
        rec = Rot([K.sb([128, 512], F32, f"rec{i}") for i in range(2)])
        osb = Rot([K.sb([128, 512], F32, f"osb{i}") for i in range(2)])
        def row_info(r):
            rs = min(max(r - 4, 0), NROW - 8)
            var = r if r < 4 else (4 if r <= 252 else 4 + (r - 252))
            return rs, var

        pS_l = pS.items

        def issue_S(r):
            rs, _ = row_info(r)
            ps = pS_l[r % 3]
            qs = slice(r * 64, r * 64 + 64)
            for j in range(8):
                K.mm(ps, ps[:, j * 64:(j + 1) * 64], k, k[:, (rs + j) * 64:(rs + j) * 64 + 64], q, q[:, qs])

        LOOK = 2
        for r in range(LOOK):
            issue_S(r)
        o = z = None
        for r in range(NROW):
            rr = r % 8
            if rr == 0:
                o, z = pO.next(), pZ.next()
            if r + LOOK < NROW:
                issue_S(r + LOOK)
            rs, var = row_info(r)
            ps = pS_l[r % 3]
            t = tS.next()
            K.stt(t, t[:], ps, ps[:], scale, bias, bias[:, var, :], ALU.mult, ALU.add)
            p = pT.next()
            K.act(p, p[:], t, t[:], AF.Exp)
            cs_ = slice(rr * 64, rr * 64 + 64)
            for j in range(8):
                K.mm(o, o[:, cs_], v, v[:, (rs + j) * 128:(rs + j) * 128 + 128], p, p[:, j * 64:(j + 1) * 64], start=(j == 0), stop=(j == 7))
            for j in range(8):
                K.mm(z, z[:, cs_], ones, ones[:], p, p[:, j * 64:(j + 1) * 64], start=(j == 0), stop=(j == 7))
            if rr == 7:
                r0 = r - 7
                rc = rec.next()
                K.op("dve", lambda e: e.reciprocal(out=rc[:], in_=z[:]), [z], [rc])
                ob = osb.next()
                K.tt("dve", ob, ob[:], o, o[:], rc, rc[:], ALU.mult)
                K.dma("pool", None, oT[:, r0 * 64:r0 * 64 + 512], ob, ob[:])
        K.finish()
    return nc


def run_na(obf, rpb):
    nc = build_na()
    maps = []
    for i in range(NCORES):
        vT = obf[2048 + 128 * i:2048 + 128 * i + 128, :]
        vR = np.ascontiguousarray(vT.reshape(128, NROW, 64).transpose(2, 1, 0).reshape(64, NROW * 128))
        bt = na_bias_tables(np.asarray(rpb[i], np.float32))
        bT = np.ascontiguousarray(bt.transpose(1, 0, 2, 3).reshape(64, 8, 512))
        maps.append({"qT": np.ascontiguousarray(obf[128 * i:128 * i + 128, :]),
                     "kT": np.ascontiguousarray(obf[1024 + 128 * i:1024 + 128 * i + 128, :]), "vR": vR, "bT": bT})
    res = _run(nc, maps)
    return np.concatenate([r["oT"] for r in res], axis=0)


NCH = SEQ // 128


def build_ssd(nchunks=NCH, ndirs=2):
    nc = bass.Bass("TRN2", target_bir_lowering=False)
    XBC = nc.dram_tensor("XBC", [2, 3, 128, SEQ + 4], F32, kind="ExternalInput").ap()
    CW = nc.dram_tensor("CW", [2, 3, 128, 6], F32, kind="ExternalInput").ap()
    DTR = nc.dram_tensor("DTR", [2, 2, 128, NCH], F32, kind="ExternalInput").ap()
    SCd = nc.dram_tensor("SC", [2, 128, 6], F32, kind="ExternalInput").ap()
    TRI = nc.dram_tensor("TRI", [128, 128], F32, kind="ExternalInput").ap()
    SUd = nc.dram_tensor("SU", [128, 128], F32, kind="ExternalInput").ap()
    IDd = nc.dram_tensor("ID", [128, 128], BF16, kind="ExternalInput").ap()
    Y = nc.dram_tensor("Y", [2, 128, NCH, 128], F32, kind="ExternalOutput").ap()
    PIECE = 2048
    with ExitStack() as st:
        K = KB(nc, st)
        tri = K.sb([128, 128], F32, "tri"); K.dma("sp", tri, tri[:], None, TRI[:, :])
        su = K.sb([128, 128], F32, "su"); K.dma("sp", su, su[:], None, SUd[:, :])
        idb = K.sb([128, 128], BF16, "idb"); K.dma("sp", idb, idb[:], None, IDd[:, :])
        ones = K.sb([128, 128], F32, "ones"); K.memset("pool", ones, ones[:], 1.0)
        stage = Rot([K.sb([128, PIECE + 4], F32, f"stage{i}") for i in range(2)])
        acc = K.sb([128, PIECE], F32, "acc")
        xfm = K.sb([128, PIECE], BF16, "xfm")
        BT = K.sb([128, SEQ], BF16, "BT"); CT = K.sb([128, SEQ], BF16, "CT")
        Xt = K.sb([128, NCH, 128], BF16, "Xt"); Bt = K.sb([128, NCH, 128], BF16, "Bt")
        cw = K.sb([128, 3, 6], F32, "cw")
        scs = K.sb([128, 6], F32, "scs")
        dtr = K.sb([128, 2, NCH], F32, "dtr")
        dt = K.sb([128, 2, NCH], F32, "dt"); adt = K.sb([128, 2, NCH], F32, "adt")
        cum = K.sb([128, 2, NCH], F32, "cum"); ecum = K.sb([128, 2, NCH], F32, "ecum")
        dts = K.sb([128, 2, NCH], F32, "dts"); etot = K.sb([128, 2, NCH], F32, "etot")
        na = K.sb([128, 2], F32, "na")
        stT = [K.sb([128, 64], F32, f"stT{h}") for h in range(2)]
        stB = [[K.sb([128, 64], BF16, f"stB{h}_{i}") for i in range(2)] for h in range(2)]
        pSC = K.ps([128, 512], F32, "pSC")
        pSEG = [K.ps([128, 512], F32, f"pSEG{h}") for h in range(2)]
        pYD = [K.ps([128, 512], F32, f"pYD{i}") for i in range(2)]
        pYO = K.ps([128, 512], F32, "pYO")
        pST = [K.ps([128, 512], F32, f"pST{i}") for i in range(2)]

        class _PA:
            def __init__(self):
                self.i = 0

            def next(self):
                self.i += 1
                return pSC if self.i % 2 else pYO
        pA = _PA()
        scm = Rot([K.sb([128, 512], F32, f"scm{i}") for i in range(2)])
        lseg = Rot([K.sb([128, 512], F32, f"lseg{i}") for i in range(4)])
        Lm = Rot([K.sb([128, 512], F32, f"Lm{i}") for i in range(4)])
        Gt = Rot([K.sb([128, 512], BF16, f"Gt{i}") for i in range(4)])
        xd = Rot([K.sb([128, 4, 64], BF16, f"xd{i}") for i in range(4)])
        xdd = Rot([K.sb([128, 4, 64], BF16, f"xdd{i}") for i in range(4)])
        ysb = Rot([K.sb([128, 512], F32, f"ysb{i}") for i in range(3)])
        ws = K.sb([128, 2], F32, "ws")
        for d in range(ndirs):
            K.dma("sp", cw, cw[:], None, CW[d].rearrange("g p t -> p g t"))
            K.dma("sp", scs, scs[:], None, SCd[d])
            K.dma("sp", dtr, dtr[:], None, DTR[d].rearrange("h p c -> p h c"))
            import os
            for g in range(0 if os.environ.get('SKIP_CONV') else 3):
                for pc in range(SEQ // PIECE):
                    sg = stage.next()
                    K.dma("sp" if pc % 2 == 0 else "pool", sg, sg[:], None, XBC[d, g, :, pc * PIECE:pc * PIECE + PIECE + 4])
                    K.ts("dve", acc, acc[:], sg, sg[:, 0:PIECE], cw[:, g, 0:1], None, ALU.mult, extra=[cw])
                    for j in range(1, 5):
                        K.stt(acc, acc[:], sg, sg[:, j:j + PIECE], cw[:, g, j:j + 1], acc, acc[:], ALU.mult, ALU.add, extra=[cw])
                    dst, dbuf = ((xfm[:, :], xfm), (BT[:, pc * PIECE:(pc + 1) * PIECE], BT), (CT[:, pc * PIECE:(pc + 1) * PIECE], CT))[g]
                    K.act(dbuf, dst, acc, acc[:], AF.Silu, extra=[cw], bias=cw[:, g, 5:6])
                    if g < 2:
                        for cc in range(PIECE // 128):
                            c = pc * (PIECE // 128) + cc
                            pa = pA.next()
                            src = xfm[:, cc * 128:(cc + 1) * 128] if g == 0 else BT[:, c * 128:(c + 1) * 128]
                            K.mm(pa, pa[:, 0:128], dbuf, src, idb, idb[:])
                            tgt = Xt if g == 0 else Bt
                            K.copy("act" if cc % 2 == 0 else "pool" if False else "dve", tgt, tgt[:, c, :], pa, pa[:, 0:128])
            import os
            for h in range(0 if os.environ.get('SKIP_DT') else 2):
                K.ts("dve", cum, cum[:, h, :], dtr, dtr[:, h, :], scs[:, h:h + 1], None, ALU.add, extra=[scs])
                K.act(ecum, ecum[:, h, :], cum, cum[:, h, :], AF.Exp)
                K.ts("dve", ecum, ecum[:, h, :], ecum, ecum[:, h, :], 1.0, None, ALU.add)
                K.ts("dve", dt, dt[:, h, :], cum, cum[:, h, :], 0.0, 0.3, ALU.max, ALU.add)
                for _it in range(6):
                    K.act(dts, dts[:, h, :], dt, dt[:, h, :], AF.Exp, scale=-1.0)
                    K.tt("dve", dts, dts[:, h, :], dts, dts[:, h, :], ecum, ecum[:, h, :], ALU.mult)
                    K.stt(dt, dt[:, h, :], dts, dts[:, h, :], -1.0, dt, dt[:, h, :], ALU.add, ALU.add)
                K.act(na, na[:, h:h + 1], scs, scs[:, 2 + h:3 + h], AF.Exp)
                K.ts("dve", na, na[:, h:h + 1], na, na[:, h:h + 1], -1.0, None, ALU.mult)
                K.ts("dve", adt, adt[:, h, :], dt, dt[:, h, :], na[:, h:h + 1], None, ALU.mult, extra=[na])
                pa = pA.next()
                K.mm(pa, pa[:, 0:128], tri, tri[:], adt, adt[:, h, :])
                K.copy("dve", cum, cum[:, h, :], pa, pa[:, 0:128])
                K.act(ecum, ecum[:, h, :], cum, cum[:, h, :], AF.Exp)
                pb = pA.next()
                K.mm(pb, pb[:, 0:128], ones, ones[:], adt, adt[:, h, :])
                K.copy("dve", dts, dts[:, h, :], pb, pb[:, 0:128])
                K.act(etot, etot[:, h, :], dts, dts[:, h, :], AF.Exp)
                K.tt("dve", dts, dts[:, h, :], dts, dts[:, h, :], cum, cum[:, h, :], ALU.subtract)
                K.act(dts, dts[:, h, :], dts, dts[:, h, :], AF.Exp)
                K.tt("dve", dts, dts[:, h, :], dts, dts[:, h, :], dt, dt[:, h, :], ALU.mult)
                K.memset("pool", stT[h], stT[h][:], 0.0)
                K.memset("pool", stB[h][0], stB[h][0][:], 0.0)
                K.memset("pool", stB[h][1], stB[h][1][:], 0.0)
            G = 4
            NG = nchunks // G

            def g4(ap, inner):
                return ap.rearrange("p (g i) -> p g i", g=G)

            def indep(k):
                c0 = k * G
                for g in range(G):
                    c = c0 + g
                    K.mm(pSC, pSC[:, g * 128:(g + 1) * 128], BT, BT[:, c * 128:(c + 1) * 128], CT, CT[:, c * 128:(c + 1) * 128])
                sm = scm.next()
                K.tt("dve", sm, g4(sm[:, :], 128), pSC, g4(pSC[:, :], 128), tri, tri[:, :].unsqueeze(1).to_broadcast([128, G, 128]), ALU.mult)
                pyd, pst = pYD[k % 2], pST[k % 2]
                for h in range(2):
                    hs = slice(64 * h, 64 * h + 64)
                    lsb = lseg.next()
                    K.tt("dve", lsb, g4(lsb[:, :], 128), su, su[:, :].unsqueeze(1).to_broadcast([128, G, 128]),
                         adt, adt[:, h, c0:c0 + G].unsqueeze(2).to_broadcast([128, G, 128]), ALU.mult)
                    for g in range(G):
                        K.mm(pSEG[h], pSEG[h][:, g * 128:(g + 1) * 128], lsb, lsb[:, g * 128:(g + 1) * 128], tri, tri[:])
                    L = Lm.next()
                    K.act(L, L[:], pSEG[h], pSEG[h][:], AF.Exp)
                    Gb = Gt.next()
                    K.tt("dve", Gb, Gb[:], L, L[:], sm, sm[:], ALU.mult)
                    x1 = xd.next()
                    K.tt("dve", x1, x1[:], Xt, Xt[:, c0:c0 + G, hs], dt, dt[:, h, c0:c0 + G].unsqueeze(2).to_broadcast([128, G, 64]), ALU.mult)
                    x2 = xdd.next()
                    K.tt("dve", x2, x2[:], Xt, Xt[:, c0:c0 + G, hs], dts, dts[:, h, c0:c0 + G].unsqueeze(2).to_broadcast([128, G, 64]), ALU.mult)
                    for g in range(G):
                        K.mm(pyd, pyd[:, g * 128 + 64 * h:g * 128 + 64 * h + 64], Gb, Gb[:, g * 128:(g + 1) * 128], x1, x1[:, g, :])
                    for g in range(G):
                        K.mm(pst, pst[:, g * 128 + 64 * h:g * 128 + 64 * h + 64], Bt, Bt[:, c0 + g, :], x2, x2[:, g, :])

            def dep(k):
                c0 = k * G
                pyd, pst = pYD[k % 2], pST[k % 2]
                for g in range(G):
                    c = c0 + g
                    for h in range(2):
                        col = slice(g * 128 + 64 * h, g * 128 + 64 * h + 64)
                        sb_old = stB[h][stp[h] % 2]
                        sb_new = stB[h][(stp[h] + 1) % 2]
                        K.mm(pYO, pYO[:, col], CT, CT[:, c * 128:(c + 1) * 128], sb_old, sb_old[:])
                        K.stt(sb_new, sb_new[:], stT[h], stT[h][:], etot[:, h, c:c + 1], pst, pst[:, col], ALU.mult, ALU.add, extra=[etot])
                        K.stt(stT[h], stT[h][:], stT[h], stT[h][:], etot[:, h, c:c + 1], pst, pst[:, col], ALU.mult, ALU.add, extra=[etot])
                        stp[h] += 1
                yb = ysb.next()
                K.tt("dve", yb, yb[:, :].rearrange("p (g h j) -> p g h j", g=G, h=2), pYO, pYO[:, :].rearrange("p (g h j) -> p g h j", g=G, h=2),
                     ecum, ecum[:, :, c0:c0 + G].rearrange("p h g -> p g h").unsqueeze(3).to_broadcast([128, G, 2, 64]), ALU.mult)
                K.tt("dve", yb, yb[:], yb, yb[:], pyd, pyd[:], ALU.add)
                if d == 0:
                    for h in range(2):
                        hs = slice(64 * h, 64 * h + 64)
                        K.stt(yb, g4(yb[:, :], 128)[:, :, hs], Xt, Xt[:, c0:c0 + G, hs], scs[:, 4 + h:5 + h], yb, g4(yb[:, :], 128)[:, :, hs],
                              ALU.mult, ALU.add, extra=[scs])
                K.dma("pool", None, Y[d, :, c0:c0 + G, :], yb, g4(yb[:, :], 128))

            stp = [0, 0]
            if NG > 0:
                indep(0)
            for k in range(NG):
                if k + 1 < NG:
                    indep(k + 1)
                dep(k)
        K.finish()
    return nc


def run_ssd(of1, conv_w, conv_b, dt_bias, a_log, ssd_d):
    nc = build_ssd()
    xbc = of1[2048:3584]
    dtr = of1[3584:3616]
    tri = np.triu(np.ones((128, 128), np.float32))
    su = np.tril(np.ones((128, 128), np.float32), -1)
    idm = np.eye(128, dtype=np.float32).astype(NPBF)
    maps = []
    for i in range(NCORES):
        g = i // 4
        rows = [slice(128 * i, 128 * i + 128), slice(1024 + 128 * g, 1024 + 128 * g + 128), slice(1280 + 128 * g, 1280 + 128 * g + 128)]
        XBC = np.zeros((2, 3, 128, SEQ + 4), np.float32)
        CWm = np.empty((2, 3, 128, 6), np.float32)
        DTR = np.empty((2, 2, 128, NCH), np.float32)
        SC = np.empty((2, 128, 6), np.float32)
        for d in range(2):
            for gi, rs in enumerate(rows):
                src = xbc[rs]
                XBC[d, gi, :, 2:SEQ + 2] = src[:, ::-1] if d == 1 else src
                w = conv_w[:, rs].T
                CWm[d, gi, :, 0:5] = w[:, ::-1] if d == 1 else w
                CWm[d, gi, :, 5] = conv_b[rs]
            for h in range(2):
                hd = 2 * i + h
                s_ = dtr[d * 16 + hd]
                s_ = s_[::-1] if d == 1 else s_
                DTR[d, h] = s_.reshape(NCH, 128).T
                SC[d, :, h] = dt_bias[d, hd]; SC[d, :, 2 + h] = a_log[d, hd]; SC[d, :, 4 + h] = ssd_d[hd]
        maps.append({"XBC": XBC, "CW": CWm, "DTR": DTR, "SC": SC, "TRI": tri, "SU": su, "ID": idm})
    res = _run(nc, maps)
    yf = np.empty((1024, SEQ), np.float32); yb = np.empty((1024, SEQ), np.float32)
    for i in range(NCORES):
        Yc = res[i]["Y"]
        for d in range(2):
            y = Yc[d].transpose(2, 1, 0).reshape(128, SEQ)
            if d == 0:
                yf[128 * i:128 * i + 128] = y
            else:
                yb[128 * i:128 * i + 128] = y[:, ::-1]
    return yf, yb


def kernel(x, c, e_norm_g, e_ada_w, e_ada_b, e_w_in, e_q_norm, e_k_norm, s5_lam_re, s5_lam_im,
           s5_log_step, s5_b_re, s5_b_im, s5_c_re, s5_c_im, s5_d, s5_w_glu, s5_b_glu, e_w_out,
           o_norm_g, o_ada_w, o_ada_b, o_w_in, o_q_norm, o_k_norm, na_rpb, ssd_conv_w, ssd_conv_b,
           ssd_dt_bias, ssd_a_log, ssd_d, ssd_norm_w, o_w_out):
    f = lambda a: np.asarray(a, np.float32)
    xT = np.ascontiguousarray(f(x)[0].T)
    obf0, of0, mod0 = run_front(0, xT, f(c), f(e_norm_g), f(e_ada_w), f(e_ada_b), f(e_w_in), f(e_q_norm), f(e_k_norm))
    oa, yf, yb = run_mix0(obf0, f(s5_lam_re)[0], f(s5_lam_im)[0], f(s5_log_step)[0], f(s5_b_re)[0], f(s5_b_im)[0],
                          f(s5_c_re)[0], f(s5_c_im)[0])
    x1T = run_post(0, xT, mod0, f(e_w_out)[0], oa, of0[0:1024], yfT=yf, ybT=yb, uT=obf0[1536:2560], gbT=of0[1024:2048],
                   s5_d=f(s5_d)[0], w_glu=f(s5_w_glu)[0], b_glu=f(s5_b_glu)[0])
    obf1, of1, mod1 = run_front(1, x1T, f(c), f(o_norm_g), f(o_ada_w), f(o_ada_b), f(o_w_in), f(o_q_norm), f(o_k_norm))
    oc = run_na(obf1, f(na_rpb)[0])
    sf, sb_ = run_ssd(of1, f(ssd_conv_w)[0], f(ssd_conv_b)[0], f(ssd_dt_bias)[0], f(ssd_a_log)[0], f(ssd_d)[0])
    x2T = run_post(1, x1T, mod1, f(o_w_out)[0], oc, of1[0:1024], ysT=sf, ys2T=sb_, zT=of1[1024:2048], norm_w=f(ssd_norm_w)[0])
    return np.ascontiguousarray(x2T.T)[None].astype(np.float32)


def build_mix0(NPR=16):
    nc = bass.Bass("TRN2", target_bir_lowering=False)
    qT = nc.dram_tensor("qT", [128, SEQ], BF16, kind="ExternalInput").ap()
    kT = nc.dram_tensor("kT", [128, SEQ], BF16, kind="ExternalInput").ap()
    vP = nc.dram_tensor("vP", [128, 128 * 129], BF16, kind="ExternalInput").ap()
    oT = nc.dram_tensor("oT", [SEQ, 128], F32, kind="ExternalOutput").ap()
    U = nc.dram_tensor("U", [NPR, 4, 128, S5NC], BF16, kind="ExternalInput").ap()
    PRM = nc.dram_tensor("PRM", [NPR, 64, 67], F32, kind="ExternalInput").ap()
    ETd = nc.dram_tensor("ET", [64, 129], F32, kind="ExternalInput").ap()
    MKd = nc.dram_tensor("MK", [128, 4, 512], F32, kind="ExternalInput").ap()
    IDd = nc.dram_tensor("ID", [64, 64], F32, kind="ExternalInput").ap()
    Y = nc.dram_tensor("Y", [NPR, 4, 128, S5NC], F32, kind="ExternalOutput").ap()
    scale = 128.0 ** -0.5
    with ExitStack() as st:
        K = KB(nc, st)
        q = K.sb([128, SEQ], BF16, "q")
        k = K.sb([128, SEQ], BF16, "k")
        v = K.sb([128, 128 * 129], BF16, "v")
        for h in range(4):
            sl = slice(h * 4096, (h + 1) * 4096)
            K.dma("sp", k, k[:, sl], None, kT[:, sl])
            K.dma("pool", q, q[:, sl], None, qT[:, sl])
            vs = slice(h * 32 * 129, (h + 1) * 32 * 129)
            K.dma("sp", v, v[:, vs], None, vP[:, vs])
        pS = [K.ps([128, 512], F32, f"pS{i}") for i in range(3)]
        pOx = [K.ps([128, 258], F32, f"pO{i}") for i in range(2)]
        pT = Rot([K.sb([128, 512], BF16, f"pT{i}") for i in range(3)])
        rec = Rot([K.sb([128, 4], F32, f"rec{i}") for i in range(2)])
        zl = K.sb([128, 128], BF16, "zl"); K.memset("pool", zl, zl[:], 0.0)
        zr = K.sb([128, 258], BF16, "zr"); K.memset("pool", zr, zr[:], 0.0)
        osb = Rot([K.sb([128, 4, 128], F32, f"osb{i}") for i in range(2)])
        load, P1a, P1b, P2a, P2b, P2c = s5_program(K, nc, U, PRM, ETd, MKd, IDd, Y, NPR, shared=True)
        NQ, NKB = SEQ // 512, SEQ // 128
        seq = [(a_, b_) for a_ in range(NQ) for b_ in range(NKB)]
        curs = {}

        def s5_hook(t):
            pr, half = t // 2, t % 2
            if pr >= NPR:
                return
            if half == 0:
                if pr + 2 < NPR:
                    load(pr + 2)
                P2a(pr)
                if pr + 1 < NPR:
                    P1a(pr + 1)
                curs[pr] = P2b(pr)
            else:
                if pr + 1 < NPR:
                    P1b(pr + 1)
                P2c(pr, curs[pr])

        load(0)
        if NPR > 1:
            load(1)
        P1a(0); P1b(0)

        def issue_S(idx):
            qt, kb = seq[idx]
            ps = pS[idx % 3]
            K.mm(ps, ps[:], k, k[:, kb * 128:(kb + 1) * 128], q, q[:, qt * 512:(qt + 1) * 512])

        issue_S(0); issue_S(1)
        for idx, (qt, kb) in enumerate(seq):
            if kb == 0:
                s5_hook(qt)
            if idx + 2 < len(seq):
                issue_S(idx + 2)
            ps = pS[idx % 3]
            p = pT.next()
            K.act(p, p[:], ps, ps[:], AF.Exp, scale=scale)
            if kb == 0:
                for po in pOx:
                    K.mm(po, po[:, :], zl, zl[:], zr, zr[:], start=True, stop=False)
            for qb in range(4):
                po = pOx[qb // 2]
                K.mm(po, po[:, (qb % 2) * 129:(qb % 2 + 1) * 129], p, p[:, qb * 128:(qb + 1) * 128], v, v[:, kb * 129:(kb + 1) * 129],
                     start=False, stop=(kb == NKB - 1))
            if kb == NKB - 1:
                r = rec.next()
                ob = osb.next()
                for qb in range(4):
                    po = pOx[qb // 2]
                    c0 = (qb % 2) * 129
                    K.op("dve", lambda e: e.reciprocal(out=r[:, qb:qb + 1], in_=po[:, c0 + 128:c0 + 129]), [po], [r])
                    K.ts("dve", ob, ob[:, qb, :], po, po[:, c0:c0 + 128], r[:, qb:qb + 1], None, ALU.mult, extra=[r])
                K.dma("pool", None, oT[qt * 512:(qt + 1) * 512, :].rearrange("(b p) d -> p b d", p=128), ob, ob[:])
        K.finish()
    return nc


def run_mix0(obf, lam_re, lam_im, log_step, b_re, b_im, c_re, c_im):
    nc = build_mix0()
    ET, MK, ID = s5_consts()
    Us = s5_pack_u(obf[1536:2560])
    Ps = s5_params(lam_re, lam_im, log_step, b_re, b_im, c_re, c_im)
    maps = []
    for i in range(NCORES):
        g = i // 4
        vT = obf[1280 + 128 * g:1280 + 128 * g + 128, :]
        vP = np.ones((128, SEQ // 128, 129), obf.dtype)
        vP[:, :, 0:128] = vT.reshape(128, SEQ // 128, 128).transpose(2, 1, 0)
        vP = np.ascontiguousarray(vP.reshape(128, 128 * 129))
        maps.append({"qT": np.ascontiguousarray(obf[128 * i:128 * i + 128, :]),
                     "kT": np.ascontiguousarray(obf[1024 + 128 * g:1024 + 128 * g + 128, :]),
                     "vP": vP, "U": Us[i], "PRM": Ps[i], "ET": ET, "MK": MK, "ID": ID})
    res = _run(nc, maps)
    oa = np.concatenate([np.ascontiguousarray(r["oT"].T) for r in res], axis=0)
    yf, yb = s5_unpack_y([r["Y"] for r in res])
    return oa, yf, yb
```
